# Optimizing a Trainium2 kernel written in Bass

```python
import math
import jax, jax.numpy as jnp
from jax import lax
import numpy as np

D_MODEL = 1024
BATCH = 8
SEQ = 2048
DEPTH = 4

D_MIX = D_MODEL
ATTN_WIDTH = D_MIX // 2
HEAD_DIM = 64
N_HEADS = ATTN_WIDTH // HEAD_DIM
CONV_WIDTH = D_MIX - ATTN_WIDTH
CONV_GROUPS = CONV_WIDTH // HEAD_DIM
CONV_K = 31
D_IN = 3 * ATTN_WIDTH + N_HEADS + 2 * CONV_WIDTH
D_FF = int(math.ceil((8 * D_MODEL / 3) / 256) * 256)
Q_BLOCK = 128
N_MOD = 6
EPS = 1e-6

kernel_name = "fox_conformer_hymba_adaln_trunk"


def rms_norm(x, g):
    xf = x.astype(jnp.float32)
    y = xf * lax.rsqrt(jnp.mean(xf * xf, axis=-1, keepdims=True) + EPS)
    return (y * g.astype(jnp.float32)).astype(x.dtype)


def layer_norm(x, g, b):
    xf = x.astype(jnp.float32)
    mu = jnp.mean(xf, axis=-1, keepdims=True)
    var = jnp.mean(jnp.square(xf - mu), axis=-1, keepdims=True)
    y = (xf - mu) * lax.rsqrt(var + EPS)
    return (y * g.astype(jnp.float32) + b.astype(jnp.float32)).astype(x.dtype)


def fox_attention(q, k, v, log_f):
    B, S, H, Dh = q.shape
    nb = S // Q_BLOCK
    scale = Dh ** -0.5
    cum = jnp.cumsum(log_f, axis=1)
    cum_k = jnp.transpose(cum, (0, 2, 1))[:, :, None, :]
    q_blocks = jnp.transpose(q.reshape(B, nb, Q_BLOCK, H, Dh), (1, 0, 2, 3, 4))
    c_blocks = jnp.transpose(cum.reshape(B, nb, Q_BLOCK, H), (1, 0, 3, 2))
    k_pos = jnp.arange(S, dtype=jnp.int32)

    def one_block(args):
        i, q_blk, c_blk = args
        s = jnp.einsum('bqhd,bkhd->bhqk', q_blk, k).astype(jnp.float32) * scale
        s = s + c_blk[..., None] - cum_k
        q_pos = i * Q_BLOCK + jnp.arange(Q_BLOCK, dtype=jnp.int32)
        mask = k_pos[None, :] <= q_pos[:, None]
        s = jnp.where(mask[None, None], s, -jnp.inf)
        p = jax.nn.softmax(s, axis=-1)
        return jnp.einsum('bhqk,bkhd->bqhd', p.astype(v.dtype), v)

    outs = lax.map(one_block, (jnp.arange(nb, dtype=jnp.int32), q_blocks, c_blocks))
    return jnp.transpose(outs, (1, 0, 2, 3, 4)).reshape(B, S, H * Dh)


def causal_depthwise_conv(u, w, b):
    C = u.shape[-1]
    out = lax.conv_general_dilated(
        u, w[:, None, :].astype(u.dtype), window_strides=(1,), padding=((CONV_K - 1, 0),),
        dimension_numbers=('NWC', 'WIO', 'NWC'), feature_group_count=C)
    return out + b.astype(u.dtype)


def hybrid_mixer(h, w_in, b_f, conv_w, conv_b, conv_ln_g, conv_ln_b, w_o):
    B, S, _ = h.shape
    proj = h @ w_in
    splits = [ATTN_WIDTH, 2 * ATTN_WIDTH, 3 * ATTN_WIDTH, 3 * ATTN_WIDTH + N_HEADS,
              3 * ATTN_WIDTH + N_HEADS + CONV_WIDTH]
    q, k, v, f_logit, conv_val, conv_gate = jnp.split(proj, splits, axis=-1)
    q = q.reshape(B, S, N_HEADS, HEAD_DIM)
    k = k.reshape(B, S, N_HEADS, HEAD_DIM)
    v = v.reshape(B, S, N_HEADS, HEAD_DIM)
    log_f = jax.nn.log_sigmoid((f_logit + b_f).astype(jnp.float32))
    attn = fox_attention(q, k, v, log_f)
    u = conv_val * jax.nn.sigmoid(conv_gate)
    u = causal_depthwise_conv(u, conv_w, conv_b)
    u = jax.nn.silu(layer_norm(u, conv_ln_g, conv_ln_b))
    return jnp.concatenate([attn, u], axis=-1) @ w_o


def swiglu_ffn(h, w_ffn_in, w_ffn_out):
    g, u = jnp.split(h @ w_ffn_in, 2, axis=-1)
    return (jax.nn.silu(g) * u) @ w_ffn_out


def setup_inputs(seed: int = 0) -> dict:
    key = jax.random.key(seed)
    ks = jax.random.split(key, 24)
    f32 = jnp.float32
    L, D = DEPTH, D_MODEL

    def nrm(k, shape, s):
        return jax.random.normal(k, shape, f32) * s

    x = jax.random.normal(ks[0], (BATCH, SEQ, D), f32)
    c = jax.random.normal(ks[1], (BATCH, D), f32)
    w_in = jnp.concatenate([
        nrm(ks[2], (L, D, 3 * ATTN_WIDTH), D ** -0.5),
        nrm(ks[3], (L, D, N_HEADS), 0.5 * D ** -0.5),
        nrm(ks[4], (L, D, 2 * CONV_WIDTH), D ** -0.5),
    ], axis=-1)
    b_f = 2.5 + 0.5 * jax.random.normal(ks[5], (L, N_HEADS), f32)
    conv_w = nrm(ks[6], (L, CONV_K, CONV_WIDTH), CONV_K ** -0.5)
    conv_b = nrm(ks[7], (L, CONV_WIDTH), 0.01)
    conv_ln_g = 1.0 + nrm(ks[8], (L, CONV_WIDTH), 0.05)
    conv_ln_b = nrm(ks[9], (L, CONV_WIDTH), 0.01)
    w_o = nrm(ks[10], (L, D_MIX, D), D_MIX ** -0.5)
    w_ffn_in = nrm(ks[11], (L, D, 2 * D_FF), D ** -0.5)
    w_ffn_out = nrm(ks[12], (L, D_FF, D), D_FF ** -0.5)
    mix_pre_g = 1.0 + nrm(ks[13], (L, D), 0.05)
    mix_post_g = 1.0 + nrm(ks[14], (L, D), 0.05)
    ffn_pre_g = 1.0 + nrm(ks[15], (L, D), 0.05)
    ffn_post_g = 1.0 + nrm(ks[16], (L, D), 0.05)
    ada_w = nrm(ks[17], (L, D, N_MOD * D), 0.5 * D ** -0.5)
    ada_b = nrm(ks[18], (L, N_MOD * D), 0.01)
    return {"x": x, "c": c, "w_in": w_in, "b_f": b_f, "conv_w": conv_w, "conv_b": conv_b,
            "conv_ln_g": conv_ln_g, "conv_ln_b": conv_ln_b, "w_o": w_o,
            "w_ffn_in": w_ffn_in, "w_ffn_out": w_ffn_out,
            "mix_pre_g": mix_pre_g, "mix_post_g": mix_post_g,
            "ffn_pre_g": ffn_pre_g, "ffn_post_g": ffn_post_g,
            "ada_w": ada_w, "ada_b": ada_b}


def reference(x, c, w_in, b_f, conv_w, conv_b, conv_ln_g, conv_ln_b, w_o,
              w_ffn_in, w_ffn_out, mix_pre_g, mix_post_g, ffn_pre_g, ffn_post_g,
              ada_w, ada_b):
    c_act = jax.nn.silu(c)
    for i in range(DEPTH):
        mod = c_act @ ada_w[i] + ada_b[i]
        sh1, sc1, g1, sh2, sc2, g2 = [m[:, None, :] for m in jnp.split(mod, N_MOD, axis=-1)]
        h = rms_norm(x, mix_pre_g[i]) * (1.0 + sc1) + sh1
        y = hybrid_mixer(h, w_in[i], b_f[i], conv_w[i], conv_b[i], conv_ln_g[i], conv_ln_b[i], w_o[i])
        x = x + g1 * rms_norm(y, mix_post_g[i])
        h = rms_norm(x, ffn_pre_g[i]) * (1.0 + sc2) + sh2
        y = swiglu_ffn(h, w_ffn_in[i], w_ffn_out[i])
        x = x + g2 * rms_norm(y, ffn_post_g[i])
    return x
```

```python
import bisect
import os
import numpy as np
import concourse.bass as bass
import concourse.mybir as mybir
from concourse.bass_utils import run_bass_kernel_spmd

F32 = mybir.dt.float32
BF16 = mybir.dt.bfloat16
U8 = mybir.dt.uint8
AF = mybir.ActivationFunctionType
ALU = mybir.AluOpType

D = 1024
T = 2048
L_FULL = 4
NH = 8
DFF = 2816
NHC = DFF // 128
CK = 31
EPS = 1e-6
NPAR = 224
P_PRE1, P_POST1, P_PRE2, P_POST2, P_ADAB, P_CONVW, P_CONVB, P_LNG, P_LNB, P_BF = 0, 8, 16, 24, 32, 80, 204, 208, 212, 216


class Res:
    __slots__ = ("w", "r")

    def __init__(self):
        self.w = None
        self.r = {}


class Tracker:
    def __init__(self, nc):
        self.nc = nc
        self.E = {}
        for name, h, am in (("pe", nc.tensor, False), ("act", nc.scalar, True), ("dve", nc.vector, True),
                            ("pool", nc.gpsimd, True), ("sp", nc.sync, True)):
            self.E[name] = dict(h=h, sem=nc.alloc_semaphore("s_" + name), am=am, n=0, marks=[], last=None, seen={})
        self.dsems = {"sp": [[nc.alloc_semaphore("dsp%d" % i), 0] for i in range(10)],
                      "pool": [[nc.alloc_semaphore("dpl%d" % i), 0] for i in range(6)]}
        self.drr = {"sp": 0, "pool": 0}
        self.nwait = 0

    def resolve(self, tok):
        if tok[0] == "d":
            return tok[1], tok[2]
        e = self.E[tok[1]]
        idx = tok[2]
        if e["am"]:
            return e["sem"], idx + 1
        marks = e["marks"]
        k = bisect.bisect_left(marks, idx)
        if k == len(marks):
            e["last"].then_inc(e["sem"], 1)
            marks.append(e["n"] - 1)
        return e["sem"], k + 1

    def _deps(self, eng, r, w, is_dma):
        need = {}
        e = self.E[eng]

        def add(tok, raw):
            if tok is None:
                return
            if tok[0] == "e" and tok[1] == eng and not is_dma:
                if eng == "pe":
                    return
            sem, val = self.resolve(tok)
            if e["seen"].get(sem.num, 0) >= val:
                return
            if need.get(sem.num, (None, 0))[1] < val:
                need[sem.num] = (sem, val)

        for res in r:
            add(res.w, True)
        for res in w:
            add(res.w, False)
            for tok in res.r.values():
                add(tok, False)
        return need

    def _emit(self, e, need, fn):
        items = list(need.values())
        for sem, val in items[:-1]:
            e["h"].wait_ge(sem, val)
            self.nwait += 1
        ins = fn(e["h"])
        if items:
            ins._wait_ge(*items[-1])
        for sem, val in items:
            e["seen"][sem.num] = val
        return ins

    def op(self, eng, fn, r=(), w=(), mark=False):
        e = self.E[eng]
        need = self._deps(eng, r, w, False)
        ins = self._emit(e, need, fn)
        idx = e["n"]
        e["n"] += 1
        e["last"] = ins
        if e["am"] or mark:
            ins.then_inc(e["sem"], 1)
            e["marks"].append(idx)
        tok = ("e", eng, idx)
        for res in r:
            res.r[eng] = tok
        for res in w:
            res.w = tok
            res.r = {}
        return ins

    def dma(self, q, out, in_, r=(), w=()):
        e = self.E[q]
        pool = self.dsems[q]
        i = self.drr[q]
        self.drr[q] = (i + 1) % len(pool)
        S = pool[i]
        need = self._deps(q, r, w, True)
        if S[1] > 0 and e["seen"].get(S[0].num, 0) < S[1]:
            if need.get(S[0].num, (None, 0))[1] < S[1]:
                need[S[0].num] = (S[0], S[1])
        ins = self._emit(e, need, lambda h: h.dma_start(out=out, in_=in_))
        ins.then_inc(S[0], 16)
        S[1] += 16
        tok = ("d", S[0], S[1])
        key = "dma_" + q + str(i)
        for res in r:
            res.r[key] = tok
        for res in w:
            res.w = tok
            res.r = {}
        return tok

    def barrier(self, engs=("pe", "act", "dve")):
        toks = {}
        for x in engs:
            if self.E[x]["n"] > 0:
                toks[x] = self.resolve(("e", x, self.E[x]["n"] - 1))
        for x in engs:
            e = self.E[x]
            for y, (sem, val) in toks.items():
                if y == x:
                    continue
                if e["seen"].get(sem.num, 0) < val:
                    e["h"].wait_ge(sem, val)
                    e["seen"][sem.num] = val

    def wait_all_dma(self, q):
        e = self.E[q]
        for S in self.dsems["sp"] + self.dsems["pool"]:
            if S[1] > 0:
                e["h"].wait_ge(S[0], S[1])


def build(NL, dbg=False, stop=99):
    nc = bass.Bass("TRN2", target_bir_lowering=False)
    tr = Tracker(nc)

    xT_d = nc.dram_tensor("xT", [D, T], F32, kind="ExternalInput").ap()
    cT_d = nc.dram_tensor("cT", [128, 8], F32, kind="ExternalInput").ap()
    par_d = nc.dram_tensor("par", [128, NL * NPAR], F32, kind="ExternalInput").ap()
    wf_d = nc.dram_tensor("wf", [128, NL * 64], F32, kind="ExternalInput").ap()
    wA_d = nc.dram_tensor("wA", [NL * 12, 128, 4096], F32, kind="ExternalInput").ap()
    wIN_d = nc.dram_tensor("wIN", [NL * 5, 128, 4096], F32, kind="ExternalInput").ap()
    wO_d = nc.dram_tensor("wO", [NL * 2, 128, 4096], F32, kind="ExternalInput").ap()
    wFI_d = nc.dram_tensor("wFI", [NL * 11, 128, 4096], F32, kind="ExternalInput").ap()
    wFO_d = nc.dram_tensor("wFO", [NL * 8, 128, DFF], F32, kind="ExternalInput").ap()
    out_d = nc.dram_tensor("outT", [D, T], F32, kind="ExternalOutput").ap()
    scr_d = nc.dram_tensor("scr", [8, 3 * T], BF16, kind="Internal").ap()

    ARENA = 212736
    arena = nc.alloc_sbuf_tensor("arena", [128, ARENA], U8)
    base = nc.lookup_mloc(arena).addr
    cur = [base]

    def at(name, shape, dt, off=None):
        nb = int(np.prod(shape[1:])) * (4 if dt == F32 else 2)
        if off is None:
            off = cur[0]
            cur[0] += (nb + 31) // 32 * 32
            assert cur[0] <= base + ARENA, (name, cur[0] - base)
        return nc.alloc_sbuf_tensor_at(name, list(shape), dt, offset=off)

    xT = at("xTs", [128, 8, T], F32)
    h_off = cur[0]
    hT = at("hT", [128, 8, T], BF16)
    r3_off = cur[0]
    qT = at("qT", [128, 4, T], BF16)
    k_off = cur[0]
    kT = at("kT", [128, 4, T], BF16)
    va = at("va", [128, 16, NH, 65], BF16)
    UW = T + 32
    u_off = cur[0]
    uT = at("uT", [128, 4, UW], BF16)
    r3_end = cur[0]
    sq_off = cur[0]
    sq = at("sq", [128, 8, 512], BF16)
    ring = [at("ring%d" % i, [128, 4096], BF16) for i in range(3)]
    tmp_off = cur[0]
    tmp = [at("tmp%d" % i, [128, 512], F32) for i in range(4)]
    rstd_t = tmp[3]
    ident = at("ident", [128, 128], BF16)
    ones_b = at("ones_b", [128, 128], BF16)
    ones_f = at("ones_f", [128, 64], F32)
    maskb = at("maskb", [128, 128], BF16)
    par = at("par_s", [128, NL * NPAR], F32)
    wf = at("wf_s", [128, NL * 64], BF16)
    cTs = at("cTs", [128, 8], F32)
    cact = at("cact", [128, 8], BF16)
    modA = [at("mod%d" % i, [128, 48], F32) for i in range(2)]
    derA = [at("der%d" % i, [128, 32], F32) for i in range(2)]
    nbfA = [at("nbf%d" % i, [128, 1], F32) for i in range(2)]
    hid = at("hid", [128, NHC, 1024], BF16, off=r3_off)
    ys0 = at("ys0", [128, 8, 512], F32, off=r3_off + NHC * 1024 * 2)
    assert r3_off + NHC * 1024 * 2 + 8 * 512 * 4 <= r3_end
    ystage = at("ystage", [128, 8, 512], F32, off=k_off)
    hF32 = at("hF32", [128, 8, 1024], F32, off=h_off)
    diag = at("diag", [128, 4, CK, 128], BF16, off=h_off)
    ptb2 = [at("ptb2_%d" % i, [128, 2, 512], BF16, off=h_off + i * 2048) for i in range(2)]
    rden = at("rden", [128, 2, 512], F32, off=h_off + 4096)
    rhi = at("rhi", [128, 2, 512], BF16, off=h_off + 8192)
    rlo = at("rlo", [128, 2, 512], BF16, off=h_off + 10240)
    otmp = at("otmp", [128, 2, 512], F32, off=h_off + 12288)
    cumq = [at("cumq0", [128, T], BF16, off=h_off + 20480), at("cumq1", [128, T], BF16, off=sq_off)]
    cumk = [at("cumk0", [128, T], BF16, off=h_off + 24576), at("cumk1", [128, T], BF16, off=sq_off + 4096)]
    spl = at("spl", [8, 3 * T], BF16, off=u_off)
    fT = at("fT", [8, T], F32, off=sq_off)
    cumT = at("cumT", [8, T], F32, off=tmp_off)

    pp = [nc.alloc_psum_tensor("pp%d" % i, [128, 2, 512], F32) for i in range(4)]
    ps = [pp[i // 2][:, i % 2, :] for i in range(8)]
    bank = [Res() for _ in range(8)]

    xres = [[Res() for _ in range(4)] for _ in range(8)]
    hres = [[Res() for _ in range(4)] for _ in range(8)]
    qres = [[Res() for _ in range(4)] for _ in range(4)]
    kres = [[Res() for _ in range(4)] for _ in range(4)]
    vres = [Res() for _ in range(16)]
    ures = [[Res() for _ in range(4)] for _ in range(4)]
    sqres = [Res() for _ in range(8)]
    ringres = [Res() for _ in range(3)]
    tmpres = [Res() for _ in range(4)]
    rstdres = Res()
    constres = Res()
    parres = Res()
    modresA = [Res(), Res()]
    derresA = [Res(), Res()]
    ysres = [Res() for _ in range(8)]
    ptres = [Res() for _ in range(2)]
    otres = [Res() for _ in range(2)]
    rdres = Res()
    rhres = Res()
    rlres = Res()
    cumres = [Res(), Res()]
    splres = Res()
    fres = Res()
    cumTres = Res()
    scrres = Res()
    hidres = [[Res() for _ in range(2)] for _ in range(NHC)]
    diagres = Res()
    recres = Res()
    miscres = Res()

    op = tr.op

    sched = []
    for t in range(12):
        sched.append((wA_d[t], 4096))
    for l in range(NL):
        for t in range(5):
            sched.append((wIN_d[l * 5 + t], 4096))
        for t in range(2):
            sched.append((wO_d[l * 2 + t], 4096))
        for half in range(2):
            for t in range(11):
                sched.append((wFI_d[l * 11 + t], 4096))
                if half == 0 and l + 1 < NL:
                    sched.append((wA_d[(l + 1) * 12 + t], 4096))
                    if t == 10:
                        sched.append((wA_d[(l + 1) * 12 + 11], 4096))
            for t in range(8):
                sched.append((wFO_d[l * 8 + t], DFF))
    wstate = dict(issued=0, used=0)

    def wissue(upto):
        while wstate["issued"] < min(upto, len(sched)):
            i = wstate["issued"]
            src, ncols = sched[i]
            s = i % 3
            tr.dma("pool", out=ring[s][:, 0:ncols], in_=src, w=[ringres[s]])
            wstate["issued"] += 1

    def wnext(ahead=3):
        n = wstate["used"]
        wstate["used"] += 1
        wissue(n + ahead)
        s = n % 3
        return ring[s], ringres[s]

    for kc in range(8):
        tr.dma("sp", out=xT[:, kc, :], in_=xT_d[kc * 128:(kc + 1) * 128, :], w=xres[kc])
    tr.dma("sp", out=par[:], in_=par_d[:, :], w=[parres])
    tr.dma("sp", out=cTs[:], in_=cT_d[:, :], w=[miscres])
    tr.dma("pool", out=wf[:], in_=wf_d[:, :], w=[constres])
    op("pool", lambda g: g.memset(ones_b[:], 1.0), w=[constres])
    op("pool", lambda g: g.memset(ones_f[:], 1.0), w=[constres])
    op("pool", lambda g: g.memset(tmp[0][:], 0.0), w=[tmpres[0]])
    op("pool", lambda g: g.memset(tmp[1][:], 1.0), w=[tmpres[1]])
    op("pool", lambda g: g.affine_select(out=maskb[:], in_=tmp[0][:, 0:128], pattern=[[1, 128]], compare_op=ALU.is_ge,
                                         fill=-30000.0, base=0, channel_multiplier=-1), r=[tmpres[0]], w=[constres])
    op("pool", lambda g: g.affine_select(out=ident[:], in_=tmp[1][:, 0:128], pattern=[[1, 128]], compare_op=ALU.is_equal,
                                         fill=0.0, base=0, channel_multiplier=-1), r=[tmpres[1]], w=[constres])
    op("pool", lambda g: g.memset(va[:, :, :, 64:65], 1.0), w=vres)
    op("pool", lambda g: g.memset(uT[:, :, 0:30], 0.0), w=[ures[fc][0] for fc in range(4)])
    op("act", lambda a: a.activation(out=cact[:], in_=cTs[:], func=AF.Silu), r=[miscres], w=[constres])
    wissue(3)
    tr.barrier(("pe", "act", "dve", "pool"))

    evac_rr = [0]

    def evac(out_ap, in_ap, r, w, scale=None):
        evac_rr[0] ^= 1
        if evac_rr[0]:
            if scale is None:
                op("act", lambda a: a.activation(out=out_ap, in_=in_ap, func=AF.Copy), r=r, w=w)
            else:
                op("act", lambda a: a.activation(out=out_ap, in_=in_ap, func=AF.Identity, scale=scale), r=r, w=w)
        else:
            if scale is None:
                op("dve", lambda v: v.tensor_copy(out=out_ap, in_=in_ap), r=r, w=w)
            else:
                op("dve", lambda v: v.tensor_scalar(out=out_ap, in0=in_ap, scalar1=scale, scalar2=None, op0=ALU.mult),
                   r=r, w=w)

    def rstd_from(bk, scale):
        op("act", lambda a: a.activation(out=rstd_t[:], in_=ps[bk][:], func=AF.Ln, bias=EPS, scale=scale),
           r=[bank[bk]], w=[rstdres])
        op("act", lambda a: a.activation(out=rstd_t[:], in_=rstd_t[:], func=AF.Exp, scale=-0.5),
           r=[rstdres], w=[rstdres])

    trr = [0]

    def gettmp():
        trr[0] = (trr[0] + 1) % 3
        return tmp[trr[0]], tmpres[trr[0]]

    def ada_tile(l, t):
        W, Wr = wnext()
        for jj in range(4):
            j = t * 4 + jj
            for kc in range(8):
                op("pe", lambda p: p.matmul(ps[7][:, j:j + 1], lhsT=W[:, kc * 512 + jj * 128: kc * 512 + jj * 128 + 128],
                                            rhs=cact[:, kc:kc + 1], start=(kc == 0), stop=(kc == 7)),
                   r=[Wr, constres], w=[bank[7]], mark=(kc == 7 and jj == 3))

    def ada_fin(l):
        pb = l * NPAR
        mod, der, nbf, modres, derres = modA[l % 2], derA[l % 2], nbfA[l % 2], modresA[l % 2], derresA[l % 2]
        op("dve", lambda v: v.tensor_tensor(out=mod[:], in0=ps[7][:, 0:48], in1=par[:, pb + P_ADAB: pb + P_ADAB + 48], op=ALU.add),
           r=[bank[7], parres], w=[modres])
        op("dve", lambda v: v.scalar_tensor_tensor(out=der[:, 0:8], in0=mod[:, 8:16], scalar=1.0, in1=par[:, pb + P_PRE1: pb + P_PRE1 + 8],
                                                   op0=ALU.add, op1=ALU.mult), r=[modres, parres], w=[derres])
        op("dve", lambda v: v.tensor_tensor(out=der[:, 8:16], in0=mod[:, 16:24], in1=par[:, pb + P_POST1: pb + P_POST1 + 8], op=ALU.mult),
           r=[modres, parres], w=[derres])
        op("dve", lambda v: v.scalar_tensor_tensor(out=der[:, 16:24], in0=mod[:, 32:40], scalar=1.0, in1=par[:, pb + P_PRE2: pb + P_PRE2 + 8],
                                                   op0=ALU.add, op1=ALU.mult), r=[modres, parres], w=[derres])
        op("dve", lambda v: v.tensor_tensor(out=der[:, 24:32], in0=mod[:, 40:48], in1=par[:, pb + P_POST2: pb + P_POST2 + 8], op=ALU.mult),
           r=[modres, parres], w=[derres])
        op("dve", lambda v: v.tensor_scalar(out=nbf[:], in0=par[:, pb + P_BF: pb + P_BF + 1], scalar1=-1.0, scalar2=None, op0=ALU.mult),
           r=[parres], w=[derres])

    def prenorm_a(tb):
        ts = slice(tb * 512, (tb + 1) * 512)
        for kc in range(8):
            op("act", lambda a: a.activation(out=sq[:, kc, :], in_=xT[:, kc, ts], func=AF.Square),
               r=[xres[kc][tb]], w=[sqres[kc]])

    def prenorm_b(l, tb, a_off, sh_off):
        mod, der, modres, derres = modA[l % 2], derA[l % 2], modresA[l % 2], derresA[l % 2]
        ts = slice(tb * 512, (tb + 1) * 512)
        for kc in range(8):
            op("pe", lambda p: p.matmul(ps[6][:], lhsT=ones_b[:], rhs=sq[:, kc, :], start=(kc == 0), stop=(kc == 7)),
               r=[sqres[kc], constres], w=[bank[6]], mark=(kc == 7))
        rstd_from(6, 1.0 / D)
        for kc in range(8):
            tt, ttr = gettmp()
            op("dve", lambda v: v.tensor_tensor(out=tt[:], in0=xT[:, kc, ts], in1=rstd_t[:], op=ALU.mult),
               r=[xres[kc][tb], rstdres], w=[ttr])
            op("act", lambda a: a.activation(out=hT[:, kc, ts], in_=tt[:], func=AF.Identity,
                                             bias=mod[:, sh_off + kc: sh_off + kc + 1], scale=der[:, a_off + kc: a_off + kc + 1]),
               r=[ttr, modres, derres], w=[hres[kc][tb]])

    def postnorm_residual(l, ys, ysr, tb, g_off, extra_r=()):
        der, derres = derA[l % 2], derresA[l % 2]
        ts = slice(tb * 512, (tb + 1) * 512)
        for kc in range(8):
            op("pe", lambda p: p.matmul(ps[6][:], lhsT=ones_b[:], rhs=sq[:, kc, :], start=(kc == 0), stop=(kc == 7)),
               r=[sqres[kc], constres], w=[bank[6]], mark=(kc == 7))
        rstd_from(6, 1.0 / D)
        for kc in range(8):
            tt, ttr = gettmp()
            op("dve", lambda v: v.scalar_tensor_tensor(out=tt[:], in0=ys[:, kc, :], scalar=der[:, g_off + kc: g_off + kc + 1],
                                                       in1=rstd_t[:], op0=ALU.mult, op1=ALU.mult),
               r=(ysr[kc] if isinstance(ysr[kc], list) else [ysr[kc]]) + [derres, rstdres] + list(extra_r), w=[ttr])
            op("dve", lambda v: v.tensor_tensor(out=xT[:, kc, ts], in0=xT[:, kc, ts], in1=tt[:], op=ALU.add),
               r=[ttr, xres[kc][tb]], w=[xres[kc][tb]])

    brr = [0]

    def nextbank(n=6):
        brr[0] = (brr[0] + 1) % n
        return brr[0]

    def proj_fm(W, Wr, col0, tb, bk):
        ts = slice(tb * 512, (tb + 1) * 512)
        for kc in range(8):
            op("pe", lambda p: p.matmul(ps[bk][:], lhsT=W[:, kc * 512 + col0: kc * 512 + col0 + 128], rhs=hT[:, kc, ts],
                                        start=(kc == 0), stop=(kc == 7)),
               r=[Wr, hres[kc][tb]], w=[bank[bk]], mark=(kc == 7))

    def inproj(l):
        op("dve", lambda v: v.memset(va[:, :, :, 64:65], 1.0), w=vres)
        for tb in range(4):
            ts = slice(tb * 512, (tb + 1) * 512)
            for kc in range(8):
                op("pe", lambda p: p.matmul(ps[6][0:8, :], lhsT=wf[:, l * 64 + kc * 8: l * 64 + kc * 8 + 8], rhs=hT[:, kc, ts],
                                            start=(kc == 0), stop=(kc == 7)),
                   r=[constres, hres[kc][tb]], w=[bank[6]], mark=(kc == 7))
            op("act", lambda a: a.activation(out=fT[0:8, ts], in_=ps[6][0:8, :], func=AF.Exp, bias=nbfA[l % 2][0:8, 0:1], scale=-1.0),
               r=[bank[6], derresA[l % 2]], w=[fres] + sqres)
        op("act", lambda a: a.activation(out=fT[:], in_=fT[:], func=AF.Ln, bias=1.0, scale=1.0), r=[fres], w=[fres])

        for which in range(2):
            W, Wr = wnext()
            dst, dres = (qT, qres) if which == 0 else (kT, kres)
            for fc in range(4):
                for tb in range(4):
                    bk = nextbank()
                    proj_fm(W, Wr, fc * 128, tb, bk)
                    evac(dst[:, fc, tb * 512:(tb + 1) * 512], ps[bk][:], [bank[bk]], [dres[fc][tb]],
                         scale=(0.125 if which == 0 else None))
            if which == 0:
                fchain()
                op("dve", lambda v: v.memset(uT[:, :, 0:30], 0.0), w=[ures[fc][0] for fc in range(4)] + [splres])
        W, Wr = wnext()
        for tk in range(16):
            bk = nextbank()
            for kc in range(8):
                op("pe", lambda p: p.matmul(ps[bk][:], lhsT=hT[:, kc, tk * 128:(tk + 1) * 128], rhs=W[:, kc * 512:(kc + 1) * 512],
                                            start=(kc == 0), stop=(kc == 7)),
                   r=[Wr, hres[kc][tk // 4]], w=[bank[bk]], mark=(kc == 7))
            evac(va[:, tk, :, 0:64], ps[bk][:].rearrange("p (h d) -> p h d", h=NH), [bank[bk]], [vres[tk]])
        for half in range(2):
            W, Wr = wnext()
            for c2 in range(2):
                fc = half * 2 + c2
                for tb in range(4):
                    b0 = nextbank()
                    proj_fm(W, Wr, c2 * 128, tb, b0)
                    b1 = nextbank()
                    proj_fm(W, Wr, 256 + c2 * 128, tb, b1)
                    tt, ttr = gettmp()
                    op("act", lambda a: a.activation(out=tt[:], in_=ps[b1][:], func=AF.Sigmoid), r=[bank[b1]], w=[ttr])
                    op("dve", lambda v: v.tensor_tensor(out=uT[:, fc, 30 + tb * 512: 30 + (tb + 1) * 512], in0=ps[b0][:], in1=tt[:],
                                                        op=ALU.mult), r=[bank[b0], ttr], w=[ures[fc][tb]])
    def fchain():
        op("dve", lambda v: v.tensor_tensor_scan(out=cumT[:], data0=fT[:], data1=fT[:], initial=0.0, op0=ALU.add, op1=ALU.max),
           r=[fres] + tmpres, w=[cumTres] + tmpres)
        op("dve", lambda v: v.tensor_copy(out=spl[:, 0:T], in_=cumT[:]), r=[cumTres], w=[splres])
        op("dve", lambda v: v.tensor_tensor(out=fT[:], in0=cumT[:], in1=spl[:, 0:T], op=ALU.subtract), r=[cumTres, splres], w=[fres])
        op("dve", lambda v: v.tensor_copy(out=spl[:, T:2 * T], in_=fT[:]), r=[fres], w=[splres])
        op("dve", lambda v: v.tensor_tensor(out=cumT[:], in0=fT[:], in1=spl[:, T:2 * T], op=ALU.subtract), r=[fres, splres], w=[cumTres])
        op("dve", lambda v: v.tensor_copy(out=spl[:, 2 * T:3 * T], in_=cumT[:]), r=[cumTres], w=[splres])
        tr.dma("sp", out=scr_d[:, :], in_=spl[:], r=[splres], w=[scrres])

    def attention():
        for b in range(2):
            extra = [] if b == 0 else ([fres] + sqres)
            op("dve", lambda v: v.memset(cumq[b][:], 1.0), w=[cumres[b]] + extra)
            op("dve", lambda v: v.memset(cumk[b][:], 0.0), w=[cumres[b]] + extra)
            op("dve", lambda v: v.memset(cumk[b][0:3, :], -1.0), w=[cumres[b]])
            op("dve", lambda v: v.memset(cumk[b][64:67, :], -1.0), w=[cumres[b]])
        cnt = 0
        pending = [None]
        for c in range(4):
            fc = c
            b = c % 2
            for hx in range(2):
                h = 2 * c + hx
                src = scr_d[h:h + 1, :].rearrange("o (r t) -> (o r) t", r=3)
                tr.dma("sp", out=cumq[b][64 * hx:64 * hx + 3, :], in_=src, r=[scrres], w=[cumres[b]])
                tr.dma("sp", out=cumk[b][64 * hx + 3:64 * hx + 6, :], in_=src, r=[scrres], w=[cumres[b]])
            for i in range(4):
                nj = 4 * i + 4

                def s_tile(j):
                    nonlocal cnt
                    t = cnt % 2
                    cnt += 1
                    c0 = max(0, j - 4 * i) * 128
                    qs = slice(i * 512 + c0, (i + 1) * 512)
                    ks = slice(j * 128, (j + 1) * 128)
                    diagonal = j >= 4 * i
                    bks = [bank[2 * t], bank[2 * t + 1]]
                    for hx in range(2):
                        pr = 64 * hx
                        op("pe", lambda p: p.matmul(ps[2 * t + hx][:, c0:512], lhsT=kT[pr:pr + 64, fc, ks], rhs=qT[pr:pr + 64, fc, qs],
                                                    start=True, stop=False),
                           r=[kres[fc][j // 4], qres[fc][i]], w=[bks[hx]])
                    if diagonal:
                        for hx in range(2):
                            op("pe", lambda p: p.matmul(ps[2 * t + hx][:, c0:c0 + 128], lhsT=ident[:], rhs=maskb[:], start=False, stop=False),
                               r=[constres], w=[bks[hx]])
                    for hx in range(2):
                        pr = 64 * hx
                        op("pe", lambda p: p.matmul(ps[2 * t + hx][:, c0:512], lhsT=cumk[b][pr:pr + 64, ks], rhs=cumq[b][pr:pr + 64, qs],
                                                    start=False, stop=True),
                           r=[cumres[b]], w=[bks[hx]], mark=(hx == 1))
                    op("act", lambda a: a.activation(out=ptb2[t][:, :, c0:512], in_=pp[t][:, :, c0:512], func=AF.Exp),
                       r=bks, w=[ptres[t]])
                    return (j, c0, t)

                def pv_tile(tl):
                    j, c0, t = tl
                    for hx in range(2):
                        hh = 2 * c + hx
                        mw = 128 if hh < 7 else 65
                        op("pe", lambda p: p.matmul(ps[4 + hx][0:mw, c0:512], lhsT=va[:, j, :, :].rearrange("p h d -> p (h d)")[:, hh * 65:hh * 65 + mw],
                                                    rhs=ptb2[t][:, hx, c0:512],
                                                    start=(j == 0), stop=(j == nj - 1)),
                           r=[ptres[t], vres[j]], w=[bank[4 + hx]], mark=(j == nj - 1 and hx == 1))

                prev = s_tile(0)
                if pending[0] is not None:
                    pending[0]()
                    pending[0] = None
                for j in range(1, nj):
                    curt = s_tile(j)
                    pv_tile(prev)
                    prev = curt
                pv_tile(prev)
                for hx in range(2):
                    ob = 4 + hx
                    op("act", lambda a: a.activation(out=rden[64:65, hx, :], in_=ps[ob][64:65, :], func=AF.Ln), r=[bank[ob]], w=[rdres])
                    op("act", lambda a: a.activation(out=rden[64:65, hx, :], in_=rden[64:65, hx, :], func=AF.Exp, scale=-1.0),
                       r=[rdres], w=[rdres])
                    op("dve", lambda v: v.tensor_copy(out=otmp[0:64, hx, :], in_=ps[ob][0:64, :]), r=[bank[ob]], w=[otres[hx]])

                def fin(fc=fc, i=i):
                    for hx in range(2):
                        bb = 6 + hx
                        op("pe", lambda p: p.matmul(ps[bb][0:64, :], lhsT=ones_f[64:65, 0:64], rhs=rden[64:65, hx, :], start=True, stop=True),
                           r=[rdres, constres], w=[bank[bb]], mark=True)
                    for hx in range(2):
                        bb = 6 + hx
                        pr = 64 * hx
                        op("dve", lambda v: v.tensor_tensor(out=qT[pr:pr + 64, fc, i * 512:(i + 1) * 512], in0=otmp[0:64, hx, :],
                                                            in1=ps[bb][0:64, :], op=ALU.mult),
                           r=[otres[hx], bank[bb]], w=[qres[fc][i]])

                pending[0] = fin
        pending[0]()

    def conv(l):
        pb = l * NPAR
        for fc in range(4):
            op("dve", lambda v: v.tensor_tensor(out=diag[:, fc, :, :],
                                                in0=ident[:].unsqueeze(1).broadcast_to([128, CK, 128]),
                                                in1=par[:, pb + P_CONVW + fc * CK: pb + P_CONVW + (fc + 1) * CK].unsqueeze(2).broadcast_to([128, CK, 128]),
                                                op=ALU.mult), r=[constres, parres], w=[diagres])
        for tb in (3, 2, 1, 0):
            for fc in range(4):
                rr = [ures[fc][tb], diagres] + ([ures[fc][tb - 1]] if tb > 0 else [])
                for k in range(CK):
                    op("pe", lambda p: p.matmul(ps[fc][:], lhsT=diag[:, fc, k, :], rhs=uT[:, fc, tb * 512 + k: tb * 512 + k + 512],
                                                start=(k == 0), stop=(k == CK - 1)), r=rr, w=[bank[fc]], mark=(k == CK - 1))
            for fc in range(4):
                op("act", lambda a: a.activation(out=ystage[:, fc, :], in_=ps[fc][:], func=AF.Identity,
                                                 bias=par[:, pb + P_CONVB + fc: pb + P_CONVB + fc + 1], scale=1.0),
                   r=[bank[fc], parres], w=[ysres[fc]])
                op("dve", lambda v: v.tensor_copy(out=sq[:, fc, :], in_=ystage[:, fc, :]), r=[ysres[fc]], w=[sqres[fc]])
                op("act", lambda a: a.activation(out=sq[:, 4 + fc, :], in_=ystage[:, fc, :], func=AF.Square),
                   r=[ysres[fc]], w=[sqres[4 + fc]])
            for fc in range(4):
                op("pe", lambda p: p.matmul(ps[4][:], lhsT=ones_b[:], rhs=sq[:, fc, :], start=(fc == 0), stop=(fc == 3)),
                   r=[sqres[fc], constres], w=[bank[4]], mark=(fc == 3))
            for fc in range(4):
                op("pe", lambda p: p.matmul(ps[5][:], lhsT=ones_b[:], rhs=sq[:, 4 + fc, :], start=(fc == 0), stop=(fc == 3)),
                   r=[sqres[4 + fc], constres], w=[bank[5]], mark=(fc == 3))
            mean, meanr = tmp[0], tmpres[0]
            var, varr = tmp[1], tmpres[1]
            op("dve", lambda v: v.tensor_scalar(out=mean[:], in0=ps[4][:], scalar1=1.0 / 512, scalar2=None, op0=ALU.mult),
               r=[bank[4]], w=[meanr])
            op("dve", lambda v: v.tensor_tensor(out=var[:], in0=mean[:], in1=mean[:], op=ALU.mult), r=[meanr], w=[varr])
            op("dve", lambda v: v.scalar_tensor_tensor(out=var[:], in0=ps[5][:], scalar=1.0 / 512, in1=var[:], op0=ALU.mult, op1=ALU.subtract),
               r=[bank[5], varr], w=[varr])
            op("act", lambda a: a.activation(out=rstd_t[:], in_=var[:], func=AF.Ln, bias=EPS, scale=1.0), r=[varr], w=[rstdres])
            op("act", lambda a: a.activation(out=rstd_t[:], in_=rstd_t[:], func=AF.Exp, scale=-0.5), r=[rstdres], w=[rstdres])
            for fc in range(4):
                op("dve", lambda v: v.tensor_tensor(out=ystage[:, fc, :], in0=ystage[:, fc, :], in1=mean[:], op=ALU.subtract),
                   r=[ysres[fc], meanr], w=[ysres[fc]])
                op("dve", lambda v: v.tensor_tensor(out=ystage[:, fc, :], in0=ystage[:, fc, :], in1=rstd_t[:], op=ALU.mult),
                   r=[ysres[fc], rstdres], w=[ysres[fc]])
                op("act", lambda a: a.activation(out=uT[:, fc, 30 + tb * 512: 30 + (tb + 1) * 512], in_=ystage[:, fc, :], func=AF.Silu,
                                                 bias=par[:, pb + P_LNB + fc: pb + P_LNB + fc + 1],
                                                 scale=par[:, pb + P_LNG + fc: pb + P_LNG + fc + 1]),
                   r=[ysres[fc], parres], w=[ures[fc][tb]])

    def wo(l):
        W0, W0r = wnext()
        W1, W1r = wnext(ahead=2)

        def mm(tb, oc):
            ts = slice(tb * 512, (tb + 1) * 512)
            us = slice(30 + tb * 512, 30 + (tb + 1) * 512)
            W, Wr = (W0, W0r) if oc < 4 else (W1, W1r)
            c0 = (oc % 4) * 128
            bk = nextbank(4)
            for kc in range(8):
                if kc < 4:
                    rhs, rr = qT[:, kc, ts], qres[kc][tb]
                else:
                    rhs, rr = uT[:, kc - 4, us], ures[kc - 4][tb]
                op("pe", lambda p: p.matmul(ps[bk][:], lhsT=W[:, kc * 512 + c0: kc * 512 + c0 + 128], rhs=rhs,
                                            start=(kc == 0), stop=(kc == 7)), r=[Wr, rr], w=[bank[bk]], mark=(kc == 7))
            return bk

        def ev(oc, bk):
            op("dve", lambda v: v.tensor_copy(out=ystage[:, oc, :], in_=ps[bk][:]), r=[bank[bk]], w=[ysres[oc]])
            op("act", lambda a: a.activation(out=sq[:, oc, :], in_=ystage[:, oc, :], func=AF.Square), r=[ysres[oc]], w=[sqres[oc]])

        for oc in range(8):
            ev(oc, mm(0, oc))
        for tb in range(4):
            ahead = [mm(tb + 1, oc) for oc in range(4)] if tb < 3 else []
            postnorm_residual(l, ystage, ysres, tb, 8)
            prenorm_a(tb)
            prenorm_b(l, tb, 16, 24)
            if tb < 3:
                for oc in range(4):
                    ev(oc, ahead[oc])
                for oc in range(4, 8):
                    ev(oc, mm(tb + 1, oc))

    def ffn(l):
        nxt = l + 1 < NL
        for half in range(2):
            fcnt = 0
            for s in range(11):
                W, Wr = wnext()
                for c2 in range(2):
                    hc = 2 * s + c2
                    for tbl in range(2):
                        tb = half * 2 + tbl
                        bg = fcnt % 2
                        bu = 2 + fcnt % 2
                        fcnt += 1
                        proj_fm(W, Wr, c2 * 128, tb, bg)
                        proj_fm(W, Wr, 256 + c2 * 128, tb, bu)
                        tt, ttr = gettmp()
                        op("act", lambda a: a.activation(out=tt[:], in_=ps[bg][:], func=AF.Silu), r=[bank[bg]], w=[ttr])
                        op("dve", lambda v: v.tensor_tensor(out=hid[:, hc, tbl * 512:(tbl + 1) * 512], in0=ps[bu][:], in1=tt[:], op=ALU.mult),
                           r=[bank[bu], ttr], w=[hidres[hc][tbl]])
                if half == 0 and nxt:
                    ada_tile(l + 1, s)
                    if s == 10:
                        ada_tile(l + 1, 11)
                        ada_fin(l + 1)
            ys1 = hF32[:, :, half * 512:(half + 1) * 512]
            ys1res = [[hres[oc][half * 2], hres[oc][half * 2 + 1]] for oc in range(8)]
            for oc in range(8):
                W, Wr = wnext()
                pair = ((0, 1), (2, 3), (4, 5))[oc % 3]
                for tbl in range(2):
                    bk = pair[tbl]
                    for hc in range(NHC):
                        op("pe", lambda p: p.matmul(ps[bk][:], lhsT=W[:, hc * 128:(hc + 1) * 128], rhs=hid[:, hc, tbl * 512:(tbl + 1) * 512],
                                                    start=(hc == 0), stop=(hc == NHC - 1)),
                           r=[Wr, hidres[hc][tbl]], w=[bank[bk]], mark=(hc == NHC - 1))
                evac(ys0[:, oc, :], ps[pair[0]][:], [bank[pair[0]]], [ysres[oc]])
                evac(ys1[:, oc, :], ps[pair[1]][:], [bank[pair[1]]], ys1res[oc])
                if half == 1 and nxt:
                    if oc == 0:
                        prenorm_a(0)
                    elif oc == 2:
                        prenorm_b(l + 1, 0, 0, 0)
                    elif oc == 4:
                        prenorm_a(1)
                    elif oc == 6:
                        prenorm_b(l + 1, 1, 0, 0)
            for kc in range(8):
                op("act", lambda a: a.activation(out=sq[:, kc, :], in_=ys0[:, kc, :], func=AF.Square), r=[ysres[kc]], w=[sqres[kc]])
            postnorm_residual(l, ys0, ysres, half * 2, 24)
            for kc in range(8):
                op("act", lambda a: a.activation(out=sq[:, kc, :], in_=ys1[:, kc, :], func=AF.Square), r=ys1res[kc], w=[sqres[kc]])
            postnorm_residual(l, ys1, ys1res, half * 2 + 1, 24)
        if nxt:
            for tb in (2, 3):
                prenorm_a(tb)
                prenorm_b(l + 1, tb, 0, 0)

    for t in range(12):
        ada_tile(0, t)
    ada_fin(0)
    for tb in range(4):
        prenorm_a(tb)
        prenorm_b(0, tb, 0, 0)
    for l in range(NL):
        tr.barrier()
        inproj(l)
        tr.barrier()
        attention()
        tr.barrier()
        conv(l)
        tr.barrier()
        wo(l)
        tr.barrier()
        ffn(l)
    tr.barrier()
    for kc in range(8):
        tr.dma("sp", out=out_d[kc * 128:(kc + 1) * 128, :], in_=xT[:, kc, :], r=xres[kc])
    tr.wait_all_dma("sp")
    return nc


def _kc_tile(w2d, cols):
    sub = w2d[:, cols]
    n = sub.shape[1]
    return np.ascontiguousarray(sub.reshape(8, 128, n).transpose(1, 0, 2).reshape(128, 8 * n))


def prep_weights(NL, w_in, b_f, conv_w, conv_b, conv_ln_g, conv_ln_b, w_o, w_ffn_in, w_ffn_out,
                 mix_pre_g, mix_post_g, ffn_pre_g, ffn_post_g, ada_w, ada_b):
    f32 = np.float32
    wA = np.empty((NL * 12, 128, 4096), f32)
    wIN = np.empty((NL * 5, 128, 4096), f32)
    wO = np.empty((NL * 2, 128, 4096), f32)
    wFI = np.empty((NL * 11, 128, 4096), f32)
    wFO = np.empty((NL * 8, 128, DFF), f32)
    par = np.zeros((128, NL * NPAR), f32)
    wf = np.empty((128, NL * 64), f32)
    ar = np.arange
    for l in range(NL):
        for t in range(12):
            wA[l * 12 + t] = _kc_tile(ada_w[l], ar(t * 512, (t + 1) * 512))
        wi = w_in[l]
        wIN[l * 5 + 0] = _kc_tile(wi, ar(0, 512))
        wIN[l * 5 + 1] = _kc_tile(wi, ar(512, 1024))
        wIN[l * 5 + 2] = _kc_tile(wi, ar(1024, 1536))
        for half in range(2):
            cols = np.concatenate([ar(1544 + half * 256, 1544 + half * 256 + 256), ar(2056 + half * 256, 2056 + half * 256 + 256)])
            wIN[l * 5 + 3 + half] = _kc_tile(wi, cols)
        wf[:, l * 64:(l + 1) * 64] = _kc_tile(wi, ar(1536, 1544))
        for t in range(2):
            wO[l * 2 + t] = _kc_tile(w_o[l], ar(t * 512, (t + 1) * 512))
        for s in range(11):
            cols = np.concatenate([ar(s * 256, s * 256 + 256), ar(DFF + s * 256, DFF + s * 256 + 256)])
            wFI[l * 11 + s] = _kc_tile(w_ffn_in[l], cols)
        wo3 = w_ffn_out[l].reshape(NHC, 128, 8, 128)
        wFO[l * 8:(l + 1) * 8] = wo3.transpose(2, 1, 0, 3).reshape(8, 128, DFF)
        pb = l * NPAR

        def fm(v):
            return v.reshape(8, 128).T

        par[:, pb + P_PRE1: pb + P_PRE1 + 8] = fm(mix_pre_g[l])
        par[:, pb + P_POST1: pb + P_POST1 + 8] = fm(mix_post_g[l])
        par[:, pb + P_PRE2: pb + P_PRE2 + 8] = fm(ffn_pre_g[l])
        par[:, pb + P_POST2: pb + P_POST2 + 8] = fm(ffn_post_g[l])
        par[:, pb + P_ADAB: pb + P_ADAB + 48] = ada_b[l].reshape(48, 128).T
        par[:, pb + P_CONVW: pb + P_CONVW + 4 * CK] = conv_w[l].reshape(CK, 4, 128).transpose(2, 1, 0).reshape(128, 4 * CK)
        par[:, pb + P_CONVB: pb + P_CONVB + 4] = conv_b[l].reshape(4, 128).T
        par[:, pb + P_LNG: pb + P_LNG + 4] = conv_ln_g[l].reshape(4, 128).T
        par[:, pb + P_LNB: pb + P_LNB + 4] = conv_ln_b[l].reshape(4, 128).T
        par[0:8, pb + P_BF] = b_f[l]
    return dict(wA=wA, wIN=wIN, wO=wO, wFI=wFI, wFO=wFO, par=par, wf=wf)


_NC_CACHE = {}


def run_layers(x, c, NL, weights, cores=None):
    B = x.shape[0]
    if cores is None:
        cores = list(range(B))
    if NL not in _NC_CACHE:
        _NC_CACHE[NL] = build(NL)
    nc = _NC_CACHE[NL]
    in_maps = []
    for b in cores:
        m = dict(weights)
        m["xT"] = np.ascontiguousarray(x[b].T)
        m["cT"] = np.ascontiguousarray(c[b].reshape(8, 128).T)
        in_maps.append(m)
    res = run_bass_kernel_spmd(nc, in_maps, core_ids=list(range(len(cores))))
    out = np.empty((len(cores), T, D), np.float32)
    for i in range(len(cores)):
        out[i] = res.results[i]["outT"].T
    return out


def kernel(x, c, w_in, b_f, conv_w, conv_b, conv_ln_g, conv_ln_b, w_o, w_ffn_in, w_ffn_out,
           mix_pre_g, mix_post_g, ffn_pre_g, ffn_post_g, ada_w, ada_b):
    a = [np.asarray(v, dtype=np.float32) for v in (w_in, b_f, conv_w, conv_b, conv_ln_g, conv_ln_b, w_o, w_ffn_in, w_ffn_out,
                                                   mix_pre_g, mix_post_g, ffn_pre_g, ffn_post_g, ada_w, ada_b)]
    weights = prep_weights(L_FULL, *a)
    x = np.asarray(x, dtype=np.float32)
    c = np.asarray(c, dtype=np.float32)
    return run_layers(x, c, L_FULL, weights)
```

```python
import bisect
import os
import numpy as np
import concourse.bass as bass
import concourse.mybir as mybir
from concourse.bass_utils import run_bass_kernel_spmd

F32 = mybir.dt.float32
BF16 = mybir.dt.bfloat16
U8 = mybir.dt.uint8
AF = mybir.ActivationFunctionType
ALU = mybir.AluOpType

D = 1024
T = 2048
L_FULL = 4
NH = 8
DFF = 2816
NHC = DFF // 128
CK = 31
EPS = 1e-6
NPAR = 224
P_PRE1, P_POST1, P_PRE2, P_POST2, P_ADAB, P_CONVW, P_CONVB, P_LNG, P_LNB, P_BF = 0, 8, 16, 24, 32, 80, 204, 208, 212, 216


class Res:
    __slots__ = ("w", "r")

    def __init__(self):
        self.w = None
        self.r = {}


class Tracker:
    def __init__(self, nc):
        self.nc = nc
        self.E = {}
        for name, h, am in (("pe", nc.tensor, False), ("act", nc.scalar, True), ("dve", nc.vector, True),
                            ("pool", nc.gpsimd, True), ("sp", nc.sync, True)):
            self.E[name] = dict(h=h, sem=nc.alloc_semaphore("s_" + name), am=am, n=0, marks=[], last=None, seen={})
        self.dsems = {"sp": [[nc.alloc_semaphore("dsp%d" % i), 0] for i in range(10)],
                      "pool": [[nc.alloc_semaphore("dpl%d" % i), 0] for i in range(6)]}
        self.drr = {"sp": 0, "pool": 0}
        self.nwait = 0

    def resolve(self, tok):
        if tok[0] == "d":
            return tok[1], tok[2]
        e = self.E[tok[1]]
        idx = tok[2]
        if e["am"]:
            return e["sem"], idx + 1
        marks = e["marks"]
        k = bisect.bisect_left(marks, idx)
        if k == len(marks):
            e["last"].then_inc(e["sem"], 1)
            marks.append(e["n"] - 1)
        return e["sem"], k + 1

    def _deps(self, eng, r, w, is_dma):
        need = {}
        e = self.E[eng]

        def add(tok, raw):
            if tok is None:
                return
            if tok[0] == "e" and tok[1] == eng and not is_dma:
                if eng == "pe":
                    return
            sem, val = self.resolve(tok)
            if e["seen"].get(sem.num, 0) >= val:
                return
            if need.get(sem.num, (None, 0))[1] < val:
                need[sem.num] = (sem, val)

        for res in r:
            add(res.w, True)
        for res in w:
            add(res.w, False)
            for tok in res.r.values():
                add(tok, False)
        return need

    def _emit(self, e, need, fn):
        items = list(need.values())
        for sem, val in items[:-1]:
            e["h"].wait_ge(sem, val)
            self.nwait += 1
        ins = fn(e["h"])
        if items:
            ins._wait_ge(*items[-1])
        for sem, val in items:
            e["seen"][sem.num] = val
        return ins

    def op(self, eng, fn, r=(), w=(), mark=False):
        e = self.E[eng]
        need = self._deps(eng, r, w, False)
        ins = self._emit(e, need, fn)
        idx = e["n"]
        e["n"] += 1
        e["last"] = ins
        if e["am"] or mark:
            ins.then_inc(e["sem"], 1)
            e["marks"].append(idx)
        tok = ("e", eng, idx)
        for res in r:
            res.r[eng] = tok
        for res in w:
            res.w = tok
            res.r = {}
        return ins

    def dma(self, q, out, in_, r=(), w=()):
        e = self.E[q]
        pool = self.dsems[q]
        i = self.drr[q]
        self.drr[q] = (i + 1) % len(pool)
        S = pool[i]
        need = self._deps(q, r, w, True)
        if S[1] > 0 and e["seen"].get(S[0].num, 0) < S[1]:
            if need.get(S[0].num, (None, 0))[1] < S[1]:
                need[S[0].num] = (S[0], S[1])
        ins = self._emit(e, need, lambda h: h.dma_start(out=out, in_=in_))
        ins.then_inc(S[0], 16)
        S[1] += 16
        tok = ("d", S[0], S[1])
        key = "dma_" + q + str(i)
        for res in r:
            res.r[key] = tok
        for res in w:
            res.w = tok
            res.r = {}
        return tok

    def barrier(self, engs=("pe", "act", "dve")):
        toks = {}
        for x in engs:
            if self.E[x]["n"] > 0:
                toks[x] = self.resolve(("e", x, self.E[x]["n"] - 1))
        for x in engs:
            e = self.E[x]
            for y, (sem, val) in toks.items():
                if y == x:
                    continue
                if e["seen"].get(sem.num, 0) < val:
                    e["h"].wait_ge(sem, val)
                    e["seen"][sem.num] = val

    def wait_all_dma(self, q):
        e = self.E[q]
        for S in self.dsems["sp"] + self.dsems["pool"]:
            if S[1] > 0:
                e["h"].wait_ge(S[0], S[1])


def build(NL, dbg=False, stop=99):
    nc = bass.Bass("TRN2", target_bir_lowering=False)
    tr = Tracker(nc)

    xT_d = nc.dram_tensor("xT", [D, T], F32, kind="ExternalInput").ap()
    cT_d = nc.dram_tensor("cT", [128, 8], F32, kind="ExternalInput").ap()
    par_d = nc.dram_tensor("par", [128, NL * NPAR], F32, kind="ExternalInput").ap()
    wf_d = nc.dram_tensor("wf", [128, NL * 64], F32, kind="ExternalInput").ap()
    wA_d = nc.dram_tensor("wA", [NL * 12, 128, 4096], F32, kind="ExternalInput").ap()
    wIN_d = nc.dram_tensor("wIN", [NL * 5, 128, 4096], F32, kind="ExternalInput").ap()
    wO_d = nc.dram_tensor("wO", [NL * 2, 128, 4096], F32, kind="ExternalInput").ap()
    wFI_d = nc.dram_tensor("wFI", [NL * 11, 128, 4096], F32, kind="ExternalInput").ap()
    wFO_d = nc.dram_tensor("wFO", [NL * 8, 128, DFF], F32, kind="ExternalInput").ap()
    out_d = nc.dram_tensor("outT", [D, T], F32, kind="ExternalOutput").ap()
    scr_d = nc.dram_tensor("scr", [8, 3 * T], BF16, kind="Internal").ap()
    scrb_d = nc.dram_tensor("scrb", [8, 512], F32, kind="Internal").ap()

    ARENA = 212736
    arena = nc.alloc_sbuf_tensor("arena", [128, ARENA], U8)
    base = nc.lookup_mloc(arena).addr
    cur = [base]

    def at(name, shape, dt, off=None):
        nb = int(np.prod(shape[1:])) * (4 if dt == F32 else 2)
        if off is None:
            off = cur[0]
            cur[0] += (nb + 31) // 32 * 32
            assert cur[0] <= base + ARENA, (name, cur[0] - base)
        return nc.alloc_sbuf_tensor_at(name, list(shape), dt, offset=off)

    xT = at("xTs", [128, 8, T], F32)
    h_off = cur[0]
    hT = at("hT", [128, 8, T], BF16)
    r3_off = cur[0]
    qT = at("qT", [128, 4, T], BF16)
    k_off = cur[0]
    kT = at("kT", [128, 4, T], BF16)
    va = at("va", [128, 16, NH, 65], BF16)
    UW = T + 32
    u_off = cur[0]
    uT = at("uT", [128, 4, UW], BF16)
    r3_end = cur[0]
    sq_off = cur[0]
    sq = at("sq", [128, 8, 512], BF16)
    ring = [at("ring%d" % i, [128, 4096], BF16) for i in range(3)]
    tmp_off = cur[0]
    tmp = [at("tmp%d" % i, [128, 512], F32) for i in range(4)]
    rstd_t = tmp[3]
    ident = at("ident", [128, 128], BF16)
    ones_b = at("ones_b", [128, 128], BF16)
    ones_f = at("ones_f", [128, 64], F32)
    maskb = at("maskb", [128, 128], BF16)
    par = at("par_s", [128, NL * NPAR], F32)
    wf = at("wf_s", [128, NL * 64], BF16)
    cTs = at("cTs", [128, 8], F32)
    cact = at("cact", [128, 8], BF16)
    modA = [at("mod%d" % i, [128, 48], F32) for i in range(2)]
    derA = [at("der%d" % i, [128, 32], F32) for i in range(2)]
    nbfA = [at("nbf%d" % i, [128, 1], F32) for i in range(2)]
    hid = at("hid", [128, NHC, 1024], BF16, off=r3_off)
    ys0 = at("ys0", [128, 8, 512], F32, off=r3_off + NHC * 1024 * 2)
    assert r3_off + NHC * 1024 * 2 + 8 * 512 * 4 <= r3_end
    ystage = at("ystage", [128, 8, 512], F32, off=k_off)
    hF32 = at("hF32", [128, 8, 1024], F32, off=h_off)
    diag = at("diag", [128, 4, CK, 128], BF16, off=h_off)
    ptb2 = [at("ptb2_%d" % i, [128, 2, 512], BF16, off=h_off + i * 2048) for i in range(2)]
    rden2 = [at("rden%d" % i, [128, 2, 512], F32, off=h_off + 4096 + i * 4096) for i in range(2)]
    bcs = [at("bcs%d" % i, [128, 2, 512], F32, off=h_off + 12288 + i * 4096) for i in range(2)]
    cumq = [at("cumq0", [128, T], BF16, off=h_off + 20480), at("cumq1", [128, T], BF16, off=sq_off)]
    cumk = [at("cumk0", [128, T], BF16, off=h_off + 24576), at("cumk1", [128, T], BF16, off=sq_off + 4096)]
    spl = at("spl", [8, 3 * T], BF16, off=u_off)
    fT = at("fT", [8, T], F32, off=sq_off)
    cumT = at("cumT", [8, T], F32, off=tmp_off)

    pp = [nc.alloc_psum_tensor("pp%d" % i, [128, 2, 512], F32) for i in range(4)]
    ps = [pp[i // 2][:, i % 2, :] for i in range(8)]
    bank = [Res() for _ in range(8)]

    xres = [[Res() for _ in range(4)] for _ in range(8)]
    hres = [[Res() for _ in range(4)] for _ in range(8)]
    qres = [[Res() for _ in range(4)] for _ in range(4)]
    kres = [[Res() for _ in range(4)] for _ in range(4)]
    vres = [Res() for _ in range(16)]
    ures = [[Res() for _ in range(4)] for _ in range(4)]
    sqres = [Res() for _ in range(8)]
    ringres = [Res() for _ in range(3)]
    tmpres = [Res() for _ in range(4)]
    rstdres = Res()
    constres = Res()
    parres = Res()
    modresA = [Res(), Res()]
    derresA = [Res(), Res()]
    ysres = [Res() for _ in range(8)]
    ptres = [Res() for _ in range(2)]
    rdres2 = [[Res(), Res()], [Res(), Res()]]
    bcres = [[Res(), Res()], [Res(), Res()]]
    scrbres = [Res() for _ in range(8)]
    cumres = [Res(), Res()]
    splres = Res()
    fres = Res()
    cumTres = Res()
    scrres = Res()
    hidres = [[Res() for _ in range(2)] for _ in range(NHC)]
    diagres = Res()
    recres = Res()
    miscres = Res()

    op = tr.op

    sched = []
    for t in range(12):
        sched.append((wA_d[t], 4096))
    for l in range(NL):
        for t in range(5):
            sched.append((wIN_d[l * 5 + t], 4096))
        for t in range(2):
            sched.append((wO_d[l * 2 + t], 4096))
        for half in range(2):
            for t in range(11):
                sched.append((wFI_d[l * 11 + t], 4096))
                if half == 0 and l + 1 < NL:
                    sched.append((wA_d[(l + 1) * 12 + t], 4096))
                    if t == 10:
                        sched.append((wA_d[(l + 1) * 12 + 11], 4096))
            for t in range(8):
                sched.append((wFO_d[l * 8 + t], DFF))
    wstate = dict(issued=0, used=0)

    def wissue(upto):
        while wstate["issued"] < min(upto, len(sched)):
            i = wstate["issued"]
            src, ncols = sched[i]
            s = i % 3
            tr.dma("pool", out=ring[s][:, 0:ncols], in_=src, w=[ringres[s]])
            wstate["issued"] += 1

    def wnext(ahead=3):
        n = wstate["used"]
        wstate["used"] += 1
        wissue(n + ahead)
        s = n % 3
        return ring[s], ringres[s]

    for kc in range(8):
        tr.dma("sp", out=xT[:, kc, :], in_=xT_d[kc * 128:(kc + 1) * 128, :], w=xres[kc])
    tr.dma("sp", out=par[:], in_=par_d[:, :], w=[parres])
    tr.dma("sp", out=cTs[:], in_=cT_d[:, :], w=[miscres])
    tr.dma("pool", out=wf[:], in_=wf_d[:, :], w=[constres])
    op("pool", lambda g: g.memset(ones_b[:], 1.0), w=[constres])
    op("pool", lambda g: g.memset(ones_f[:], 1.0), w=[constres])
    op("pool", lambda g: g.memset(tmp[0][:], 0.0), w=[tmpres[0]])
    op("pool", lambda g: g.memset(tmp[1][:], 1.0), w=[tmpres[1]])
    op("pool", lambda g: g.affine_select(out=maskb[:], in_=tmp[0][:, 0:128], pattern=[[1, 128]], compare_op=ALU.is_ge,
                                         fill=-30000.0, base=0, channel_multiplier=-1), r=[tmpres[0]], w=[constres])
    op("pool", lambda g: g.affine_select(out=ident[:], in_=tmp[1][:, 0:128], pattern=[[1, 128]], compare_op=ALU.is_equal,
                                         fill=0.0, base=0, channel_multiplier=-1), r=[tmpres[1]], w=[constres])
    op("pool", lambda g: g.memset(va[:, :, :, 64:65], 1.0), w=vres)
    op("pool", lambda g: g.memset(uT[:, :, 0:30], 0.0), w=[ures[fc][0] for fc in range(4)])
    op("act", lambda a: a.activation(out=cact[:], in_=cTs[:], func=AF.Silu), r=[miscres], w=[constres])
    wissue(3)
    tr.barrier(("pe", "act", "dve", "pool"))

    evac_rr = [0]

    def evac(out_ap, in_ap, r, w, scale=None):
        evac_rr[0] ^= 1
        if evac_rr[0]:
            if scale is None:
                op("act", lambda a: a.activation(out=out_ap, in_=in_ap, func=AF.Copy), r=r, w=w)
            else:
                op("act", lambda a: a.activation(out=out_ap, in_=in_ap, func=AF.Identity, scale=scale), r=r, w=w)
        else:
            if scale is None:
                op("dve", lambda v: v.tensor_copy(out=out_ap, in_=in_ap), r=r, w=w)
            else:
                op("dve", lambda v: v.tensor_scalar(out=out_ap, in0=in_ap, scalar1=scale, scalar2=None, op0=ALU.mult),
                   r=r, w=w)

    def rstd_from(bk, scale):
        op("act", lambda a: a.activation(out=rstd_t[:], in_=ps[bk][:], func=AF.Ln, bias=EPS, scale=scale),
           r=[bank[bk]], w=[rstdres])
        op("act", lambda a: a.activation(out=rstd_t[:], in_=rstd_t[:], func=AF.Exp, scale=-0.5),
           r=[rstdres], w=[rstdres])

    trr = [0]

    def gettmp():
        trr[0] = (trr[0] + 1) % 3
        return tmp[trr[0]], tmpres[trr[0]]

    def ada_tile(l, t):
        W, Wr = wnext()
        for jj in range(4):
            j = t * 4 + jj
            for kc in range(8):
                op("pe", lambda p: p.matmul(ps[7][:, j:j + 1], lhsT=W[:, kc * 512 + jj * 128: kc * 512 + jj * 128 + 128],
                                            rhs=cact[:, kc:kc + 1], start=(kc == 0), stop=(kc == 7)),
                   r=[Wr, constres], w=[bank[7]], mark=(kc == 7 and jj == 3))

    def ada_fin(l):
        pb = l * NPAR
        mod, der, nbf, modres, derres = modA[l % 2], derA[l % 2], nbfA[l % 2], modresA[l % 2], derresA[l % 2]
        op("dve", lambda v: v.tensor_tensor(out=mod[:], in0=ps[7][:, 0:48], in1=par[:, pb + P_ADAB: pb + P_ADAB + 48], op=ALU.add),
           r=[bank[7], parres], w=[modres])
        op("dve", lambda v: v.scalar_tensor_tensor(out=der[:, 0:8], in0=mod[:, 8:16], scalar=1.0, in1=par[:, pb + P_PRE1: pb + P_PRE1 + 8],
                                                   op0=ALU.add, op1=ALU.mult), r=[modres, parres], w=[derres])
        op("dve", lambda v: v.tensor_tensor(out=der[:, 8:16], in0=mod[:, 16:24], in1=par[:, pb + P_POST1: pb + P_POST1 + 8], op=ALU.mult),
           r=[modres, parres], w=[derres])
        op("dve", lambda v: v.scalar_tensor_tensor(out=der[:, 16:24], in0=mod[:, 32:40], scalar=1.0, in1=par[:, pb + P_PRE2: pb + P_PRE2 + 8],
                                                   op0=ALU.add, op1=ALU.mult), r=[modres, parres], w=[derres])
        op("dve", lambda v: v.tensor_tensor(out=der[:, 24:32], in0=mod[:, 40:48], in1=par[:, pb + P_POST2: pb + P_POST2 + 8], op=ALU.mult),
           r=[modres, parres], w=[derres])
        op("dve", lambda v: v.tensor_scalar(out=nbf[:], in0=par[:, pb + P_BF: pb + P_BF + 1], scalar1=-1.0, scalar2=None, op0=ALU.mult),
           r=[parres], w=[derres])

    def prenorm_a(tb):
        ts = slice(tb * 512, (tb + 1) * 512)
        for kc in range(8):
            op("act", lambda a: a.activation(out=sq[:, kc, :], in_=xT[:, kc, ts], func=AF.Square),
               r=[xres[kc][tb]], w=[sqres[kc]])

    def prenorm_b(l, tb, a_off, sh_off):
        mod, der, modres, derres = modA[l % 2], derA[l % 2], modresA[l % 2], derresA[l % 2]
        ts = slice(tb * 512, (tb + 1) * 512)
        for kc in range(8):
            op("pe", lambda p: p.matmul(ps[6][:], lhsT=ones_b[:], rhs=sq[:, kc, :], start=(kc == 0), stop=(kc == 7)),
               r=[sqres[kc], constres], w=[bank[6]], mark=(kc == 7))
        rstd_from(6, 1.0 / D)
        for kc in range(8):
            tt, ttr = gettmp()
            op("dve", lambda v: v.tensor_tensor(out=tt[:], in0=xT[:, kc, ts], in1=rstd_t[:], op=ALU.mult),
               r=[xres[kc][tb], rstdres], w=[ttr])
            op("act", lambda a: a.activation(out=hT[:, kc, ts], in_=tt[:], func=AF.Identity,
                                             bias=mod[:, sh_off + kc: sh_off + kc + 1], scale=der[:, a_off + kc: a_off + kc + 1]),
               r=[ttr, modres, derres], w=[hres[kc][tb]])

    def postnorm_residual(l, ys, ysr, tb, g_off, extra_r=()):
        der, derres = derA[l % 2], derresA[l % 2]
        ts = slice(tb * 512, (tb + 1) * 512)
        for kc in range(8):
            op("pe", lambda p: p.matmul(ps[6][:], lhsT=ones_b[:], rhs=sq[:, kc, :], start=(kc == 0), stop=(kc == 7)),
               r=[sqres[kc], constres], w=[bank[6]], mark=(kc == 7))
        rstd_from(6, 1.0 / D)
        for kc in range(8):
            tt, ttr = gettmp()
            op("dve", lambda v: v.scalar_tensor_tensor(out=tt[:], in0=ys[:, kc, :], scalar=der[:, g_off + kc: g_off + kc + 1],
                                                       in1=rstd_t[:], op0=ALU.mult, op1=ALU.mult),
               r=(ysr[kc] if isinstance(ysr[kc], list) else [ysr[kc]]) + [derres, rstdres] + list(extra_r), w=[ttr])
            op("dve", lambda v: v.tensor_tensor(out=xT[:, kc, ts], in0=xT[:, kc, ts], in1=tt[:], op=ALU.add),
               r=[ttr, xres[kc][tb]], w=[xres[kc][tb]])

    brr = [0]

    def nextbank(n=6):
        brr[0] = (brr[0] + 1) % n
        return brr[0]

    def proj_fm(W, Wr, col0, tb, bk):
        ts = slice(tb * 512, (tb + 1) * 512)
        for kc in range(8):
            op("pe", lambda p: p.matmul(ps[bk][:], lhsT=W[:, kc * 512 + col0: kc * 512 + col0 + 128], rhs=hT[:, kc, ts],
                                        start=(kc == 0), stop=(kc == 7)),
               r=[Wr, hres[kc][tb]], w=[bank[bk]], mark=(kc == 7))

    def inproj(l):
        op("dve", lambda v: v.memset(va[:, :, :, 64:65], 1.0), w=vres)
        for tb in range(4):
            ts = slice(tb * 512, (tb + 1) * 512)
            for kc in range(8):
                op("pe", lambda p: p.matmul(ps[6][0:8, :], lhsT=wf[:, l * 64 + kc * 8: l * 64 + kc * 8 + 8], rhs=hT[:, kc, ts],
                                            start=(kc == 0), stop=(kc == 7)),
                   r=[constres, hres[kc][tb]], w=[bank[6]], mark=(kc == 7))
            op("act", lambda a: a.activation(out=fT[0:8, ts], in_=ps[6][0:8, :], func=AF.Exp, bias=nbfA[l % 2][0:8, 0:1], scale=-1.0),
               r=[bank[6], derresA[l % 2]], w=[fres] + sqres)
        op("act", lambda a: a.activation(out=fT[:], in_=fT[:], func=AF.Ln, bias=1.0, scale=1.0), r=[fres], w=[fres])

        for which in range(2):
            W, Wr = wnext()
            dst, dres = (qT, qres) if which == 0 else (kT, kres)
            for fc in range(4):
                for tb in range(4):
                    bk = nextbank()
                    proj_fm(W, Wr, fc * 128, tb, bk)
                    evac(dst[:, fc, tb * 512:(tb + 1) * 512], ps[bk][:], [bank[bk]], [dres[fc][tb]],
                         scale=(0.125 if which == 0 else None))
            if which == 0:
                fchain()
                op("dve", lambda v: v.memset(uT[:, :, 0:30], 0.0), w=[ures[fc][0] for fc in range(4)] + [splres])
        W, Wr = wnext()
        for tk in range(16):
            bk = nextbank()
            for kc in range(8):
                op("pe", lambda p: p.matmul(ps[bk][:], lhsT=hT[:, kc, tk * 128:(tk + 1) * 128], rhs=W[:, kc * 512:(kc + 1) * 512],
                                            start=(kc == 0), stop=(kc == 7)),
                   r=[Wr, hres[kc][tk // 4]], w=[bank[bk]], mark=(kc == 7))
            evac(va[:, tk, :, 0:64], ps[bk][:].rearrange("p (h d) -> p h d", h=NH), [bank[bk]], [vres[tk]])
        for half in range(2):
            W, Wr = wnext()
            for c2 in range(2):
                fc = half * 2 + c2
                for tb in range(4):
                    b0 = nextbank()
                    proj_fm(W, Wr, c2 * 128, tb, b0)
                    b1 = nextbank()
                    proj_fm(W, Wr, 256 + c2 * 128, tb, b1)
                    tt, ttr = gettmp()
                    op("act", lambda a: a.activation(out=tt[:], in_=ps[b1][:], func=AF.Sigmoid), r=[bank[b1]], w=[ttr])
                    op("dve", lambda v: v.tensor_tensor(out=uT[:, fc, 30 + tb * 512: 30 + (tb + 1) * 512], in0=ps[b0][:], in1=tt[:],
                                                        op=ALU.mult), r=[bank[b0], ttr], w=[ures[fc][tb]])
    def fchain():
        op("dve", lambda v: v.tensor_tensor_scan(out=cumT[:], data0=fT[:], data1=fT[:], initial=0.0, op0=ALU.add, op1=ALU.max),
           r=[fres] + tmpres, w=[cumTres] + tmpres)
        op("dve", lambda v: v.tensor_copy(out=spl[:, 0:T], in_=cumT[:]), r=[cumTres], w=[splres])
        op("dve", lambda v: v.tensor_tensor(out=fT[:], in0=cumT[:], in1=spl[:, 0:T], op=ALU.subtract), r=[cumTres, splres], w=[fres])
        op("dve", lambda v: v.tensor_copy(out=spl[:, T:2 * T], in_=fT[:]), r=[fres], w=[splres])
        op("dve", lambda v: v.tensor_tensor(out=cumT[:], in0=fT[:], in1=spl[:, T:2 * T], op=ALU.subtract), r=[fres, splres], w=[cumTres])
        op("dve", lambda v: v.tensor_copy(out=spl[:, 2 * T:3 * T], in_=cumT[:]), r=[cumTres], w=[splres])
        tr.dma("sp", out=scr_d[:, :], in_=spl[:], r=[splres], w=[scrres])

    def attention():
        for b in range(2):
            extra = [] if b == 0 else ([fres] + sqres)
            op("dve", lambda v: v.memset(cumq[b][:], 1.0), w=[cumres[b]] + extra)
            op("dve", lambda v: v.memset(cumk[b][:], 0.0), w=[cumres[b]] + extra)
            op("dve", lambda v: v.memset(cumk[b][0:3, :], -1.0), w=[cumres[b]])
            op("dve", lambda v: v.memset(cumk[b][64:67, :], -1.0), w=[cumres[b]])
        cnt = 0
        ocnt = 0
        pending = [None]
        for c in range(4):
            fc = c
            b = c % 2
            for hx in range(2):
                h = 2 * c + hx
                src = scr_d[h:h + 1, :].rearrange("o (r t) -> (o r) t", r=3)
                tr.dma("sp", out=cumq[b][64 * hx:64 * hx + 3, :], in_=src, r=[scrres], w=[cumres[b]])
                tr.dma("sp", out=cumk[b][64 * hx + 3:64 * hx + 6, :], in_=src, r=[scrres], w=[cumres[b]])
            for i in range(4):
                nj = 4 * i + 4
                pbuf = ocnt % 2
                ob0 = 4 + 2 * pbuf
                ocnt += 1

                def s_tile(j):
                    nonlocal cnt
                    t = cnt % 2
                    cnt += 1
                    c0 = max(0, j - 4 * i) * 128
                    qs = slice(i * 512 + c0, (i + 1) * 512)
                    ks = slice(j * 128, (j + 1) * 128)
                    diagonal = j >= 4 * i
                    bks = [bank[2 * t], bank[2 * t + 1]]
                    for hx in range(2):
                        pr = 64 * hx
                        op("pe", lambda p: p.matmul(ps[2 * t + hx][:, c0:512], lhsT=kT[pr:pr + 64, fc, ks], rhs=qT[pr:pr + 64, fc, qs],
                                                    start=True, stop=False),
                           r=[kres[fc][j // 4], qres[fc][i]], w=[bks[hx]])
                    if diagonal:
                        for hx in range(2):
                            op("pe", lambda p: p.matmul(ps[2 * t + hx][:, c0:c0 + 128], lhsT=ident[:], rhs=maskb[:], start=False, stop=False),
                               r=[constres], w=[bks[hx]])
                    for hx in range(2):
                        pr = 64 * hx
                        op("pe", lambda p: p.matmul(ps[2 * t + hx][:, c0:512], lhsT=cumk[b][pr:pr + 64, ks], rhs=cumq[b][pr:pr + 64, qs],
                                                    start=False, stop=True),
                           r=[cumres[b]], w=[bks[hx]], mark=(hx == 1))
                    op("act", lambda a: a.activation(out=ptb2[t][:, :, c0:512], in_=pp[t][:, :, c0:512], func=AF.Exp),
                       r=bks, w=[ptres[t]])
                    return (j, c0, t)

                def pv_tile(tl):
                    j, c0, t = tl
                    for hx in range(2):
                        hh = 2 * c + hx
                        mw = 128 if hh < 7 else 65
                        op("pe", lambda p: p.matmul(ps[ob0 + hx][0:mw, c0:512], lhsT=va[:, j, :, :].rearrange("p h d -> p (h d)")[:, hh * 65:hh * 65 + mw],
                                                    rhs=ptb2[t][:, hx, c0:512],
                                                    start=(j == 0), stop=(j == nj - 1)),
                           r=[ptres[t], vres[j]], w=[bank[ob0 + hx]], mark=(j == nj - 1 and hx == 1))

                prev = s_tile(0)
                if pending[0] is not None:
                    pending[0]()
                    pending[0] = None
                for j in range(1, nj):
                    curt = s_tile(j)
                    pv_tile(prev)
                    prev = curt
                pv_tile(prev)
                for hx in range(2):
                    ob = ob0 + hx
                    k = (2 * ocnt + hx) % 8
                    rd, rdr = rden2[pbuf], rdres2[pbuf][hx]
                    op("act", lambda a: a.activation(out=rd[64:65, hx, :], in_=ps[ob][64:65, :], func=AF.Ln), r=[bank[ob]], w=[rdr])
                    op("act", lambda a: a.activation(out=rd[64:65, hx, :], in_=rd[64:65, hx, :], func=AF.Exp, scale=-1.0), r=[rdr], w=[rdr])
                    tr.dma("sp", out=scrb_d[k:k + 1, :], in_=rd[64:65, hx, :], r=[rdr], w=[scrbres[k]])
                    tr.dma("sp", out=bcs[pbuf][0:64, hx, :], in_=scrb_d[k:k + 1, :].broadcast_to([64, 512]), r=[scrbres[k]], w=[bcres[pbuf][hx]])

                def fin(fc=fc, i=i, pbuf=pbuf, ob0=ob0):
                    for hx in range(2):
                        pr = 64 * hx
                        op("dve", lambda v: v.tensor_tensor(out=qT[pr:pr + 64, fc, i * 512:(i + 1) * 512], in0=ps[ob0 + hx][0:64, :],
                                                            in1=bcs[pbuf][0:64, hx, :], op=ALU.mult),
                           r=[bank[ob0 + hx], bcres[pbuf][hx]], w=[qres[fc][i]])

                pending[0] = fin
        pending[0]()

    def conv(l):
        pb = l * NPAR
        for fc in range(4):
            op("dve", lambda v: v.tensor_tensor(out=diag[:, fc, :, :],
                                                in0=ident[:].unsqueeze(1).broadcast_to([128, CK, 128]),
                                                in1=par[:, pb + P_CONVW + fc * CK: pb + P_CONVW + (fc + 1) * CK].unsqueeze(2).broadcast_to([128, CK, 128]),
                                                op=ALU.mult), r=[constres, parres], w=[diagres])
        for tb in (3, 2, 1, 0):
            for fc in range(4):
                rr = [ures[fc][tb], diagres] + ([ures[fc][tb - 1]] if tb > 0 else [])
                for k in range(CK):
                    op("pe", lambda p: p.matmul(ps[fc][:], lhsT=diag[:, fc, k, :], rhs=uT[:, fc, tb * 512 + k: tb * 512 + k + 512],
                                                start=(k == 0), stop=(k == CK - 1)), r=rr, w=[bank[fc]], mark=(k == CK - 1))
            for fc in range(4):
                op("act", lambda a: a.activation(out=ystage[:, fc, :], in_=ps[fc][:], func=AF.Identity,
                                                 bias=par[:, pb + P_CONVB + fc: pb + P_CONVB + fc + 1], scale=1.0),
                   r=[bank[fc], parres], w=[ysres[fc]])
                op("dve", lambda v: v.tensor_copy(out=sq[:, fc, :], in_=ystage[:, fc, :]), r=[ysres[fc]], w=[sqres[fc]])
                op("act", lambda a: a.activation(out=sq[:, 4 + fc, :], in_=ystage[:, fc, :], func=AF.Square),
                   r=[ysres[fc]], w=[sqres[4 + fc]])
            for fc in range(4):
                op("pe", lambda p: p.matmul(ps[4][:], lhsT=ones_b[:], rhs=sq[:, fc, :], start=(fc == 0), stop=(fc == 3)),
                   r=[sqres[fc], constres], w=[bank[4]], mark=(fc == 3))
            for fc in range(4):
                op("pe", lambda p: p.matmul(ps[5][:], lhsT=ones_b[:], rhs=sq[:, 4 + fc, :], start=(fc == 0), stop=(fc == 3)),
                   r=[sqres[4 + fc], constres], w=[bank[5]], mark=(fc == 3))
            mean, meanr = tmp[0], tmpres[0]
            var, varr = tmp[1], tmpres[1]
            op("dve", lambda v: v.tensor_scalar(out=mean[:], in0=ps[4][:], scalar1=1.0 / 512, scalar2=None, op0=ALU.mult),
               r=[bank[4]], w=[meanr])
            op("dve", lambda v: v.tensor_tensor(out=var[:], in0=mean[:], in1=mean[:], op=ALU.mult), r=[meanr], w=[varr])
            op("dve", lambda v: v.scalar_tensor_tensor(out=var[:], in0=ps[5][:], scalar=1.0 / 512, in1=var[:], op0=ALU.mult, op1=ALU.subtract),
               r=[bank[5], varr], w=[varr])
            op("act", lambda a: a.activation(out=rstd_t[:], in_=var[:], func=AF.Ln, bias=EPS, scale=1.0), r=[varr], w=[rstdres])
            op("act", lambda a: a.activation(out=rstd_t[:], in_=rstd_t[:], func=AF.Exp, scale=-0.5), r=[rstdres], w=[rstdres])
            for fc in range(4):
                op("dve", lambda v: v.tensor_tensor(out=ystage[:, fc, :], in0=ystage[:, fc, :], in1=mean[:], op=ALU.subtract),
                   r=[ysres[fc], meanr], w=[ysres[fc]])
                op("dve", lambda v: v.tensor_tensor(out=ystage[:, fc, :], in0=ystage[:, fc, :], in1=rstd_t[:], op=ALU.mult),
                   r=[ysres[fc], rstdres], w=[ysres[fc]])
                op("act", lambda a: a.activation(out=uT[:, fc, 30 + tb * 512: 30 + (tb + 1) * 512], in_=ystage[:, fc, :], func=AF.Silu,
                                                 bias=par[:, pb + P_LNB + fc: pb + P_LNB + fc + 1],
                                                 scale=par[:, pb + P_LNG + fc: pb + P_LNG + fc + 1]),
                   r=[ysres[fc], parres], w=[ures[fc][tb]])

    def wo(l):
        W0, W0r = wnext()
        W1, W1r = wnext(ahead=2)

        def mm(tb, oc):
            ts = slice(tb * 512, (tb + 1) * 512)
            us = slice(30 + tb * 512, 30 + (tb + 1) * 512)
            W, Wr = (W0, W0r) if oc < 4 else (W1, W1r)
            c0 = (oc % 4) * 128
            bk = nextbank(4)
            for kc in range(8):
                if kc < 4:
                    rhs, rr = qT[:, kc, ts], qres[kc][tb]
                else:
                    rhs, rr = uT[:, kc - 4, us], ures[kc - 4][tb]
                op("pe", lambda p: p.matmul(ps[bk][:], lhsT=W[:, kc * 512 + c0: kc * 512 + c0 + 128], rhs=rhs,
                                            start=(kc == 0), stop=(kc == 7)), r=[Wr, rr], w=[bank[bk]], mark=(kc == 7))
            return bk

        def ev(oc, bk):
            op("dve", lambda v: v.tensor_copy(out=ystage[:, oc, :], in_=ps[bk][:]), r=[bank[bk]], w=[ysres[oc]])
            op("act", lambda a: a.activation(out=sq[:, oc, :], in_=ystage[:, oc, :], func=AF.Square), r=[ysres[oc]], w=[sqres[oc]])

        for oc in range(8):
            ev(oc, mm(0, oc))
        for tb in range(4):
            ahead = [mm(tb + 1, oc) for oc in range(4)] if tb < 3 else []
            postnorm_residual(l, ystage, ysres, tb, 8)
            prenorm_a(tb)
            prenorm_b(l, tb, 16, 24)
            if tb < 3:
                for oc in range(4):
                    ev(oc, ahead[oc])
                for oc in range(4, 8):
                    ev(oc, mm(tb + 1, oc))

    def ffn(l):
        nxt = l + 1 < NL
        for half in range(2):
            fcnt = 0
            for s in range(11):
                W, Wr = wnext()
                for c2 in range(2):
                    hc = 2 * s + c2
                    for tbl in range(2):
                        tb = half * 2 + tbl
                        bg = fcnt % 2
                        bu = 2 + fcnt % 2
                        fcnt += 1
                        proj_fm(W, Wr, c2 * 128, tb, bg)
                        proj_fm(W, Wr, 256 + c2 * 128, tb, bu)
                        tt, ttr = gettmp()
                        op("act", lambda a: a.activation(out=tt[:], in_=ps[bg][:], func=AF.Silu), r=[bank[bg]], w=[ttr])
                        op("dve", lambda v: v.tensor_tensor(out=hid[:, hc, tbl * 512:(tbl + 1) * 512], in0=ps[bu][:], in1=tt[:], op=ALU.mult),
                           r=[bank[bu], ttr], w=[hidres[hc][tbl]])
                if half == 0 and nxt:
                    ada_tile(l + 1, s)
                    if s == 10:
                        ada_tile(l + 1, 11)
                        ada_fin(l + 1)
            ys1 = hF32[:, :, half * 512:(half + 1) * 512]
            ys1res = [[hres[oc][half * 2], hres[oc][half * 2 + 1]] for oc in range(8)]
            for oc in range(8):
                W, Wr = wnext()
                pair = ((0, 1), (2, 3), (4, 5))[oc % 3]
                for tbl in range(2):
                    bk = pair[tbl]
                    for hc in range(NHC):
                        op("pe", lambda p: p.matmul(ps[bk][:], lhsT=W[:, hc * 128:(hc + 1) * 128], rhs=hid[:, hc, tbl * 512:(tbl + 1) * 512],
                                                    start=(hc == 0), stop=(hc == NHC - 1)),
                           r=[Wr, hidres[hc][tbl]], w=[bank[bk]], mark=(hc == NHC - 1))
                evac(ys0[:, oc, :], ps[pair[0]][:], [bank[pair[0]]], [ysres[oc]])
                evac(ys1[:, oc, :], ps[pair[1]][:], [bank[pair[1]]], ys1res[oc])
                if half == 1 and nxt:
                    if oc == 0:
                        prenorm_a(0)
                    elif oc == 2:
                        prenorm_b(l + 1, 0, 0, 0)
                    elif oc == 4:
                        prenorm_a(1)
                    elif oc == 6:
                        prenorm_b(l + 1, 1, 0, 0)
            for kc in range(8):
                op("act", lambda a: a.activation(out=sq[:, kc, :], in_=ys0[:, kc, :], func=AF.Square), r=[ysres[kc]], w=[sqres[kc]])
            postnorm_residual(l, ys0, ysres, half * 2, 24)
            for kc in range(8):
                op("act", lambda a: a.activation(out=sq[:, kc, :], in_=ys1[:, kc, :], func=AF.Square), r=ys1res[kc], w=[sqres[kc]])
            postnorm_residual(l, ys1, ys1res, half * 2 + 1, 24)
        if nxt:
            for tb in (2, 3):
                prenorm_a(tb)
                prenorm_b(l + 1, tb, 0, 0)

    for t in range(12):
        ada_tile(0, t)
    ada_fin(0)
    for tb in range(4):
        prenorm_a(tb)
        prenorm_b(0, tb, 0, 0)
    for l in range(NL):
        tr.barrier()
        inproj(l)
        tr.barrier()
        attention()
        tr.barrier()
        conv(l)
        tr.barrier()
        wo(l)
        tr.barrier()
        ffn(l)
    tr.barrier()
    for kc in range(8):
        tr.dma("sp", out=out_d[kc * 128:(kc + 1) * 128, :], in_=xT[:, kc, :], r=xres[kc])
    tr.wait_all_dma("sp")
    return nc


def _kc_tile(w2d, cols):
    sub = w2d[:, cols]
    n = sub.shape[1]
    return np.ascontiguousarray(sub.reshape(8, 128, n).transpose(1, 0, 2).reshape(128, 8 * n))


def prep_weights(NL, w_in, b_f, conv_w, conv_b, conv_ln_g, conv_ln_b, w_o, w_ffn_in, w_ffn_out,
                 mix_pre_g, mix_post_g, ffn_pre_g, ffn_post_g, ada_w, ada_b):
    f32 = np.float32
    wA = np.empty((NL * 12, 128, 4096), f32)
    wIN = np.empty((NL * 5, 128, 4096), f32)
    wO = np.empty((NL * 2, 128, 4096), f32)
    wFI = np.empty((NL * 11, 128, 4096), f32)
    wFO = np.empty((NL * 8, 128, DFF), f32)
    par = np.zeros((128, NL * NPAR), f32)
    wf = np.empty((128, NL * 64), f32)
    ar = np.arange
    for l in range(NL):
        for t in range(12):
            wA[l * 12 + t] = _kc_tile(ada_w[l], ar(t * 512, (t + 1) * 512))
        wi = w_in[l]
        wIN[l * 5 + 0] = _kc_tile(wi, ar(0, 512))
        wIN[l * 5 + 1] = _kc_tile(wi, ar(512, 1024))
        wIN[l * 5 + 2] = _kc_tile(wi, ar(1024, 1536))
        for half in range(2):
            cols = np.concatenate([ar(1544 + half * 256, 1544 + half * 256 + 256), ar(2056 + half * 256, 2056 + half * 256 + 256)])
            wIN[l * 5 + 3 + half] = _kc_tile(wi, cols)
        wf[:, l * 64:(l + 1) * 64] = _kc_tile(wi, ar(1536, 1544))
        for t in range(2):
            wO[l * 2 + t] = _kc_tile(w_o[l], ar(t * 512, (t + 1) * 512))
        for s in range(11):
            cols = np.concatenate([ar(s * 256, s * 256 + 256), ar(DFF + s * 256, DFF + s * 256 + 256)])
            wFI[l * 11 + s] = _kc_tile(w_ffn_in[l], cols)
        wo3 = w_ffn_out[l].reshape(NHC, 128, 8, 128)
        wFO[l * 8:(l + 1) * 8] = wo3.transpose(2, 1, 0, 3).reshape(8, 128, DFF)
        pb = l * NPAR

        def fm(v):
            return v.reshape(8, 128).T

        par[:, pb + P_PRE1: pb + P_PRE1 + 8] = fm(mix_pre_g[l])
        par[:, pb + P_POST1: pb + P_POST1 + 8] = fm(mix_post_g[l])
        par[:, pb + P_PRE2: pb + P_PRE2 + 8] = fm(ffn_pre_g[l])
        par[:, pb + P_POST2: pb + P_POST2 + 8] = fm(ffn_post_g[l])
        par[:, pb + P_ADAB: pb + P_ADAB + 48] = ada_b[l].reshape(48, 128).T
        par[:, pb + P_CONVW: pb + P_CONVW + 4 * CK] = conv_w[l].reshape(CK, 4, 128).transpose(2, 1, 0).reshape(128, 4 * CK)
        par[:, pb + P_CONVB: pb + P_CONVB + 4] = conv_b[l].reshape(4, 128).T
        par[:, pb + P_LNG: pb + P_LNG + 4] = conv_ln_g[l].reshape(4, 128).T
        par[:, pb + P_LNB: pb + P_LNB + 4] = conv_ln_b[l].reshape(4, 128).T
        par[0:8, pb + P_BF] = b_f[l]
    return dict(wA=wA, wIN=wIN, wO=wO, wFI=wFI, wFO=wFO, par=par, wf=wf)


_NC_CACHE = {}


def run_layers(x, c, NL, weights, cores=None):
    B = x.shape[0]
    if cores is None:
        cores = list(range(B))
    if NL not in _NC_CACHE:
        _NC_CACHE[NL] = build(NL)
    nc = _NC_CACHE[NL]
    in_maps = []
    for b in cores:
        m = dict(weights)
        m["xT"] = np.ascontiguousarray(x[b].T)
        m["cT"] = np.ascontiguousarray(c[b].reshape(8, 128).T)
        in_maps.append(m)
    res = run_bass_kernel_spmd(nc, in_maps, core_ids=list(range(len(cores))))
    out = np.empty((len(cores), T, D), np.float32)
    for i in range(len(cores)):
        out[i] = res.results[i]["outT"].T
    return out


def kernel(x, c, w_in, b_f, conv_w, conv_b, conv_ln_g, conv_ln_b, w_o, w_ffn_in, w_ffn_out,
           mix_pre_g, mix_post_g, ffn_pre_g, ffn_post_g, ada_w, ada_b):
    a = [np.asarray(v, dtype=np.float32) for v in (w_in, b_f, conv_w, conv_b, conv_ln_g, conv_ln_b, w_o, w_ffn_in, w_ffn_out,
                                                   mix_pre_g, mix_post_g, ffn_pre_g, ffn_post_g, ada_w, ada_b)]
    weights = prep_weights(L_FULL, *a)
    x = np.asarray(x, dtype=np.float32)
    c = np.asarray(c, dtype=np.float32)
    return run_layers(x, c, L_FULL, weights)
```

```python
import bisect
import os
import numpy as np
import concourse.bass as bass
import concourse.mybir as mybir
from concourse.bass_utils import run_bass_kernel_spmd

F32 = mybir.dt.float32
BF16 = mybir.dt.bfloat16
U8 = mybir.dt.uint8
AF = mybir.ActivationFunctionType
ALU = mybir.AluOpType

D = 1024
T = 2048
L_FULL = 4
NH = 8
DFF = 2816
NHC = DFF // 128
CK = 31
EPS = 1e-6
NPAR = 224
P_PRE1, P_POST1, P_PRE2, P_POST2, P_ADAB, P_CONVW, P_CONVB, P_LNG, P_LNB, P_BF = 0, 8, 16, 24, 32, 80, 204, 208, 212, 216


class Res:
    __slots__ = ("w", "r")

    def __init__(self):
        self.w = None
        self.r = {}


class Tracker:
    def __init__(self, nc):
        self.nc = nc
        self.E = {}
        for name, h, am in (("pe", nc.tensor, False), ("act", nc.scalar, True), ("dve", nc.vector, True),
                            ("pool", nc.gpsimd, True), ("sp", nc.sync, True)):
            self.E[name] = dict(h=h, sem=nc.alloc_semaphore("s_" + name), am=am, n=0, marks=[], last=None, seen={})
        self.dsems = {"sp": [[nc.alloc_semaphore("dsp%d" % i), 0] for i in range(10)],
                      "pool": [[nc.alloc_semaphore("dpl%d" % i), 0] for i in range(6)]}
        self.drr = {"sp": 0, "pool": 0}
        self.nwait = 0

    def resolve(self, tok):
        if tok[0] == "d":
            return tok[1], tok[2]
        e = self.E[tok[1]]
        idx = tok[2]
        if e["am"]:
            return e["sem"], idx + 1
        marks = e["marks"]
        k = bisect.bisect_left(marks, idx)
        if k == len(marks):
            e["last"].then_inc(e["sem"], 1)
            marks.append(e["n"] - 1)
        return e["sem"], k + 1

    def _deps(self, eng, r, w, is_dma):
        need = {}
        e = self.E[eng]

        def add(tok, raw):
            if tok is None:
                return
            if tok[0] == "e" and tok[1] == eng and not is_dma:
                if eng == "pe":
                    return
            sem, val = self.resolve(tok)
            if e["seen"].get(sem.num, 0) >= val:
                return
            if need.get(sem.num, (None, 0))[1] < val:
                need[sem.num] = (sem, val)

        for res in r:
            add(res.w, True)
        for res in w:
            add(res.w, False)
            for tok in res.r.values():
                add(tok, False)
        return need

    def _emit(self, e, need, fn):
        items = list(need.values())
        for sem, val in items[:-1]:
            e["h"].wait_ge(sem, val)
            self.nwait += 1
        ins = fn(e["h"])
        if items:
            ins._wait_ge(*items[-1])
        for sem, val in items:
            e["seen"][sem.num] = val
        return ins

    def op(self, eng, fn, r=(), w=(), mark=False):
        e = self.E[eng]
        need = self._deps(eng, r, w, False)
        ins = self._emit(e, need, fn)
        idx = e["n"]
        e["n"] += 1
        e["last"] = ins
        if e["am"] or mark:
            ins.then_inc(e["sem"], 1)
            e["marks"].append(idx)
        tok = ("e", eng, idx)
        for res in r:
            res.r[eng] = tok
        for res in w:
            res.w = tok
            res.r = {}
        return ins

    def dma(self, q, out, in_, r=(), w=()):
        e = self.E[q]
        pool = self.dsems[q]
        i = self.drr[q]
        self.drr[q] = (i + 1) % len(pool)
        S = pool[i]
        need = self._deps(q, r, w, True)
        if S[1] > 0 and e["seen"].get(S[0].num, 0) < S[1]:
            if need.get(S[0].num, (None, 0))[1] < S[1]:
                need[S[0].num] = (S[0], S[1])
        ins = self._emit(e, need, lambda h: h.dma_start(out=out, in_=in_))
        ins.then_inc(S[0], 16)
        S[1] += 16
        tok = ("d", S[0], S[1])
        key = "dma_" + q + str(i)
        for res in r:
            res.r[key] = tok
        for res in w:
            res.w = tok
            res.r = {}
        return tok

    def barrier(self, engs=("pe", "act", "dve")):
        toks = {}
        for x in engs:
            if self.E[x]["n"] > 0:
                toks[x] = self.resolve(("e", x, self.E[x]["n"] - 1))
        for x in engs:
            e = self.E[x]
            for y, (sem, val) in toks.items():
                if y == x:
                    continue
                if e["seen"].get(sem.num, 0) < val:
                    e["h"].wait_ge(sem, val)
                    e["seen"][sem.num] = val

    def wait_all_dma(self, q):
        e = self.E[q]
        for S in self.dsems["sp"] + self.dsems["pool"]:
            if S[1] > 0:
                e["h"].wait_ge(S[0], S[1])


def build(NL, dbg=False, stop=99):
    nc = bass.Bass("TRN2", target_bir_lowering=False)
    tr = Tracker(nc)

    xT_d = nc.dram_tensor("xT", [D, T], F32, kind="ExternalInput").ap()
    cT_d = nc.dram_tensor("cT", [128, 8], F32, kind="ExternalInput").ap()
    par_d = nc.dram_tensor("par", [128, NL * NPAR], F32, kind="ExternalInput").ap()
    wf_d = nc.dram_tensor("wf", [128, NL * 64], F32, kind="ExternalInput").ap()
    wA_d = nc.dram_tensor("wA", [NL * 12, 128, 4096], F32, kind="ExternalInput").ap()
    wIN_d = nc.dram_tensor("wIN", [NL * 5, 128, 4096], F32, kind="ExternalInput").ap()
    wO_d = nc.dram_tensor("wO", [NL * 2, 128, 4096], F32, kind="ExternalInput").ap()
    wFI_d = nc.dram_tensor("wFI", [NL * 11, 128, 4096], F32, kind="ExternalInput").ap()
    wFO_d = nc.dram_tensor("wFO", [NL * 8, 128, DFF], F32, kind="ExternalInput").ap()
    out_d = nc.dram_tensor("outT", [D, T], F32, kind="ExternalOutput").ap()
    scr_d = nc.dram_tensor("scr", [8, 3 * T], BF16, kind="Internal").ap()
    scrb_d = nc.dram_tensor("scrb", [8, 512], F32, kind="Internal").ap()

    ARENA = 212736
    arena = nc.alloc_sbuf_tensor("arena", [128, ARENA], U8)
    base = nc.lookup_mloc(arena).addr
    cur = [base]

    def at(name, shape, dt, off=None):
        nb = int(np.prod(shape[1:])) * (4 if dt == F32 else 2)
        if off is None:
            off = cur[0]
            cur[0] += (nb + 31) // 32 * 32
            assert cur[0] <= base + ARENA, (name, cur[0] - base)
        return nc.alloc_sbuf_tensor_at(name, list(shape), dt, offset=off)

    xT = at("xTs", [128, 8, T], F32)
    h_off = cur[0]
    hT = at("hT", [128, 8, T], BF16)
    r3_off = cur[0]
    qT = at("qT", [128, 4, T], BF16)
    k_off = cur[0]
    kT = at("kT", [128, 4, T], BF16)
    va = at("va", [128, 16, NH, 65], BF16)
    UW = T + 32
    u_off = cur[0]
    uT = at("uT", [128, 4, UW], BF16)
    r3_end = cur[0]
    sq_off = cur[0]
    sq = at("sq", [128, 8, 512], BF16)
    ring = [at("ring%d" % i, [128, 4096], BF16) for i in range(3)]
    tmp_off = cur[0]
    tmp = [at("tmp%d" % i, [128, 512], F32) for i in range(4)]
    rstd_t = tmp[3]
    ident = at("ident", [128, 128], BF16)
    ones_b = at("ones_b", [128, 128], BF16)
    ones_f = at("ones_f", [128, 64], F32)
    maskb = at("maskb", [128, 128], BF16)
    par = at("par_s", [128, NL * NPAR], F32)
    wf = at("wf_s", [128, NL * 64], BF16)
    cTs = at("cTs", [128, 8], F32)
    cact = at("cact", [128, 8], BF16)
    modA = [at("mod%d" % i, [128, 48], F32) for i in range(2)]
    derA = [at("der%d" % i, [128, 32], F32) for i in range(2)]
    nbfA = [at("nbf%d" % i, [128, 1], F32) for i in range(2)]
    hid = at("hid", [128, NHC, 1024], BF16, off=r3_off)
    ys0 = at("ys0", [128, 8, 512], F32, off=r3_off + NHC * 1024 * 2)
    assert r3_off + NHC * 1024 * 2 + 8 * 512 * 4 <= r3_end
    ystage = at("ystage", [128, 8, 512], F32, off=k_off)
    hF32 = at("hF32", [128, 8, 1024], F32, off=h_off)
    diag = at("diag", [128, 4, CK, 128], BF16, off=h_off)
    ptb2 = [at("ptb2_%d" % i, [128, 2, 512], BF16, off=h_off + i * 2048) for i in range(2)]
    rden2 = [at("rden%d" % i, [128, 2, 512], F32, off=h_off + 4096 + i * 4096) for i in range(2)]
    bcs = [at("bcs%d" % i, [128, 2, 512], F32, off=h_off + 12288 + i * 4096) for i in range(2)]
    cumq = [at("cumq0", [128, T], BF16, off=h_off + 20480), at("cumq1", [128, T], BF16, off=sq_off)]
    cumk = [at("cumk0", [128, T], BF16, off=h_off + 24576), at("cumk1", [128, T], BF16, off=sq_off + 4096)]
    spl = at("spl", [8, 3 * T], BF16, off=u_off)
    fT = at("fT", [8, T], F32, off=sq_off)
    cumT = at("cumT", [8, T], F32, off=tmp_off)

    pp = [nc.alloc_psum_tensor("pp%d" % i, [128, 2, 512], F32) for i in range(4)]
    ps = [pp[i // 2][:, i % 2, :] for i in range(8)]
    bank = [Res() for _ in range(8)]

    xres = [[Res() for _ in range(4)] for _ in range(8)]
    hres = [[Res() for _ in range(4)] for _ in range(8)]
    qres = [[Res() for _ in range(4)] for _ in range(4)]
    kres = [[Res() for _ in range(4)] for _ in range(4)]
    vres = [Res() for _ in range(16)]
    ures = [[Res() for _ in range(4)] for _ in range(4)]
    sqres = [Res() for _ in range(8)]
    ringres = [Res() for _ in range(3)]
    tmpres = [Res() for _ in range(4)]
    rstdres = Res()
    constres = Res()
    parres = Res()
    modresA = [Res(), Res()]
    derresA = [Res(), Res()]
    ysres = [Res() for _ in range(8)]
    ptres = [Res() for _ in range(2)]
    rdres2 = [[Res(), Res()], [Res(), Res()]]
    bcres = [[Res(), Res()], [Res(), Res()]]
    scrbres = [Res() for _ in range(8)]
    cumres = [Res(), Res()]
    splres = Res()
    fres = Res()
    cumTres = Res()
    scrres = Res()
    hidres = [[Res() for _ in range(2)] for _ in range(NHC)]
    diagres = [Res() for _ in range(4)]
    recres = Res()
    miscres = Res()

    op = tr.op

    sched = []
    for t in range(12):
        sched.append((wA_d[t], 4096))
    for l in range(NL):
        for t in range(5):
            sched.append((wIN_d[l * 5 + t], 4096))
        for t in range(2):
            sched.append((wO_d[l * 2 + t], 4096))
        for half in range(2):
            for t in range(11):
                sched.append((wFI_d[l * 11 + t], 4096))
                if half == 0 and l + 1 < NL:
                    sched.append((wA_d[(l + 1) * 12 + t], 4096))
                    if t == 10:
                        sched.append((wA_d[(l + 1) * 12 + 11], 4096))
            for t in range(8):
                sched.append((wFO_d[l * 8 + t], DFF))
    wstate = dict(issued=0, used=0)

    def wissue(upto):
        while wstate["issued"] < min(upto, len(sched)):
            i = wstate["issued"]
            src, ncols = sched[i]
            s = i % 3
            tr.dma("pool", out=ring[s][:, 0:ncols], in_=src, w=[ringres[s]])
            wstate["issued"] += 1

    def wnext(ahead=3):
        n = wstate["used"]
        wstate["used"] += 1
        wissue(n + ahead)
        s = n % 3
        return ring[s], ringres[s]

    for kc in range(8):
        tr.dma("sp", out=xT[:, kc, :], in_=xT_d[kc * 128:(kc + 1) * 128, :], w=xres[kc])
    tr.dma("sp", out=par[:], in_=par_d[:, :], w=[parres])
    tr.dma("sp", out=cTs[:], in_=cT_d[:, :], w=[miscres])
    tr.dma("pool", out=wf[:], in_=wf_d[:, :], w=[constres])
    op("pool", lambda g: g.memset(ones_b[:], 1.0), w=[constres])
    op("pool", lambda g: g.memset(ones_f[:], 1.0), w=[constres])
    op("pool", lambda g: g.memset(tmp[0][:], 0.0), w=[tmpres[0]])
    op("pool", lambda g: g.memset(tmp[1][:], 1.0), w=[tmpres[1]])
    op("pool", lambda g: g.affine_select(out=maskb[:], in_=tmp[0][:, 0:128], pattern=[[1, 128]], compare_op=ALU.is_ge,
                                         fill=-30000.0, base=0, channel_multiplier=-1), r=[tmpres[0]], w=[constres])
    op("pool", lambda g: g.affine_select(out=ident[:], in_=tmp[1][:, 0:128], pattern=[[1, 128]], compare_op=ALU.is_equal,
                                         fill=0.0, base=0, channel_multiplier=-1), r=[tmpres[1]], w=[constres])
    op("pool", lambda g: g.memset(va[:, :, :, 64:65], 1.0), w=vres)
    op("pool", lambda g: g.memset(uT[:, :, 0:30], 0.0), w=[ures[fc][0] for fc in range(4)])
    op("act", lambda a: a.activation(out=cact[:], in_=cTs[:], func=AF.Silu), r=[miscres], w=[constres])
    wissue(3)
    tr.barrier(("pe", "act", "dve", "pool"))

    evac_rr = [0]

    def evac(out_ap, in_ap, r, w, scale=None):
        evac_rr[0] ^= 1
        if evac_rr[0]:
            if scale is None:
                op("act", lambda a: a.activation(out=out_ap, in_=in_ap, func=AF.Copy), r=r, w=w)
            else:
                op("act", lambda a: a.activation(out=out_ap, in_=in_ap, func=AF.Identity, scale=scale), r=r, w=w)
        else:
            if scale is None:
                op("dve", lambda v: v.tensor_copy(out=out_ap, in_=in_ap), r=r, w=w)
            else:
                op("dve", lambda v: v.tensor_scalar(out=out_ap, in0=in_ap, scalar1=scale, scalar2=None, op0=ALU.mult),
                   r=r, w=w)

    def rstd_from(bk, scale):
        op("act", lambda a: a.activation(out=rstd_t[:], in_=ps[bk][:], func=AF.Ln, bias=EPS, scale=scale),
           r=[bank[bk]], w=[rstdres])
        op("act", lambda a: a.activation(out=rstd_t[:], in_=rstd_t[:], func=AF.Exp, scale=-0.5),
           r=[rstdres], w=[rstdres])

    trr = [0]

    def gettmp():
        trr[0] = (trr[0] + 1) % 3
        return tmp[trr[0]], tmpres[trr[0]]

    def ada_tile(l, t):
        W, Wr = wnext()
        for jj in range(4):
            j = t * 4 + jj
            for kc in range(8):
                op("pe", lambda p: p.matmul(ps[7][:, j:j + 1], lhsT=W[:, kc * 512 + jj * 128: kc * 512 + jj * 128 + 128],
                                            rhs=cact[:, kc:kc + 1], start=(kc == 0), stop=(kc == 7)),
                   r=[Wr, constres], w=[bank[7]], mark=(kc == 7 and jj == 3))

    def ada_fin(l):
        pb = l * NPAR
        mod, der, nbf, modres, derres = modA[l % 2], derA[l % 2], nbfA[l % 2], modresA[l % 2], derresA[l % 2]
        op("dve", lambda v: v.tensor_tensor(out=mod[:], in0=ps[7][:, 0:48], in1=par[:, pb + P_ADAB: pb + P_ADAB + 48], op=ALU.add),
           r=[bank[7], parres], w=[modres])
        op("dve", lambda v: v.scalar_tensor_tensor(out=der[:, 0:8], in0=mod[:, 8:16], scalar=1.0, in1=par[:, pb + P_PRE1: pb + P_PRE1 + 8],
                                                   op0=ALU.add, op1=ALU.mult), r=[modres, parres], w=[derres])
        op("dve", lambda v: v.tensor_tensor(out=der[:, 8:16], in0=mod[:, 16:24], in1=par[:, pb + P_POST1: pb + P_POST1 + 8], op=ALU.mult),
           r=[modres, parres], w=[derres])
        op("dve", lambda v: v.scalar_tensor_tensor(out=der[:, 16:24], in0=mod[:, 32:40], scalar=1.0, in1=par[:, pb + P_PRE2: pb + P_PRE2 + 8],
                                                   op0=ALU.add, op1=ALU.mult), r=[modres, parres], w=[derres])
        op("dve", lambda v: v.tensor_tensor(out=der[:, 24:32], in0=mod[:, 40:48], in1=par[:, pb + P_POST2: pb + P_POST2 + 8], op=ALU.mult),
           r=[modres, parres], w=[derres])
        op("dve", lambda v: v.tensor_scalar(out=nbf[:], in0=par[:, pb + P_BF: pb + P_BF + 1], scalar1=-1.0, scalar2=None, op0=ALU.mult),
           r=[parres], w=[derres])

    def prenorm_a(tb):
        ts = slice(tb * 512, (tb + 1) * 512)
        for kc in range(8):
            op("act", lambda a: a.activation(out=sq[:, kc, :], in_=xT[:, kc, ts], func=AF.Square),
               r=[xres[kc][tb]], w=[sqres[kc]])

    def prenorm_b(l, tb, a_off, sh_off):
        mod, der, modres, derres = modA[l % 2], derA[l % 2], modresA[l % 2], derresA[l % 2]
        ts = slice(tb * 512, (tb + 1) * 512)
        for kc in range(8):
            op("pe", lambda p: p.matmul(ps[6][:], lhsT=ones_b[:], rhs=sq[:, kc, :], start=(kc == 0), stop=(kc == 7)),
               r=[sqres[kc], constres], w=[bank[6]], mark=(kc == 7))
        rstd_from(6, 1.0 / D)
        for kc in range(8):
            tt, ttr = gettmp()
            op("dve", lambda v: v.tensor_tensor(out=tt[:], in0=xT[:, kc, ts], in1=rstd_t[:], op=ALU.mult),
               r=[xres[kc][tb], rstdres], w=[ttr])
            op("act", lambda a: a.activation(out=hT[:, kc, ts], in_=tt[:], func=AF.Identity,
                                             bias=mod[:, sh_off + kc: sh_off + kc + 1], scale=der[:, a_off + kc: a_off + kc + 1]),
               r=[ttr, modres, derres], w=[hres[kc][tb]])

    def postnorm_residual(l, ys, ysr, tb, g_off, extra_r=()):
        der, derres = derA[l % 2], derresA[l % 2]
        ts = slice(tb * 512, (tb + 1) * 512)
        for kc in range(8):
            op("pe", lambda p: p.matmul(ps[6][:], lhsT=ones_b[:], rhs=sq[:, kc, :], start=(kc == 0), stop=(kc == 7)),
               r=[sqres[kc], constres], w=[bank[6]], mark=(kc == 7))
        rstd_from(6, 1.0 / D)
        for kc in range(8):
            tt, ttr = gettmp()
            op("dve", lambda v: v.scalar_tensor_tensor(out=tt[:], in0=ys[:, kc, :], scalar=der[:, g_off + kc: g_off + kc + 1],
                                                       in1=rstd_t[:], op0=ALU.mult, op1=ALU.mult),
               r=(ysr[kc] if isinstance(ysr[kc], list) else [ysr[kc]]) + [derres, rstdres] + list(extra_r), w=[ttr])
            op("dve", lambda v: v.tensor_tensor(out=xT[:, kc, ts], in0=xT[:, kc, ts], in1=tt[:], op=ALU.add),
               r=[ttr, xres[kc][tb]], w=[xres[kc][tb]])

    brr = [0]

    def nextbank(n=6):
        brr[0] = (brr[0] + 1) % n
        return brr[0]

    def proj_fm(W, Wr, col0, tb, bk):
        ts = slice(tb * 512, (tb + 1) * 512)
        for kc in range(8):
            op("pe", lambda p: p.matmul(ps[bk][:], lhsT=W[:, kc * 512 + col0: kc * 512 + col0 + 128], rhs=hT[:, kc, ts],
                                        start=(kc == 0), stop=(kc == 7)),
               r=[Wr, hres[kc][tb]], w=[bank[bk]], mark=(kc == 7))

    def inproj(l):
        op("dve", lambda v: v.memset(va[:, :, :, 64:65], 1.0), w=vres)
        for tb in range(4):
            ts = slice(tb * 512, (tb + 1) * 512)
            for kc in range(8):
                op("pe", lambda p: p.matmul(ps[6][0:8, :], lhsT=wf[:, l * 64 + kc * 8: l * 64 + kc * 8 + 8], rhs=hT[:, kc, ts],
                                            start=(kc == 0), stop=(kc == 7)),
                   r=[constres, hres[kc][tb]], w=[bank[6]], mark=(kc == 7))
            op("act", lambda a: a.activation(out=fT[0:8, ts], in_=ps[6][0:8, :], func=AF.Exp, bias=nbfA[l % 2][0:8, 0:1], scale=-1.0),
               r=[bank[6], derresA[l % 2]], w=[fres] + sqres)
        op("act", lambda a: a.activation(out=fT[:], in_=fT[:], func=AF.Ln, bias=1.0, scale=1.0), r=[fres], w=[fres])

        for which in range(2):
            W, Wr = wnext()
            dst, dres = (qT, qres) if which == 0 else (kT, kres)
            for fc in range(4):
                for tb in range(4):
                    bk = nextbank()
                    proj_fm(W, Wr, fc * 128, tb, bk)
                    evac(dst[:, fc, tb * 512:(tb + 1) * 512], ps[bk][:], [bank[bk]], [dres[fc][tb]],
                         scale=(0.125 if which == 0 else None))
            if which == 0:
                fchain()
                op("dve", lambda v: v.memset(uT[:, :, 0:30], 0.0), w=[ures[fc][0] for fc in range(4)] + [splres])
        W, Wr = wnext()
        for tk in range(16):
            bk = nextbank()
            for kc in range(8):
                op("pe", lambda p: p.matmul(ps[bk][:], lhsT=hT[:, kc, tk * 128:(tk + 1) * 128], rhs=W[:, kc * 512:(kc + 1) * 512],
                                            start=(kc == 0), stop=(kc == 7)),
                   r=[Wr, hres[kc][tk // 4]], w=[bank[bk]], mark=(kc == 7))
            evac(va[:, tk, :, 0:64], ps[bk][:].rearrange("p (h d) -> p h d", h=NH), [bank[bk]], [vres[tk]])
        for half in range(2):
            W, Wr = wnext()
            for c2 in range(2):
                fc = half * 2 + c2
                for tb in range(4):
                    b0 = nextbank()
                    proj_fm(W, Wr, c2 * 128, tb, b0)
                    b1 = nextbank()
                    proj_fm(W, Wr, 256 + c2 * 128, tb, b1)
                    tt, ttr = gettmp()
                    op("act", lambda a: a.activation(out=tt[:], in_=ps[b1][:], func=AF.Sigmoid), r=[bank[b1]], w=[ttr])
                    op("dve", lambda v: v.tensor_tensor(out=uT[:, fc, 30 + tb * 512: 30 + (tb + 1) * 512], in0=ps[b0][:], in1=tt[:],
                                                        op=ALU.mult), r=[bank[b0], ttr], w=[ures[fc][tb]])
    def fchain():
        op("dve", lambda v: v.tensor_tensor_scan(out=cumT[:], data0=fT[:], data1=fT[:], initial=0.0, op0=ALU.add, op1=ALU.max),
           r=[fres] + tmpres, w=[cumTres] + tmpres)
        op("dve", lambda v: v.tensor_copy(out=spl[:, 0:T], in_=cumT[:]), r=[cumTres], w=[splres])
        op("dve", lambda v: v.tensor_tensor(out=fT[:], in0=cumT[:], in1=spl[:, 0:T], op=ALU.subtract), r=[cumTres, splres], w=[fres])
        op("dve", lambda v: v.tensor_copy(out=spl[:, T:2 * T], in_=fT[:]), r=[fres], w=[splres])
        op("dve", lambda v: v.tensor_tensor(out=cumT[:], in0=fT[:], in1=spl[:, T:2 * T], op=ALU.subtract), r=[fres, splres], w=[cumTres])
        op("dve", lambda v: v.tensor_copy(out=spl[:, 2 * T:3 * T], in_=cumT[:]), r=[cumTres], w=[splres])
        tr.dma("sp", out=scr_d[:, :], in_=spl[:], r=[splres], w=[scrres])

    def attention():
        for b in range(2):
            extra = [] if b == 0 else ([fres] + sqres)
            op("dve", lambda v: v.memset(cumq[b][:], 1.0), w=[cumres[b]] + extra)
            op("dve", lambda v: v.memset(cumk[b][:], 0.0), w=[cumres[b]] + extra)
            op("dve", lambda v: v.memset(cumk[b][0:3, :], -1.0), w=[cumres[b]])
            op("dve", lambda v: v.memset(cumk[b][64:67, :], -1.0), w=[cumres[b]])
        def cum_load(c):
            b = c % 2
            for hx in range(2):
                h = 2 * c + hx
                src = scr_d[h:h + 1, :].rearrange("o (r t) -> (o r) t", r=3)
                tr.dma("sp", out=cumq[b][64 * hx:64 * hx + 3, :], in_=src, r=[scrres], w=[cumres[b]])
                tr.dma("sp", out=cumk[b][64 * hx + 3:64 * hx + 6, :], in_=src, r=[scrres], w=[cumres[b]])

        cnt = 0
        ocnt = 0
        pending = [None]
        for c in range(4):
            fc = c
            b = c % 2
            if c == 0:
                cum_load(0)
            for i in range(4):
                nj = 4 * i + 4
                if i == 3 and c < 3:
                    cum_load(c + 1)
                pbuf = ocnt % 2
                ob0 = 4 + 2 * pbuf
                ocnt += 1

                def s_tile(j):
                    nonlocal cnt
                    t = cnt % 2
                    cnt += 1
                    c0 = max(0, j - 4 * i) * 128
                    qs = slice(i * 512 + c0, (i + 1) * 512)
                    ks = slice(j * 128, (j + 1) * 128)
                    diagonal = j >= 4 * i
                    bks = [bank[2 * t], bank[2 * t + 1]]
                    for hx in range(2):
                        pr = 64 * hx
                        op("pe", lambda p: p.matmul(ps[2 * t + hx][:, c0:512], lhsT=kT[pr:pr + 64, fc, ks], rhs=qT[pr:pr + 64, fc, qs],
                                                    start=True, stop=False),
                           r=[kres[fc][j // 4], qres[fc][i]], w=[bks[hx]])
                    if diagonal:
                        for hx in range(2):
                            op("pe", lambda p: p.matmul(ps[2 * t + hx][:, c0:c0 + 128], lhsT=ident[:], rhs=maskb[:], start=False, stop=False),
                               r=[constres], w=[bks[hx]])
                    for hx in range(2):
                        pr = 64 * hx
                        op("pe", lambda p: p.matmul(ps[2 * t + hx][:, c0:512], lhsT=cumk[b][pr:pr + 64, ks], rhs=cumq[b][pr:pr + 64, qs],
                                                    start=False, stop=True),
                           r=[cumres[b]], w=[bks[hx]], mark=(hx == 1))
                    op("act", lambda a: a.activation(out=ptb2[t][:, :, c0:512], in_=pp[t][:, :, c0:512], func=AF.Exp),
                       r=bks, w=[ptres[t]])
                    return (j, c0, t)

                def pv_tile(tl):
                    j, c0, t = tl
                    for hx in range(2):
                        hh = 2 * c + hx
                        mw = 128 if hh < 7 else 65
                        op("pe", lambda p: p.matmul(ps[ob0 + hx][0:mw, c0:512], lhsT=va[:, j, :, :].rearrange("p h d -> p (h d)")[:, hh * 65:hh * 65 + mw],
                                                    rhs=ptb2[t][:, hx, c0:512],
                                                    start=(j == 0), stop=(j == nj - 1)),
                           r=[ptres[t], vres[j]], w=[bank[ob0 + hx]], mark=(j == nj - 1 and hx == 1))

                prev = s_tile(0)
                if pending[0] is not None:
                    pending[0]()
                    pending[0] = None
                for j in range(1, nj):
                    curt = s_tile(j)
                    pv_tile(prev)
                    prev = curt
                pv_tile(prev)
                for hx in range(2):
                    ob = ob0 + hx
                    k = (2 * ocnt + hx) % 8
                    rd, rdr = rden2[pbuf], rdres2[pbuf][hx]
                    op("act", lambda a: a.activation(out=rd[64:65, hx, :], in_=ps[ob][64:65, :], func=AF.Ln), r=[bank[ob]], w=[rdr])
                    op("act", lambda a: a.activation(out=rd[64:65, hx, :], in_=rd[64:65, hx, :], func=AF.Exp, scale=-1.0), r=[rdr], w=[rdr])
                    tr.dma("sp", out=scrb_d[k:k + 1, :], in_=rd[64:65, hx, :], r=[rdr], w=[scrbres[k]])
                    tr.dma("sp", out=bcs[pbuf][0:64, hx, :], in_=scrb_d[k:k + 1, :].broadcast_to([64, 512]), r=[scrbres[k]], w=[bcres[pbuf][hx]])

                def fin(fc=fc, i=i, pbuf=pbuf, ob0=ob0):
                    for hx in range(2):
                        pr = 64 * hx
                        op("dve", lambda v: v.tensor_tensor(out=qT[pr:pr + 64, fc, i * 512:(i + 1) * 512], in0=ps[ob0 + hx][0:64, :],
                                                            in1=bcs[pbuf][0:64, hx, :], op=ALU.mult),
                           r=[bank[ob0 + hx], bcres[pbuf][hx]], w=[qres[fc][i]])

                pending[0] = fin
        pending[0]()

    def conv(l):
        pb = l * NPAR
        for fc in range(4):
            op("dve", lambda v: v.tensor_tensor(out=diag[:, fc, :, :],
                                                in0=ident[:].unsqueeze(1).broadcast_to([128, CK, 128]),
                                                in1=par[:, pb + P_CONVW + fc * CK: pb + P_CONVW + (fc + 1) * CK].unsqueeze(2).broadcast_to([128, CK, 128]),
                                                op=ALU.mult), r=[constres, parres], w=[diagres[fc]])
        for tb in (3, 2, 1, 0):
            for fc in range(4):
                rr = [ures[fc][tb], diagres[fc]] + ([ures[fc][tb - 1]] if tb > 0 else [])
                for k in range(CK):
                    op("pe", lambda p: p.matmul(ps[fc][:], lhsT=diag[:, fc, k, :], rhs=uT[:, fc, tb * 512 + k: tb * 512 + k + 512],
                                                start=(k == 0), stop=(k == CK - 1)), r=rr, w=[bank[fc]], mark=(k == CK - 1))
            for fc in range(4):
                op("act", lambda a: a.activation(out=ystage[:, fc, :], in_=ps[fc][:], func=AF.Identity,
                                                 bias=par[:, pb + P_CONVB + fc: pb + P_CONVB + fc + 1], scale=1.0),
                   r=[bank[fc], parres], w=[ysres[fc]])
                op("dve", lambda v: v.tensor_copy(out=sq[:, fc, :], in_=ystage[:, fc, :]), r=[ysres[fc]], w=[sqres[fc]])
                op("act", lambda a: a.activation(out=sq[:, 4 + fc, :], in_=ystage[:, fc, :], func=AF.Square),
                   r=[ysres[fc]], w=[sqres[4 + fc]])
            for fc in range(4):
                op("pe", lambda p: p.matmul(ps[4][:], lhsT=ones_b[:], rhs=sq[:, fc, :], start=(fc == 0), stop=(fc == 3)),
                   r=[sqres[fc], constres], w=[bank[4]], mark=(fc == 3))
            for fc in range(4):
                op("pe", lambda p: p.matmul(ps[5][:], lhsT=ones_b[:], rhs=sq[:, 4 + fc, :], start=(fc == 0), stop=(fc == 3)),
                   r=[sqres[4 + fc], constres], w=[bank[5]], mark=(fc == 3))
            mean, meanr = tmp[0], tmpres[0]
            var, varr = tmp[1], tmpres[1]
            op("dve", lambda v: v.tensor_scalar(out=mean[:], in0=ps[4][:], scalar1=1.0 / 512, scalar2=None, op0=ALU.mult),
               r=[bank[4]], w=[meanr])
            op("dve", lambda v: v.tensor_tensor(out=var[:], in0=mean[:], in1=mean[:], op=ALU.mult), r=[meanr], w=[varr])
            op("dve", lambda v: v.scalar_tensor_tensor(out=var[:], in0=ps[5][:], scalar=1.0 / 512, in1=var[:], op0=ALU.mult, op1=ALU.subtract),
               r=[bank[5], varr], w=[varr])
            op("act", lambda a: a.activation(out=rstd_t[:], in_=var[:], func=AF.Ln, bias=EPS, scale=1.0), r=[varr], w=[rstdres])
            op("act", lambda a: a.activation(out=rstd_t[:], in_=rstd_t[:], func=AF.Exp, scale=-0.5), r=[rstdres], w=[rstdres])
            for fc in range(4):
                op("dve", lambda v: v.tensor_tensor(out=ystage[:, fc, :], in0=ystage[:, fc, :], in1=mean[:], op=ALU.subtract),
                   r=[ysres[fc], meanr], w=[ysres[fc]])
                op("dve", lambda v: v.tensor_tensor(out=ystage[:, fc, :], in0=ystage[:, fc, :], in1=rstd_t[:], op=ALU.mult),
                   r=[ysres[fc], rstdres], w=[ysres[fc]])
                op("act", lambda a: a.activation(out=uT[:, fc, 30 + tb * 512: 30 + (tb + 1) * 512], in_=ystage[:, fc, :], func=AF.Silu,
                                                 bias=par[:, pb + P_LNB + fc: pb + P_LNB + fc + 1],
                                                 scale=par[:, pb + P_LNG + fc: pb + P_LNG + fc + 1]),
                   r=[ysres[fc], parres], w=[ures[fc][tb]])

    def wo(l):
        W0, W0r = wnext()
        W1, W1r = wnext(ahead=2)

        def mm(tb, oc):
            ts = slice(tb * 512, (tb + 1) * 512)
            us = slice(30 + tb * 512, 30 + (tb + 1) * 512)
            W, Wr = (W0, W0r) if oc < 4 else (W1, W1r)
            c0 = (oc % 4) * 128
            bk = nextbank(4)
            for kc in range(8):
                if kc < 4:
                    rhs, rr = qT[:, kc, ts], qres[kc][tb]
                else:
                    rhs, rr = uT[:, kc - 4, us], ures[kc - 4][tb]
                op("pe", lambda p: p.matmul(ps[bk][:], lhsT=W[:, kc * 512 + c0: kc * 512 + c0 + 128], rhs=rhs,
                                            start=(kc == 0), stop=(kc == 7)), r=[Wr, rr], w=[bank[bk]], mark=(kc == 7))
            return bk

        def ev(oc, bk):
            op("dve", lambda v: v.tensor_copy(out=ystage[:, oc, :], in_=ps[bk][:]), r=[bank[bk]], w=[ysres[oc]])
            op("act", lambda a: a.activation(out=sq[:, oc, :], in_=ystage[:, oc, :], func=AF.Square), r=[ysres[oc]], w=[sqres[oc]])

        for oc in range(8):
            ev(oc, mm(0, oc))
        for tb in range(4):
            ahead = [mm(tb + 1, oc) for oc in range(4)] if tb < 3 else []
            postnorm_residual(l, ystage, ysres, tb, 8)
            prenorm_a(tb)
            prenorm_b(l, tb, 16, 24)
            if tb < 3:
                for oc in range(4):
                    ev(oc, ahead[oc])
                for oc in range(4, 8):
                    ev(oc, mm(tb + 1, oc))

    def ffn(l):
        nxt = l + 1 < NL
        for half in range(2):
            fcnt = 0
            for s in range(11):
                W, Wr = wnext()
                for c2 in range(2):
                    hc = 2 * s + c2
                    for tbl in range(2):
                        tb = half * 2 + tbl
                        bg = fcnt % 2
                        bu = 2 + fcnt % 2
                        fcnt += 1
                        proj_fm(W, Wr, c2 * 128, tb, bg)
                        proj_fm(W, Wr, 256 + c2 * 128, tb, bu)
                        tt, ttr = gettmp()
                        op("act", lambda a: a.activation(out=tt[:], in_=ps[bg][:], func=AF.Silu), r=[bank[bg]], w=[ttr])
                        op("dve", lambda v: v.tensor_tensor(out=hid[:, hc, tbl * 512:(tbl + 1) * 512], in0=ps[bu][:], in1=tt[:], op=ALU.mult),
                           r=[bank[bu], ttr], w=[hidres[hc][tbl]])
                if half == 0 and nxt:
                    ada_tile(l + 1, s)
                    if s == 10:
                        ada_tile(l + 1, 11)
                        ada_fin(l + 1)
            ys1 = hF32[:, :, half * 512:(half + 1) * 512]
            ys1res = [[hres[oc][half * 2], hres[oc][half * 2 + 1]] for oc in range(8)]
            for oc in range(8):
                W, Wr = wnext()
                pair = ((0, 1), (2, 3), (4, 5))[oc % 3]
                for tbl in range(2):
                    bk = pair[tbl]
                    for hc in range(NHC):
                        op("pe", lambda p: p.matmul(ps[bk][:], lhsT=W[:, hc * 128:(hc + 1) * 128], rhs=hid[:, hc, tbl * 512:(tbl + 1) * 512],
                                                    start=(hc == 0), stop=(hc == NHC - 1)),
                           r=[Wr, hidres[hc][tbl]], w=[bank[bk]], mark=(hc == NHC - 1))
                evac(ys0[:, oc, :], ps[pair[0]][:], [bank[pair[0]]], [ysres[oc]])
                evac(ys1[:, oc, :], ps[pair[1]][:], [bank[pair[1]]], ys1res[oc])
                if half == 1 and nxt:
                    if oc == 0:
                        prenorm_a(0)
                    elif oc == 2:
                        prenorm_b(l + 1, 0, 0, 0)
                    elif oc == 4:
                        prenorm_a(1)
                    elif oc == 6:
                        prenorm_b(l + 1, 1, 0, 0)
            for kc in range(8):
                op("act", lambda a: a.activation(out=sq[:, kc, :], in_=ys0[:, kc, :], func=AF.Square), r=[ysres[kc]], w=[sqres[kc]])
            postnorm_residual(l, ys0, ysres, half * 2, 24)
            for kc in range(8):
                op("act", lambda a: a.activation(out=sq[:, kc, :], in_=ys1[:, kc, :], func=AF.Square), r=ys1res[kc], w=[sqres[kc]])
            postnorm_residual(l, ys1, ys1res, half * 2 + 1, 24)
        if nxt:
            for tb in (2, 3):
                prenorm_a(tb)
                prenorm_b(l + 1, tb, 0, 0)

    for t in range(12):
        ada_tile(0, t)
    ada_fin(0)
    for tb in range(4):
        prenorm_a(tb)
        prenorm_b(0, tb, 0, 0)
    for l in range(NL):
        tr.barrier()
        inproj(l)
        tr.barrier()
        attention()
        tr.barrier()
        conv(l)
        tr.barrier()
        wo(l)
        tr.barrier()
        ffn(l)
    tr.barrier()
    for kc in range(8):
        tr.dma("sp", out=out_d[kc * 128:(kc + 1) * 128, :], in_=xT[:, kc, :], r=xres[kc])
    tr.wait_all_dma("sp")
    return nc


def _kc_tile(w2d, cols):
    sub = w2d[:, cols]
    n = sub.shape[1]
    return np.ascontiguousarray(sub.reshape(8, 128, n).transpose(1, 0, 2).reshape(128, 8 * n))


def prep_weights(NL, w_in, b_f, conv_w, conv_b, conv_ln_g, conv_ln_b, w_o, w_ffn_in, w_ffn_out,
                 mix_pre_g, mix_post_g, ffn_pre_g, ffn_post_g, ada_w, ada_b):
    f32 = np.float32
    wA = np.empty((NL * 12, 128, 4096), f32)
    wIN = np.empty((NL * 5, 128, 4096), f32)
    wO = np.empty((NL * 2, 128, 4096), f32)
    wFI = np.empty((NL * 11, 128, 4096), f32)
    wFO = np.empty((NL * 8, 128, DFF), f32)
    par = np.zeros((128, NL * NPAR), f32)
    wf = np.empty((128, NL * 64), f32)
    ar = np.arange
    for l in range(NL):
        for t in range(12):
            wA[l * 12 + t] = _kc_tile(ada_w[l], ar(t * 512, (t + 1) * 512))
        wi = w_in[l]
        wIN[l * 5 + 0] = _kc_tile(wi, ar(0, 512))
        wIN[l * 5 + 1] = _kc_tile(wi, ar(512, 1024))
        wIN[l * 5 + 2] = _kc_tile(wi, ar(1024, 1536))
        for half in range(2):
            cols = np.concatenate([ar(1544 + half * 256, 1544 + half * 256 + 256), ar(2056 + half * 256, 2056 + half * 256 + 256)])
            wIN[l * 5 + 3 + half] = _kc_tile(wi, cols)
        wf[:, l * 64:(l + 1) * 64] = _kc_tile(wi, ar(1536, 1544))
        for t in range(2):
            wO[l * 2 + t] = _kc_tile(w_o[l], ar(t * 512, (t + 1) * 512))
        for s in range(11):
            cols = np.concatenate([ar(s * 256, s * 256 + 256), ar(DFF + s * 256, DFF + s * 256 + 256)])
            wFI[l * 11 + s] = _kc_tile(w_ffn_in[l], cols)
        wo3 = w_ffn_out[l].reshape(NHC, 128, 8, 128)
        wFO[l * 8:(l + 1) * 8] = wo3.transpose(2, 1, 0, 3).reshape(8, 128, DFF)
        pb = l * NPAR

        def fm(v):
            return v.reshape(8, 128).T

        par[:, pb + P_PRE1: pb + P_PRE1 + 8] = fm(mix_pre_g[l])
        par[:, pb + P_POST1: pb + P_POST1 + 8] = fm(mix_post_g[l])
        par[:, pb + P_PRE2: pb + P_PRE2 + 8] = fm(ffn_pre_g[l])
        par[:, pb + P_POST2: pb + P_POST2 + 8] = fm(ffn_post_g[l])
        par[:, pb + P_ADAB: pb + P_ADAB + 48] = ada_b[l].reshape(48, 128).T
        par[:, pb + P_CONVW: pb + P_CONVW + 4 * CK] = conv_w[l].reshape(CK, 4, 128).transpose(2, 1, 0).reshape(128, 4 * CK)
        par[:, pb + P_CONVB: pb + P_CONVB + 4] = conv_b[l].reshape(4, 128).T
        par[:, pb + P_LNG: pb + P_LNG + 4] = conv_ln_g[l].reshape(4, 128).T
        par[:, pb + P_LNB: pb + P_LNB + 4] = conv_ln_b[l].reshape(4, 128).T
        par[0:8, pb + P_BF] = b_f[l]
    return dict(wA=wA, wIN=wIN, wO=wO, wFI=wFI, wFO=wFO, par=par, wf=wf)


_NC_CACHE = {}


def run_layers(x, c, NL, weights, cores=None):
    B = x.shape[0]
    if cores is None:
        cores = list(range(B))
    if NL not in _NC_CACHE:
        _NC_CACHE[NL] = build(NL)
    nc = _NC_CACHE[NL]
    in_maps = []
    for b in cores:
        m = dict(weights)
        m["xT"] = np.ascontiguousarray(x[b].T)
        m["cT"] = np.ascontiguousarray(c[b].reshape(8, 128).T)
        in_maps.append(m)
    res = run_bass_kernel_spmd(nc, in_maps, core_ids=list(range(len(cores))))
    out = np.empty((len(cores), T, D), np.float32)
    for i in range(len(cores)):
        out[i] = res.results[i]["outT"].T
    return out


def kernel(x, c, w_in, b_f, conv_w, conv_b, conv_ln_g, conv_ln_b, w_o, w_ffn_in, w_ffn_out,
           mix_pre_g, mix_post_g, ffn_pre_g, ffn_post_g, ada_w, ada_b):
    a = [np.asarray(v, dtype=np.float32) for v in (w_in, b_f, conv_w, conv_b, conv_ln_g, conv_ln_b, w_o, w_ffn_in, w_ffn_out,
                                                   mix_pre_g, mix_post_g, ffn_pre_g, ffn_post_g, ada_w, ada_b)]
    weights = prep_weights(L_FULL, *a)
    x = np.asarray(x, dtype=np.float32)
    c = np.asarray(c, dtype=np.float32)
    return run_layers(x, c, L_FULL, weights)
```

```python
import bisect
import os
import numpy as np
import concourse.bass as bass
import concourse.mybir as mybir
from concourse.bass_utils import run_bass_kernel_spmd

F32 = mybir.dt.float32
BF16 = mybir.dt.bfloat16
U8 = mybir.dt.uint8
AF = mybir.ActivationFunctionType
ALU = mybir.AluOpType

D = 1024
T = 2048
L_FULL = 4
NH = 8
DFF = 2816
NHC = DFF // 128
CK = 31
EPS = 1e-6
NPAR = 224
P_PRE1, P_POST1, P_PRE2, P_POST2, P_ADAB, P_CONVW, P_CONVB, P_LNG, P_LNB, P_BF = 0, 8, 16, 24, 32, 80, 204, 208, 212, 216


class Res:
    __slots__ = ("w", "r")

    def __init__(self):
        self.w = None
        self.r = {}


class Tracker:
    def __init__(self, nc):
        self.nc = nc
        self.E = {}
        for name, h, am in (("pe", nc.tensor, False), ("act", nc.scalar, True), ("dve", nc.vector, True),
                            ("pool", nc.gpsimd, True), ("sp", nc.sync, True)):
            self.E[name] = dict(h=h, sem=nc.alloc_semaphore("s_" + name), am=am, n=0, marks=[], last=None, seen={})
        self.dsems = {"sp": [[nc.alloc_semaphore("dsp%d" % i), 0] for i in range(10)],
                      "pool": [[nc.alloc_semaphore("dpl%d" % i), 0] for i in range(6)]}
        self.drr = {"sp": 0, "pool": 0}
        self.nwait = 0

    def resolve(self, tok):
        if tok[0] == "d":
            return tok[1], tok[2]
        e = self.E[tok[1]]
        idx = tok[2]
        if e["am"]:
            return e["sem"], idx + 1
        marks = e["marks"]
        k = bisect.bisect_left(marks, idx)
        if k == len(marks):
            e["last"].then_inc(e["sem"], 1)
            marks.append(e["n"] - 1)
        return e["sem"], k + 1

    def _deps(self, eng, r, w, is_dma):
        need = {}
        e = self.E[eng]

        def add(tok, raw):
            if tok is None:
                return
            if tok[0] == "e" and tok[1] == eng and not is_dma:
                if eng == "pe":
                    return
            sem, val = self.resolve(tok)
            if e["seen"].get(sem.num, 0) >= val:
                return
            if need.get(sem.num, (None, 0))[1] < val:
                need[sem.num] = (sem, val)

        for res in r:
            add(res.w, True)
        for res in w:
            add(res.w, False)
            for tok in res.r.values():
                add(tok, False)
        return need

    def _emit(self, e, need, fn):
        items = list(need.values())
        for sem, val in items[:-1]:
            e["h"].wait_ge(sem, val)
            self.nwait += 1
        ins = fn(e["h"])
        if items:
            ins._wait_ge(*items[-1])
        for sem, val in items:
            e["seen"][sem.num] = val
        return ins

    def op(self, eng, fn, r=(), w=(), mark=False):
        e = self.E[eng]
        need = self._deps(eng, r, w, False)
        ins = self._emit(e, need, fn)
        idx = e["n"]
        e["n"] += 1
        e["last"] = ins
        if e["am"] or mark:
            ins.then_inc(e["sem"], 1)
            e["marks"].append(idx)
        tok = ("e", eng, idx)
        for res in r:
            res.r[eng] = tok
        for res in w:
            res.w = tok
            res.r = {}
        return ins

    def dma(self, q, out, in_, r=(), w=()):
        e = self.E[q]
        pool = self.dsems[q]
        i = self.drr[q]
        self.drr[q] = (i + 1) % len(pool)
        S = pool[i]
        need = self._deps(q, r, w, True)
        if S[1] > 0 and e["seen"].get(S[0].num, 0) < S[1]:
            if need.get(S[0].num, (None, 0))[1] < S[1]:
                need[S[0].num] = (S[0], S[1])
        ins = self._emit(e, need, lambda h: h.dma_start(out=out, in_=in_))
        ins.then_inc(S[0], 16)
        S[1] += 16
        tok = ("d", S[0], S[1])
        key = "dma_" + q + str(i)
        for res in r:
            res.r[key] = tok
        for res in w:
            res.w = tok
            res.r = {}
        return tok

    def barrier(self, engs=("pe", "act", "dve")):
        toks = {}
        for x in engs:
            if self.E[x]["n"] > 0:
                toks[x] = self.resolve(("e", x, self.E[x]["n"] - 1))
        for x in engs:
            e = self.E[x]
            for y, (sem, val) in toks.items():
                if y == x:
                    continue
                if e["seen"].get(sem.num, 0) < val:
                    e["h"].wait_ge(sem, val)
                    e["seen"][sem.num] = val

    def wait_all_dma(self, q):
        e = self.E[q]
        for S in self.dsems["sp"] + self.dsems["pool"]:
            if S[1] > 0:
                e["h"].wait_ge(S[0], S[1])


def build(NL, dbg=False, stop=99):
    nc = bass.Bass("TRN2", target_bir_lowering=False)
    tr = Tracker(nc)

    xT_d = nc.dram_tensor("xT", [D, T], F32, kind="ExternalInput").ap()
    cT_d = nc.dram_tensor("cT", [128, 8], F32, kind="ExternalInput").ap()
    par_d = nc.dram_tensor("par", [128, NL * NPAR], F32, kind="ExternalInput").ap()
    wf_d = nc.dram_tensor("wf", [128, NL * 64], F32, kind="ExternalInput").ap()
    wA_d = nc.dram_tensor("wA", [NL * 12, 128, 4096], F32, kind="ExternalInput").ap()
    wIN_d = nc.dram_tensor("wIN", [NL * 5, 128, 4096], F32, kind="ExternalInput").ap()
    wO_d = nc.dram_tensor("wO", [NL * 2, 128, 4096], F32, kind="ExternalInput").ap()
    wFI_d = nc.dram_tensor("wFI", [NL * 11, 128, 4096], F32, kind="ExternalInput").ap()
    wFO_d = nc.dram_tensor("wFO", [NL * 8, 128, DFF], F32, kind="ExternalInput").ap()
    out_d = nc.dram_tensor("outT", [D, T], F32, kind="ExternalOutput").ap()
    scr_d = nc.dram_tensor("scr", [8, 3 * T], BF16, kind="Internal").ap()
    scrb_d = nc.dram_tensor("scrb", [8, 512], F32, kind="Internal").ap()

    ARENA = 212736
    arena = nc.alloc_sbuf_tensor("arena", [128, ARENA], U8)
    base = nc.lookup_mloc(arena).addr
    cur = [base]

    def at(name, shape, dt, off=None):
        nb = int(np.prod(shape[1:])) * (4 if dt == F32 else 2)
        if off is None:
            off = cur[0]
            cur[0] += (nb + 31) // 32 * 32
            assert cur[0] <= base + ARENA, (name, cur[0] - base)
        return nc.alloc_sbuf_tensor_at(name, list(shape), dt, offset=off)

    xT = at("xTs", [128, 8, T], F32)
    h_off = cur[0]
    hT = at("hT", [128, 8, T], BF16)
    r3_off = cur[0]
    qT = at("qT", [128, 4, T], BF16)
    k_off = cur[0]
    kT = at("kT", [128, 4, T], BF16)
    va = at("va", [128, 16, NH, 65], BF16)
    UW = T + 32
    u_off = cur[0]
    uT = at("uT", [128, 4, UW], BF16)
    r3_end = cur[0]
    sq_off = cur[0]
    sq = at("sq", [128, 8, 512], BF16)
    ring = [at("ring%d" % i, [128, 4096], BF16) for i in range(3)]
    tmp_off = cur[0]
    tmp = [at("tmp%d" % i, [128, 512], F32) for i in range(4)]
    rstd_t = tmp[3]
    ident = at("ident", [128, 128], BF16)
    ones_b = at("ones_b", [128, 128], BF16)
    ones_f = at("ones_f", [128, 64], F32)
    maskb = at("maskb", [128, 128], BF16)
    par = at("par_s", [128, NL * NPAR], F32)
    wf = at("wf_s", [128, NL * 64], BF16)
    cTs = at("cTs", [128, 8], F32)
    cact = at("cact", [128, 8], BF16)
    modA = [at("mod%d" % i, [128, 48], F32) for i in range(2)]
    derA = [at("der%d" % i, [128, 32], F32) for i in range(2)]
    nbfA = [at("nbf%d" % i, [128, 1], F32) for i in range(2)]
    hid = at("hid", [128, NHC, 1024], BF16, off=r3_off)
    ys0 = at("ys0", [128, 8, 512], F32, off=r3_off + NHC * 1024 * 2)
    assert r3_off + NHC * 1024 * 2 + 8 * 512 * 4 <= r3_end
    ystage = at("ystage", [128, 8, 512], F32, off=k_off)
    hF32 = at("hF32", [128, 8, 1024], F32, off=h_off)
    diag = at("diag", [128, 4, CK, 128], BF16, off=h_off)
    ptb2 = [at("ptb2_%d" % i, [128, 2, 512], BF16, off=h_off + i * 2048) for i in range(2)]
    rden2 = [at("rden%d" % i, [128, 2, 512], F32, off=h_off + 4096 + i * 4096) for i in range(2)]
    bcs = [at("bcs%d" % i, [128, 2, 512], F32, off=h_off + 12288 + i * 4096) for i in range(2)]
    cumq = [at("cumq0", [128, T], BF16, off=h_off + 20480), at("cumq1", [128, T], BF16, off=sq_off)]
    cumk = [at("cumk0", [128, T], BF16, off=h_off + 24576), at("cumk1", [128, T], BF16, off=sq_off + 4096)]
    spl = at("spl", [8, 3 * T], BF16, off=u_off)
    fT = at("fT", [8, T], F32, off=sq_off)
    cumT = at("cumT", [8, T], F32, off=tmp_off)

    pp = [nc.alloc_psum_tensor("pp%d" % i, [128, 2, 512], F32) for i in range(4)]
    ps = [pp[i // 2][:, i % 2, :] for i in range(8)]
    bank = [Res() for _ in range(8)]

    xres = [[Res() for _ in range(4)] for _ in range(8)]
    hres = [[Res() for _ in range(4)] for _ in range(8)]
    qres = [[Res() for _ in range(4)] for _ in range(4)]
    kres = [[Res() for _ in range(4)] for _ in range(4)]
    vres = [Res() for _ in range(16)]
    ures = [[Res() for _ in range(4)] for _ in range(4)]
    sqres = [Res() for _ in range(8)]
    ringres = [Res() for _ in range(3)]
    tmpres = [Res() for _ in range(4)]
    rstdres = Res()
    constres = Res()
    parres = Res()
    modresA = [Res(), Res()]
    derresA = [Res(), Res()]
    ysres = [Res() for _ in range(8)]
    ptres = [Res() for _ in range(2)]
    rdres2 = [[Res(), Res()], [Res(), Res()]]
    bcres = [[Res(), Res()], [Res(), Res()]]
    scrbres = [Res() for _ in range(8)]
    cumres = [Res(), Res()]
    splres = Res()
    fres = Res()
    cumTres = Res()
    scrres = Res()
    hidres = [[Res() for _ in range(2)] for _ in range(NHC)]
    diagres = [Res() for _ in range(4)]
    recres = Res()
    miscres = Res()

    op = tr.op

    sched = []
    for t in range(12):
        sched.append((wA_d[t], 4096))
    for l in range(NL):
        for t in range(5):
            sched.append((wIN_d[l * 5 + t], 4096))
        for t in range(2):
            sched.append((wO_d[l * 2 + t], 4096))
        for half in range(2):
            for t in range(11):
                sched.append((wFI_d[l * 11 + t], 4096))
                if half == 0 and l + 1 < NL:
                    sched.append((wA_d[(l + 1) * 12 + t], 4096))
                    if t == 10:
                        sched.append((wA_d[(l + 1) * 12 + 11], 4096))
            for t in range(8):
                sched.append((wFO_d[l * 8 + t], DFF))
    wstate = dict(issued=0, used=0)

    def wissue(upto):
        while wstate["issued"] < min(upto, len(sched)):
            i = wstate["issued"]
            src, ncols = sched[i]
            s = i % 3
            tr.dma("pool", out=ring[s][:, 0:ncols], in_=src, w=[ringres[s]])
            wstate["issued"] += 1

    def wnext(ahead=3):
        n = wstate["used"]
        wstate["used"] += 1
        wissue(n + ahead)
        s = n % 3
        return ring[s], ringres[s]

    for kc in range(8):
        tr.dma("sp", out=xT[:, kc, :], in_=xT_d[kc * 128:(kc + 1) * 128, :], w=xres[kc])
    tr.dma("sp", out=par[:], in_=par_d[:, :], w=[parres])
    tr.dma("sp", out=cTs[:], in_=cT_d[:, :], w=[miscres])
    tr.dma("pool", out=wf[:], in_=wf_d[:, :], w=[constres])
    op("pool", lambda g: g.memset(ones_b[:], 1.0), w=[constres])
    op("pool", lambda g: g.memset(ones_f[:], 1.0), w=[constres])
    op("pool", lambda g: g.memset(tmp[0][:], 0.0), w=[tmpres[0]])
    op("pool", lambda g: g.memset(tmp[1][:], 1.0), w=[tmpres[1]])
    op("pool", lambda g: g.affine_select(out=maskb[:], in_=tmp[0][:, 0:128], pattern=[[1, 128]], compare_op=ALU.is_ge,
                                         fill=-30000.0, base=0, channel_multiplier=-1), r=[tmpres[0]], w=[constres])
    op("pool", lambda g: g.affine_select(out=ident[:], in_=tmp[1][:, 0:128], pattern=[[1, 128]], compare_op=ALU.is_equal,
                                         fill=0.0, base=0, channel_multiplier=-1), r=[tmpres[1]], w=[constres])
    op("pool", lambda g: g.memset(va[:, :, :, 64:65], 1.0), w=vres)
    op("pool", lambda g: g.memset(uT[:, :, 0:30], 0.0), w=[ures[fc][0] for fc in range(4)])
    op("act", lambda a: a.activation(out=cact[:], in_=cTs[:], func=AF.Silu), r=[miscres], w=[constres])
    wissue(3)
    tr.barrier(("pe", "act", "dve", "pool"))

    evac_rr = [0]

    def evac(out_ap, in_ap, r, w, scale=None):
        evac_rr[0] ^= 1
        if evac_rr[0]:
            if scale is None:
                op("act", lambda a: a.activation(out=out_ap, in_=in_ap, func=AF.Copy), r=r, w=w)
            else:
                op("act", lambda a: a.activation(out=out_ap, in_=in_ap, func=AF.Identity, scale=scale), r=r, w=w)
        else:
            if scale is None:
                op("dve", lambda v: v.tensor_copy(out=out_ap, in_=in_ap), r=r, w=w)
            else:
                op("dve", lambda v: v.tensor_scalar(out=out_ap, in0=in_ap, scalar1=scale, scalar2=None, op0=ALU.mult),
                   r=r, w=w)

    def rstd_from(bk, scale):
        op("act", lambda a: a.activation(out=rstd_t[:], in_=ps[bk][:], func=AF.Ln, bias=EPS, scale=scale),
           r=[bank[bk]], w=[rstdres])
        op("act", lambda a: a.activation(out=rstd_t[:], in_=rstd_t[:], func=AF.Exp, scale=-0.5),
           r=[rstdres], w=[rstdres])

    trr = [0]

    def gettmp():
        trr[0] = (trr[0] + 1) % 3
        return tmp[trr[0]], tmpres[trr[0]]

    def ada_tile(l, t):
        W, Wr = wnext()
        for jj in range(4):
            j = t * 4 + jj
            for kc in range(8):
                op("pe", lambda p: p.matmul(ps[7][:, j:j + 1], lhsT=W[:, kc * 512 + jj * 128: kc * 512 + jj * 128 + 128],
                                            rhs=cact[:, kc:kc + 1], start=(kc == 0), stop=(kc == 7)),
                   r=[Wr, constres], w=[bank[7]], mark=(kc == 7 and jj == 3))

    def ada_fin(l):
        pb = l * NPAR
        mod, der, nbf, modres, derres = modA[l % 2], derA[l % 2], nbfA[l % 2], modresA[l % 2], derresA[l % 2]
        op("dve", lambda v: v.tensor_tensor(out=mod[:], in0=ps[7][:, 0:48], in1=par[:, pb + P_ADAB: pb + P_ADAB + 48], op=ALU.add),
           r=[bank[7], parres], w=[modres])
        op("dve", lambda v: v.scalar_tensor_tensor(out=der[:, 0:8], in0=mod[:, 8:16], scalar=1.0, in1=par[:, pb + P_PRE1: pb + P_PRE1 + 8],
                                                   op0=ALU.add, op1=ALU.mult), r=[modres, parres], w=[derres])
        op("dve", lambda v: v.tensor_tensor(out=der[:, 8:16], in0=mod[:, 16:24], in1=par[:, pb + P_POST1: pb + P_POST1 + 8], op=ALU.mult),
           r=[modres, parres], w=[derres])
        op("dve", lambda v: v.scalar_tensor_tensor(out=der[:, 16:24], in0=mod[:, 32:40], scalar=1.0, in1=par[:, pb + P_PRE2: pb + P_PRE2 + 8],
                                                   op0=ALU.add, op1=ALU.mult), r=[modres, parres], w=[derres])
        op("dve", lambda v: v.tensor_tensor(out=der[:, 24:32], in0=mod[:, 40:48], in1=par[:, pb + P_POST2: pb + P_POST2 + 8], op=ALU.mult),
           r=[modres, parres], w=[derres])
        op("dve", lambda v: v.tensor_scalar(out=nbf[:], in0=par[:, pb + P_BF: pb + P_BF + 1], scalar1=-1.0, scalar2=None, op0=ALU.mult),
           r=[parres], w=[derres])

    def prenorm_a(tb):
        ts = slice(tb * 512, (tb + 1) * 512)
        for kc in range(8):
            op("act", lambda a: a.activation(out=sq[:, kc, :], in_=xT[:, kc, ts], func=AF.Square),
               r=[xres[kc][tb]], w=[sqres[kc]])

    def prenorm_b(l, tb, a_off, sh_off):
        mod, der, modres, derres = modA[l % 2], derA[l % 2], modresA[l % 2], derresA[l % 2]
        ts = slice(tb * 512, (tb + 1) * 512)
        for kc in range(8):
            op("pe", lambda p: p.matmul(ps[6][:], lhsT=ones_b[:], rhs=sq[:, kc, :], start=(kc == 0), stop=(kc == 7)),
               r=[sqres[kc], constres], w=[bank[6]], mark=(kc == 7))
        rstd_from(6, 1.0 / D)
        for kc in range(8):
            tt, ttr = gettmp()
            op("dve", lambda v: v.tensor_tensor(out=tt[:], in0=xT[:, kc, ts], in1=rstd_t[:], op=ALU.mult),
               r=[xres[kc][tb], rstdres], w=[ttr])
            op("act", lambda a: a.activation(out=hT[:, kc, ts], in_=tt[:], func=AF.Identity,
                                             bias=mod[:, sh_off + kc: sh_off + kc + 1], scale=der[:, a_off + kc: a_off + kc + 1]),
               r=[ttr, modres, derres], w=[hres[kc][tb]])

    def postnorm_residual(l, ys, ysr, tb, g_off, extra_r=()):
        der, derres = derA[l % 2], derresA[l % 2]
        ts = slice(tb * 512, (tb + 1) * 512)
        for kc in range(8):
            op("pe", lambda p: p.matmul(ps[6][:], lhsT=ones_b[:], rhs=sq[:, kc, :], start=(kc == 0), stop=(kc == 7)),
               r=[sqres[kc], constres], w=[bank[6]], mark=(kc == 7))
        rstd_from(6, 1.0 / D)
        for kc in range(8):
            tt, ttr = gettmp()
            op("dve", lambda v: v.scalar_tensor_tensor(out=tt[:], in0=ys[:, kc, :], scalar=der[:, g_off + kc: g_off + kc + 1],
                                                       in1=rstd_t[:], op0=ALU.mult, op1=ALU.mult),
               r=(ysr[kc] if isinstance(ysr[kc], list) else [ysr[kc]]) + [derres, rstdres] + list(extra_r), w=[ttr])
            op("dve", lambda v: v.tensor_tensor(out=xT[:, kc, ts], in0=xT[:, kc, ts], in1=tt[:], op=ALU.add),
               r=[ttr, xres[kc][tb]], w=[xres[kc][tb]])

    brr = [0]

    def nextbank(n=6):
        brr[0] = (brr[0] + 1) % n
        return brr[0]

    def proj_fm(W, Wr, col0, tb, bk):
        ts = slice(tb * 512, (tb + 1) * 512)
        for kc in range(8):
            op("pe", lambda p: p.matmul(ps[bk][:], lhsT=W[:, kc * 512 + col0: kc * 512 + col0 + 128], rhs=hT[:, kc, ts],
                                        start=(kc == 0), stop=(kc == 7)),
               r=[Wr, hres[kc][tb]], w=[bank[bk]], mark=(kc == 7))

    def inproj(l):
        op("dve", lambda v: v.memset(va[:, :, :, 64:65], 1.0), w=vres)
        for tb in range(4):
            ts = slice(tb * 512, (tb + 1) * 512)
            for kc in range(8):
                op("pe", lambda p: p.matmul(ps[6][0:8, :], lhsT=wf[:, l * 64 + kc * 8: l * 64 + kc * 8 + 8], rhs=hT[:, kc, ts],
                                            start=(kc == 0), stop=(kc == 7)),
                   r=[constres, hres[kc][tb]], w=[bank[6]], mark=(kc == 7))
            op("act", lambda a: a.activation(out=fT[0:8, ts], in_=ps[6][0:8, :], func=AF.Exp, bias=nbfA[l % 2][0:8, 0:1], scale=-1.0),
               r=[bank[6], derresA[l % 2]], w=[fres] + sqres)
        op("act", lambda a: a.activation(out=fT[:], in_=fT[:], func=AF.Ln, bias=1.0, scale=1.0), r=[fres], w=[fres])

        for which in range(2):
            W, Wr = wnext()
            dst, dres = (qT, qres) if which == 0 else (kT, kres)
            for fc in range(4):
                for tb in range(4):
                    bk = nextbank()
                    proj_fm(W, Wr, fc * 128, tb, bk)
                    evac(dst[:, fc, tb * 512:(tb + 1) * 512], ps[bk][:], [bank[bk]], [dres[fc][tb]],
                         scale=(0.125 if which == 0 else None))
            if which == 0:
                fchain()
                op("dve", lambda v: v.memset(uT[:, :, 0:30], 0.0), w=[ures[fc][0] for fc in range(4)] + [splres])
        W, Wr = wnext()
        for tk in range(16):
            bk = nextbank()
            for kc in range(8):
                op("pe", lambda p: p.matmul(ps[bk][:], lhsT=hT[:, kc, tk * 128:(tk + 1) * 128], rhs=W[:, kc * 512:(kc + 1) * 512],
                                            start=(kc == 0), stop=(kc == 7)),
                   r=[Wr, hres[kc][tk // 4]], w=[bank[bk]], mark=(kc == 7))
            evac(va[:, tk, :, 0:64], ps[bk][:].rearrange("p (h d) -> p h d", h=NH), [bank[bk]], [vres[tk]])
        for half in range(2):
            W, Wr = wnext()
            for c2 in range(2):
                fc = half * 2 + c2
                for tb in range(4):
                    b0 = nextbank()
                    proj_fm(W, Wr, c2 * 128, tb, b0)
                    b1 = nextbank()
                    proj_fm(W, Wr, 256 + c2 * 128, tb, b1)
                    tt, ttr = gettmp()
                    op("act", lambda a: a.activation(out=tt[:], in_=ps[b1][:], func=AF.Sigmoid), r=[bank[b1]], w=[ttr])
                    op("dve", lambda v: v.tensor_tensor(out=uT[:, fc, 30 + tb * 512: 30 + (tb + 1) * 512], in0=ps[b0][:], in1=tt[:],
                                                        op=ALU.mult), r=[bank[b0], ttr], w=[ures[fc][tb]])
    def fchain():
        op("dve", lambda v: v.tensor_tensor_scan(out=cumT[:], data0=fT[:], data1=fT[:], initial=0.0, op0=ALU.add, op1=ALU.max),
           r=[fres] + tmpres, w=[cumTres] + tmpres)
        op("dve", lambda v: v.tensor_copy(out=spl[:, 0:T], in_=cumT[:]), r=[cumTres], w=[splres])
        op("dve", lambda v: v.tensor_tensor(out=fT[:], in0=cumT[:], in1=spl[:, 0:T], op=ALU.subtract), r=[cumTres, splres], w=[fres])
        op("dve", lambda v: v.tensor_copy(out=spl[:, T:2 * T], in_=fT[:]), r=[fres], w=[splres])
        op("dve", lambda v: v.tensor_tensor(out=cumT[:], in0=fT[:], in1=spl[:, T:2 * T], op=ALU.subtract), r=[fres, splres], w=[cumTres])
        op("dve", lambda v: v.tensor_copy(out=spl[:, 2 * T:3 * T], in_=cumT[:]), r=[cumTres], w=[splres])
        tr.dma("sp", out=scr_d[:, :], in_=spl[:], r=[splres], w=[scrres])

    def attention():
        for b in range(2):
            extra = [] if b == 0 else ([fres] + sqres)
            op("dve", lambda v: v.memset(cumq[b][:], 1.0), w=[cumres[b]] + extra)
            op("dve", lambda v: v.memset(cumk[b][:], 0.0), w=[cumres[b]] + extra)
            op("dve", lambda v: v.memset(cumk[b][0:3, :], -1.0), w=[cumres[b]])
            op("dve", lambda v: v.memset(cumk[b][64:67, :], -1.0), w=[cumres[b]])
        def cum_load(c):
            b = c % 2
            for hx in range(2):
                h = 2 * c + hx
                src = scr_d[h:h + 1, :].rearrange("o (r t) -> (o r) t", r=3)
                tr.dma("sp", out=cumq[b][64 * hx:64 * hx + 3, :], in_=src, r=[scrres], w=[cumres[b]])
                tr.dma("sp", out=cumk[b][64 * hx + 3:64 * hx + 6, :], in_=src, r=[scrres], w=[cumres[b]])

        cnt = 0
        ocnt = 0
        pending = [None]
        for c in range(4):
            fc = c
            b = c % 2
            if c == 0:
                cum_load(0)
            for i in range(4):
                nj = 4 * i + 4
                if i == 3 and c < 3:
                    cum_load(c + 1)
                pbuf = ocnt % 2
                ob0 = 4 + 2 * pbuf
                ocnt += 1

                def s_tile(j):
                    nonlocal cnt
                    t = cnt % 2
                    cnt += 1
                    c0 = max(0, j - 4 * i) * 128
                    qs = slice(i * 512 + c0, (i + 1) * 512)
                    ks = slice(j * 128, (j + 1) * 128)
                    diagonal = j >= 4 * i
                    bks = [bank[2 * t], bank[2 * t + 1]]
                    for hx in range(2):
                        pr = 64 * hx
                        op("pe", lambda p: p.matmul(ps[2 * t + hx][:, c0:512], lhsT=kT[pr:pr + 64, fc, ks], rhs=qT[pr:pr + 64, fc, qs],
                                                    start=True, stop=False),
                           r=[kres[fc][j // 4], qres[fc][i]], w=[bks[hx]])
                    if diagonal:
                        for hx in range(2):
                            op("pe", lambda p: p.matmul(ps[2 * t + hx][:, c0:c0 + 128], lhsT=ident[:], rhs=maskb[:], start=False, stop=False),
                               r=[constres], w=[bks[hx]])
                    for hx in range(2):
                        pr = 64 * hx
                        op("pe", lambda p: p.matmul(ps[2 * t + hx][:, c0:512], lhsT=cumk[b][pr:pr + 64, ks], rhs=cumq[b][pr:pr + 64, qs],
                                                    start=False, stop=True),
                           r=[cumres[b]], w=[bks[hx]], mark=(hx == 1))
                    op("act", lambda a: a.activation(out=ptb2[t][:, :, c0:512], in_=pp[t][:, :, c0:512], func=AF.Exp),
                       r=bks, w=[ptres[t]])
                    return (j, c0, t)

                def pv_tile(tl):
                    j, c0, t = tl
                    for hx in range(2):
                        hh = 2 * c + hx
                        mw = 128 if hh < 7 else 65
                        op("pe", lambda p: p.matmul(ps[ob0 + hx][0:mw, c0:512], lhsT=va[:, j, :, :].rearrange("p h d -> p (h d)")[:, hh * 65:hh * 65 + mw],
                                                    rhs=ptb2[t][:, hx, c0:512],
                                                    start=(j == 0), stop=(j == nj - 1)),
                           r=[ptres[t], vres[j]], w=[bank[ob0 + hx]], mark=(j == nj - 1 and hx == 1))

                prev = s_tile(0)
                if pending[0] is not None:
                    pending[0]()
                    pending[0] = None
                for j in range(1, nj):
                    curt = s_tile(j)
                    pv_tile(prev)
                    prev = curt
                pv_tile(prev)
                for hx in range(2):
                    ob = ob0 + hx
                    k = (2 * ocnt + hx) % 8
                    rd, rdr = rden2[pbuf], rdres2[pbuf][hx]
                    op("dve", lambda v: v.reciprocal(out=rd[64:65, hx, :], in_=ps[ob][64:65, :]), r=[bank[ob]], w=[rdr])
                    tr.dma("sp", out=scrb_d[k:k + 1, :], in_=rd[64:65, hx, :], r=[rdr], w=[scrbres[k]])
                    tr.dma("sp", out=bcs[pbuf][0:64, hx, :], in_=scrb_d[k:k + 1, :].broadcast_to([64, 512]), r=[scrbres[k]], w=[bcres[pbuf][hx]])

                def fin(fc=fc, i=i, pbuf=pbuf, ob0=ob0):
                    for hx in range(2):
                        pr = 64 * hx
                        op("dve", lambda v: v.tensor_tensor(out=qT[pr:pr + 64, fc, i * 512:(i + 1) * 512], in0=ps[ob0 + hx][0:64, :],
                                                            in1=bcs[pbuf][0:64, hx, :], op=ALU.mult),
                           r=[bank[ob0 + hx], bcres[pbuf][hx]], w=[qres[fc][i]])

                pending[0] = fin
        pending[0]()

    def conv(l):
        pb = l * NPAR
        for fc in range(4):
            op("dve", lambda v: v.tensor_tensor(out=diag[:, fc, :, :],
                                                in0=ident[:].unsqueeze(1).broadcast_to([128, CK, 128]),
                                                in1=par[:, pb + P_CONVW + fc * CK: pb + P_CONVW + (fc + 1) * CK].unsqueeze(2).broadcast_to([128, CK, 128]),
                                                op=ALU.mult), r=[constres, parres], w=[diagres[fc]])
        for tb in (3, 2, 1, 0):
            for fc in range(4):
                rr = [ures[fc][tb], diagres[fc]] + ([ures[fc][tb - 1]] if tb > 0 else [])
                for k in range(CK):
                    op("pe", lambda p: p.matmul(ps[fc][:], lhsT=diag[:, fc, k, :], rhs=uT[:, fc, tb * 512 + k: tb * 512 + k + 512],
                                                start=(k == 0), stop=(k == CK - 1)), r=rr, w=[bank[fc]], mark=(k == CK - 1))
            for fc in range(4):
                op("act", lambda a: a.activation(out=ystage[:, fc, :], in_=ps[fc][:], func=AF.Identity,
                                                 bias=par[:, pb + P_CONVB + fc: pb + P_CONVB + fc + 1], scale=1.0),
                   r=[bank[fc], parres], w=[ysres[fc]])
                op("dve", lambda v: v.tensor_copy(out=sq[:, fc, :], in_=ystage[:, fc, :]), r=[ysres[fc]], w=[sqres[fc]])
                op("act", lambda a: a.activation(out=sq[:, 4 + fc, :], in_=ystage[:, fc, :], func=AF.Square),
                   r=[ysres[fc]], w=[sqres[4 + fc]])
            for fc in range(4):
                op("pe", lambda p: p.matmul(ps[4][:], lhsT=ones_b[:], rhs=sq[:, fc, :], start=(fc == 0), stop=(fc == 3)),
                   r=[sqres[fc], constres], w=[bank[4]], mark=(fc == 3))
            for fc in range(4):
                op("pe", lambda p: p.matmul(ps[5][:], lhsT=ones_b[:], rhs=sq[:, 4 + fc, :], start=(fc == 0), stop=(fc == 3)),
                   r=[sqres[4 + fc], constres], w=[bank[5]], mark=(fc == 3))
            mean, meanr = tmp[0], tmpres[0]
            var, varr = tmp[1], tmpres[1]
            op("dve", lambda v: v.tensor_scalar(out=mean[:], in0=ps[4][:], scalar1=1.0 / 512, scalar2=None, op0=ALU.mult),
               r=[bank[4]], w=[meanr])
            op("dve", lambda v: v.tensor_tensor(out=var[:], in0=mean[:], in1=mean[:], op=ALU.mult), r=[meanr], w=[varr])
            op("dve", lambda v: v.scalar_tensor_tensor(out=var[:], in0=ps[5][:], scalar=1.0 / 512, in1=var[:], op0=ALU.mult, op1=ALU.subtract),
               r=[bank[5], varr], w=[varr])
            op("act", lambda a: a.activation(out=rstd_t[:], in_=var[:], func=AF.Ln, bias=EPS, scale=1.0), r=[varr], w=[rstdres])
            op("act", lambda a: a.activation(out=rstd_t[:], in_=rstd_t[:], func=AF.Exp, scale=-0.5), r=[rstdres], w=[rstdres])
            for fc in range(4):
                op("dve", lambda v: v.tensor_tensor(out=ystage[:, fc, :], in0=ystage[:, fc, :], in1=mean[:], op=ALU.subtract),
                   r=[ysres[fc], meanr], w=[ysres[fc]])
                op("dve", lambda v: v.tensor_tensor(out=ystage[:, fc, :], in0=ystage[:, fc, :], in1=rstd_t[:], op=ALU.mult),
                   r=[ysres[fc], rstdres], w=[ysres[fc]])
                op("act", lambda a: a.activation(out=uT[:, fc, 30 + tb * 512: 30 + (tb + 1) * 512], in_=ystage[:, fc, :], func=AF.Silu,
                                                 bias=par[:, pb + P_LNB + fc: pb + P_LNB + fc + 1],
                                                 scale=par[:, pb + P_LNG + fc: pb + P_LNG + fc + 1]),
                   r=[ysres[fc], parres], w=[ures[fc][tb]])

    def wo(l):
        W0, W0r = wnext()
        W1, W1r = wnext(ahead=2)

        def mm(tb, oc):
            ts = slice(tb * 512, (tb + 1) * 512)
            us = slice(30 + tb * 512, 30 + (tb + 1) * 512)
            W, Wr = (W0, W0r) if oc < 4 else (W1, W1r)
            c0 = (oc % 4) * 128
            bk = nextbank(4)
            for kc in range(8):
                if kc < 4:
                    rhs, rr = qT[:, kc, ts], qres[kc][tb]
                else:
                    rhs, rr = uT[:, kc - 4, us], ures[kc - 4][tb]
                op("pe", lambda p: p.matmul(ps[bk][:], lhsT=W[:, kc * 512 + c0: kc * 512 + c0 + 128], rhs=rhs,
                                            start=(kc == 0), stop=(kc == 7)), r=[Wr, rr], w=[bank[bk]], mark=(kc == 7))
            return bk

        def ev(oc, bk):
            op("dve", lambda v: v.tensor_copy(out=ystage[:, oc, :], in_=ps[bk][:]), r=[bank[bk]], w=[ysres[oc]])
            op("act", lambda a: a.activation(out=sq[:, oc, :], in_=ystage[:, oc, :], func=AF.Square), r=[ysres[oc]], w=[sqres[oc]])

        for oc in range(8):
            ev(oc, mm(0, oc))
        for tb in range(4):
            ahead = [mm(tb + 1, oc) for oc in range(4)] if tb < 3 else []
            postnorm_residual(l, ystage, ysres, tb, 8)
            prenorm_a(tb)
            prenorm_b(l, tb, 16, 24)
            if tb < 3:
                for oc in range(4):
                    ev(oc, ahead[oc])
                for oc in range(4, 8):
                    ev(oc, mm(tb + 1, oc))

    def ffn(l):
        nxt = l + 1 < NL
        for half in range(2):
            fcnt = 0
            for s in range(11):
                W, Wr = wnext()
                for c2 in range(2):
                    hc = 2 * s + c2
                    for tbl in range(2):
                        tb = half * 2 + tbl
                        bg = fcnt % 2
                        bu = 2 + fcnt % 2
                        fcnt += 1
                        proj_fm(W, Wr, c2 * 128, tb, bg)
                        proj_fm(W, Wr, 256 + c2 * 128, tb, bu)
                        tt, ttr = gettmp()
                        op("act", lambda a: a.activation(out=tt[:], in_=ps[bg][:], func=AF.Silu), r=[bank[bg]], w=[ttr])
                        op("dve", lambda v: v.tensor_tensor(out=hid[:, hc, tbl * 512:(tbl + 1) * 512], in0=ps[bu][:], in1=tt[:], op=ALU.mult),
                           r=[bank[bu], ttr], w=[hidres[hc][tbl]])
                if half == 0 and nxt:
                    ada_tile(l + 1, s)
                    if s == 10:
                        ada_tile(l + 1, 11)
                        ada_fin(l + 1)
            ys1 = hF32[:, :, half * 512:(half + 1) * 512]
            ys1res = [[hres[oc][half * 2], hres[oc][half * 2 + 1]] for oc in range(8)]
            for oc in range(8):
                W, Wr = wnext()
                pair = ((0, 1), (2, 3), (4, 5))[oc % 3]
                for tbl in range(2):
                    bk = pair[tbl]
                    for hc in range(NHC):
                        op("pe", lambda p: p.matmul(ps[bk][:], lhsT=W[:, hc * 128:(hc + 1) * 128], rhs=hid[:, hc, tbl * 512:(tbl + 1) * 512],
                                                    start=(hc == 0), stop=(hc == NHC - 1)),
                           r=[Wr, hidres[hc][tbl]], w=[bank[bk]], mark=(hc == NHC - 1))
                evac(ys0[:, oc, :], ps[pair[0]][:], [bank[pair[0]]], [ysres[oc]])
                evac(ys1[:, oc, :], ps[pair[1]][:], [bank[pair[1]]], ys1res[oc])
                if half == 1 and nxt:
                    if oc == 0:
                        prenorm_a(0)
                    elif oc == 2:
                        prenorm_b(l + 1, 0, 0, 0)
                    elif oc == 4:
                        prenorm_a(1)
                    elif oc == 6:
                        prenorm_b(l + 1, 1, 0, 0)
            for kc in range(8):
                op("act", lambda a: a.activation(out=sq[:, kc, :], in_=ys0[:, kc, :], func=AF.Square), r=[ysres[kc]], w=[sqres[kc]])
            postnorm_residual(l, ys0, ysres, half * 2, 24)
            for kc in range(8):
                op("act", lambda a: a.activation(out=sq[:, kc, :], in_=ys1[:, kc, :], func=AF.Square), r=ys1res[kc], w=[sqres[kc]])
            postnorm_residual(l, ys1, ys1res, half * 2 + 1, 24)
        if nxt:
            for tb in (2, 3):
                prenorm_a(tb)
                prenorm_b(l + 1, tb, 0, 0)

    for t in range(12):
        ada_tile(0, t)
    ada_fin(0)
    for tb in range(4):
        prenorm_a(tb)
        prenorm_b(0, tb, 0, 0)
    for l in range(NL):
        tr.barrier()
        inproj(l)
        tr.barrier()
        attention()
        tr.barrier()
        conv(l)
        tr.barrier()
        wo(l)
        tr.barrier()
        ffn(l)
    tr.barrier()
    for kc in range(8):
        tr.dma("sp", out=out_d[kc * 128:(kc + 1) * 128, :], in_=xT[:, kc, :], r=xres[kc])
    tr.wait_all_dma("sp")
    return nc


def _kc_tile(w2d, cols):
    sub = w2d[:, cols]
    n = sub.shape[1]
    return np.ascontiguousarray(sub.reshape(8, 128, n).transpose(1, 0, 2).reshape(128, 8 * n))


def prep_weights(NL, w_in, b_f, conv_w, conv_b, conv_ln_g, conv_ln_b, w_o, w_ffn_in, w_ffn_out,
                 mix_pre_g, mix_post_g, ffn_pre_g, ffn_post_g, ada_w, ada_b):
    f32 = np.float32
    wA = np.empty((NL * 12, 128, 4096), f32)
    wIN = np.empty((NL * 5, 128, 4096), f32)
    wO = np.empty((NL * 2, 128, 4096), f32)
    wFI = np.empty((NL * 11, 128, 4096), f32)
    wFO = np.empty((NL * 8, 128, DFF), f32)
    par = np.zeros((128, NL * NPAR), f32)
    wf = np.empty((128, NL * 64), f32)
    ar = np.arange
    for l in range(NL):
        for t in range(12):
            wA[l * 12 + t] = _kc_tile(ada_w[l], ar(t * 512, (t + 1) * 512))
        wi = w_in[l]
        wIN[l * 5 + 0] = _kc_tile(wi, ar(0, 512))
        wIN[l * 5 + 1] = _kc_tile(wi, ar(512, 1024))
        wIN[l * 5 + 2] = _kc_tile(wi, ar(1024, 1536))
        for half in range(2):
            cols = np.concatenate([ar(1544 + half * 256, 1544 + half * 256 + 256), ar(2056 + half * 256, 2056 + half * 256 + 256)])
            wIN[l * 5 + 3 + half] = _kc_tile(wi, cols)
        wf[:, l * 64:(l + 1) * 64] = _kc_tile(wi, ar(1536, 1544))
        for t in range(2):
            wO[l * 2 + t] = _kc_tile(w_o[l], ar(t * 512, (t + 1) * 512))
        for s in range(11):
            cols = np.concatenate([ar(s * 256, s * 256 + 256), ar(DFF + s * 256, DFF + s * 256 + 256)])
            wFI[l * 11 + s] = _kc_tile(w_ffn_in[l], cols)
        wo3 = w_ffn_out[l].reshape(NHC, 128, 8, 128)
        wFO[l * 8:(l + 1) * 8] = wo3.transpose(2, 1, 0, 3).reshape(8, 128, DFF)
        pb = l * NPAR

        def fm(v):
            return v.reshape(8, 128).T

        par[:, pb + P_PRE1: pb + P_PRE1 + 8] = fm(mix_pre_g[l])
        par[:, pb + P_POST1: pb + P_POST1 + 8] = fm(mix_post_g[l])
        par[:, pb + P_PRE2: pb + P_PRE2 + 8] = fm(ffn_pre_g[l])
        par[:, pb + P_POST2: pb + P_POST2 + 8] = fm(ffn_post_g[l])
        par[:, pb + P_ADAB: pb + P_ADAB + 48] = ada_b[l].reshape(48, 128).T
        par[:, pb + P_CONVW: pb + P_CONVW + 4 * CK] = conv_w[l].reshape(CK, 4, 128).transpose(2, 1, 0).reshape(128, 4 * CK)
        par[:, pb + P_CONVB: pb + P_CONVB + 4] = conv_b[l].reshape(4, 128).T
        par[:, pb + P_LNG: pb + P_LNG + 4] = conv_ln_g[l].reshape(4, 128).T
        par[:, pb + P_LNB: pb + P_LNB + 4] = conv_ln_b[l].reshape(4, 128).T
        par[0:8, pb + P_BF] = b_f[l]
    return dict(wA=wA, wIN=wIN, wO=wO, wFI=wFI, wFO=wFO, par=par, wf=wf)


_NC_CACHE = {}


def run_layers(x, c, NL, weights, cores=None):
    B = x.shape[0]
    if cores is None:
        cores = list(range(B))
    if NL not in _NC_CACHE:
        _NC_CACHE[NL] = build(NL)
    nc = _NC_CACHE[NL]
    in_maps = []
    for b in cores:
        m = dict(weights)
        m["xT"] = np.ascontiguousarray(x[b].T)
        m["cT"] = np.ascontiguousarray(c[b].reshape(8, 128).T)
        in_maps.append(m)
    res = run_bass_kernel_spmd(nc, in_maps, core_ids=list(range(len(cores))))
    out = np.empty((len(cores), T, D), np.float32)
    for i in range(len(cores)):
        out[i] = res.results[i]["outT"].T
    return out


def kernel(x, c, w_in, b_f, conv_w, conv_b, conv_ln_g, conv_ln_b, w_o, w_ffn_in, w_ffn_out,
           mix_pre_g, mix_post_g, ffn_pre_g, ffn_post_g, ada_w, ada_b):
    a = [np.asarray(v, dtype=np.float32) for v in (w_in, b_f, conv_w, conv_b, conv_ln_g, conv_ln_b, w_o, w_ffn_in, w_ffn_out,
                                                   mix_pre_g, mix_post_g, ffn_pre_g, ffn_post_g, ada_w, ada_b)]
    weights = prep_weights(L_FULL, *a)
    x = np.asarray(x, dtype=np.float32)
    c = np.asarray(c, dtype=np.float32)
    return run_layers(x, c, L_FULL, weights)
```

```python
import bisect
import os
import numpy as np
import concourse.bass as bass
import concourse.mybir as mybir
from concourse.bass_utils import run_bass_kernel_spmd

F32 = mybir.dt.float32
BF16 = mybir.dt.bfloat16
U8 = mybir.dt.uint8
AF = mybir.ActivationFunctionType
ALU = mybir.AluOpType

D = 1024
T = 2048
L_FULL = 4
NH = 8
DFF = 2816
NHC = DFF // 128
CK = 31
EPS = 1e-6
NPAR = 224
P_PRE1, P_POST1, P_PRE2, P_POST2, P_ADAB, P_CONVW, P_CONVB, P_LNG, P_LNB, P_BF = 0, 8, 16, 24, 32, 80, 204, 208, 212, 216


class Res:
    __slots__ = ("w", "r")

    def __init__(self):
        self.w = None
        self.r = {}


class Tracker:
    def __init__(self, nc):
        self.nc = nc
        self.E = {}
        for name, h, am in (("pe", nc.tensor, False), ("act", nc.scalar, True), ("dve", nc.vector, True),
                            ("pool", nc.gpsimd, True), ("sp", nc.sync, True)):
            self.E[name] = dict(h=h, sem=nc.alloc_semaphore("s_" + name), am=am, n=0, marks=[], last=None, seen={})
        self.dsems = {"sp": [[nc.alloc_semaphore("dsp%d" % i), 0] for i in range(10)],
                      "pool": [[nc.alloc_semaphore("dpl%d" % i), 0] for i in range(6)]}
        self.drr = {"sp": 0, "pool": 0}
        self.nwait = 0

    def resolve(self, tok):
        if tok[0] == "d":
            return tok[1], tok[2]
        e = self.E[tok[1]]
        idx = tok[2]
        if e["am"]:
            return e["sem"], idx + 1
        marks = e["marks"]
        k = bisect.bisect_left(marks, idx)
        if k == len(marks):
            e["last"].then_inc(e["sem"], 1)
            marks.append(e["n"] - 1)
        return e["sem"], k + 1

    def _deps(self, eng, r, w, is_dma):
        need = {}
        e = self.E[eng]

        def add(tok, raw):
            if tok is None:
                return
            if tok[0] == "e" and tok[1] == eng and not is_dma:
                if eng == "pe":
                    return
            sem, val = self.resolve(tok)
            if e["seen"].get(sem.num, 0) >= val:
                return
            if need.get(sem.num, (None, 0))[1] < val:
                need[sem.num] = (sem, val)

        for res in r:
            add(res.w, True)
        for res in w:
            add(res.w, False)
            for tok in res.r.values():
                add(tok, False)
        return need

    def _emit(self, e, need, fn):
        items = list(need.values())
        for sem, val in items[:-1]:
            e["h"].wait_ge(sem, val)
            self.nwait += 1
        ins = fn(e["h"])
        if items:
            ins._wait_ge(*items[-1])
        for sem, val in items:
            e["seen"][sem.num] = val
        return ins

    def op(self, eng, fn, r=(), w=(), mark=False):
        e = self.E[eng]
        need = self._deps(eng, r, w, False)
        ins = self._emit(e, need, fn)
        idx = e["n"]
        e["n"] += 1
        e["last"] = ins
        if e["am"] or mark:
            ins.then_inc(e["sem"], 1)
            e["marks"].append(idx)
        tok = ("e", eng, idx)
        for res in r:
            res.r[eng] = tok
        for res in w:
            res.w = tok
            res.r = {}
        return ins

    def dma(self, q, out, in_, r=(), w=()):
        e = self.E[q]
        pool = self.dsems[q]
        i = self.drr[q]
        self.drr[q] = (i + 1) % len(pool)
        S = pool[i]
        need = self._deps(q, r, w, True)
        if S[1] > 0 and e["seen"].get(S[0].num, 0) < S[1]:
            if need.get(S[0].num, (None, 0))[1] < S[1]:
                need[S[0].num] = (S[0], S[1])
        ins = self._emit(e, need, lambda h: h.dma_start(out=out, in_=in_))
        ins.then_inc(S[0], 16)
        S[1] += 16
        tok = ("d", S[0], S[1])
        key = "dma_" + q + str(i)
        for res in r:
            res.r[key] = tok
        for res in w:
            res.w = tok
            res.r = {}
        return tok

    def barrier(self, engs=("pe", "act", "dve")):
        toks = {}
        for x in engs:
            if self.E[x]["n"] > 0:
                toks[x] = self.resolve(("e", x, self.E[x]["n"] - 1))
        for x in engs:
            e = self.E[x]
            for y, (sem, val) in toks.items():
                if y == x:
                    continue
                if e["seen"].get(sem.num, 0) < val:
                    e["h"].wait_ge(sem, val)
                    e["seen"][sem.num] = val

    def wait_all_dma(self, q):
        e = self.E[q]
        for S in self.dsems["sp"] + self.dsems["pool"]:
            if S[1] > 0:
                e["h"].wait_ge(S[0], S[1])


def build(NL, dbg=False, stop=99):
    nc = bass.Bass("TRN2", target_bir_lowering=False)
    tr = Tracker(nc)

    xT_d = nc.dram_tensor("xT", [D, T], F32, kind="ExternalInput").ap()
    cT_d = nc.dram_tensor("cT", [128, 8], F32, kind="ExternalInput").ap()
    par_d = nc.dram_tensor("par", [128, NL * NPAR], F32, kind="ExternalInput").ap()
    wf_d = nc.dram_tensor("wf", [128, NL * 64], F32, kind="ExternalInput").ap()
    wA_d = nc.dram_tensor("wA", [NL * 12, 128, 4096], F32, kind="ExternalInput").ap()
    wIN_d = nc.dram_tensor("wIN", [NL * 5, 128, 4096], F32, kind="ExternalInput").ap()
    wO_d = nc.dram_tensor("wO", [NL * 2, 128, 4096], F32, kind="ExternalInput").ap()
    wFI_d = nc.dram_tensor("wFI", [NL * 11, 128, 4096], F32, kind="ExternalInput").ap()
    wFO_d = nc.dram_tensor("wFO", [NL * 8, 128, DFF], F32, kind="ExternalInput").ap()
    out_d = nc.dram_tensor("outT", [D, T], F32, kind="ExternalOutput").ap()
    scr_d = nc.dram_tensor("scr", [8, 3 * T], BF16, kind="Internal").ap()
    scrb_d = nc.dram_tensor("scrb", [8, 512], F32, kind="Internal").ap()

    ARENA = 212736
    arena = nc.alloc_sbuf_tensor("arena", [128, ARENA], U8)
    base = nc.lookup_mloc(arena).addr
    cur = [base]

    def at(name, shape, dt, off=None):
        nb = int(np.prod(shape[1:])) * (4 if dt == F32 else 2)
        if off is None:
            off = cur[0]
            cur[0] += (nb + 31) // 32 * 32
            assert cur[0] <= base + ARENA, (name, cur[0] - base)
        return nc.alloc_sbuf_tensor_at(name, list(shape), dt, offset=off)

    xT = at("xTs", [128, 8, T], F32)
    h_off = cur[0]
    hT = at("hT", [128, 8, T], BF16)
    r3_off = cur[0]
    qT = at("qT", [128, 4, T], BF16)
    k_off = cur[0]
    kT = at("kT", [128, 4, T], BF16)
    va = at("va", [128, 16, NH, 65], BF16)
    UW = T + 32
    u_off = cur[0]
    uT = at("uT", [128, 4, UW], BF16)
    r3_end = cur[0]
    sq_off = cur[0]
    sq = at("sq", [128, 8, 512], BF16)
    ring = [at("ring%d" % i, [128, 4096], BF16) for i in range(3)]
    tmp_off = cur[0]
    tmp = [at("tmp%d" % i, [128, 512], F32) for i in range(4)]
    rstd_t = tmp[3]
    ident = at("ident", [128, 128], BF16)
    ones_b = at("ones_b", [128, 128], BF16)
    ones_f = at("ones_f", [128, 64], F32)
    maskb = at("maskb", [128, 128], BF16)
    par = at("par_s", [128, NL * NPAR], F32)
    wf = at("wf_s", [128, NL * 64], BF16)
    cTs = at("cTs", [128, 8], F32)
    cact = at("cact", [128, 8], BF16)
    modA = [at("mod%d" % i, [128, 48], F32) for i in range(2)]
    derA = [at("der%d" % i, [128, 32], F32) for i in range(2)]
    nbfA = [at("nbf%d" % i, [128, 1], F32) for i in range(2)]
    hid = at("hid", [128, NHC, 1024], BF16, off=r3_off)
    ys0 = at("ys0", [128, 8, 512], F32, off=r3_off + NHC * 1024 * 2)
    assert r3_off + NHC * 1024 * 2 + 8 * 512 * 4 <= r3_end
    ystage = at("ystage", [128, 8, 512], F32, off=k_off)
    hF32 = at("hF32", [128, 8, 1024], F32, off=h_off)
    diag = at("diag", [128, 4, CK, 128], BF16, off=h_off)
    ptb2 = [at("ptb2_%d" % i, [128, 2, 512], BF16, off=h_off + i * 2048) for i in range(2)]
    rden2 = [at("rden%d" % i, [128, 2, 512], F32, off=h_off + 4096 + i * 4096) for i in range(2)]
    bcs = [at("bcs%d" % i, [128, 2, 512], F32, off=h_off + 12288 + i * 4096) for i in range(2)]
    cumq = [at("cumq0", [128, T], BF16, off=h_off + 20480), at("cumq1", [128, T], BF16, off=sq_off)]
    cumk = [at("cumk0", [128, T], BF16, off=h_off + 24576), at("cumk1", [128, T], BF16, off=sq_off + 4096)]
    spl = at("spl", [8, 3 * T], BF16, off=u_off)
    fT = at("fT", [8, T], F32, off=sq_off)
    cumT = at("cumT", [8, T], F32, off=tmp_off)

    pp = [nc.alloc_psum_tensor("pp%d" % i, [128, 2, 512], F32) for i in range(4)]
    ps = [pp[i // 2][:, i % 2, :] for i in range(8)]
    bank = [Res() for _ in range(8)]

    xres = [[Res() for _ in range(4)] for _ in range(8)]
    hres = [[Res() for _ in range(4)] for _ in range(8)]
    qres = [[Res() for _ in range(4)] for _ in range(4)]
    kres = [[Res() for _ in range(4)] for _ in range(4)]
    vres = [Res() for _ in range(16)]
    ures = [[Res() for _ in range(4)] for _ in range(4)]
    sqres = [Res() for _ in range(8)]
    ringres = [Res() for _ in range(3)]
    tmpres = [Res() for _ in range(4)]
    rstdres = Res()
    constres = Res()
    parres = Res()
    modresA = [Res(), Res()]
    derresA = [Res(), Res()]
    ysres = [Res() for _ in range(8)]
    ptres = [Res() for _ in range(2)]
    rdres2 = [[Res(), Res()], [Res(), Res()]]
    bcres = [[Res(), Res()], [Res(), Res()]]
    scrbres = [Res() for _ in range(8)]
    cumres = [Res(), Res()]
    splres = Res()
    fres = Res()
    cumTres = Res()
    scrres = Res()
    hidres = [[Res() for _ in range(2)] for _ in range(NHC)]
    diagres = [Res() for _ in range(4)]
    recres = Res()
    miscres = Res()

    op = tr.op

    sched = []
    for t in range(12):
        sched.append((wA_d[t], 4096))
    for l in range(NL):
        for t in range(5):
            sched.append((wIN_d[l * 5 + t], 4096))
        for t in range(2):
            sched.append((wO_d[l * 2 + t], 4096))
        for half in range(2):
            for t in range(11):
                sched.append((wFI_d[l * 11 + t], 4096))
                if half == 0 and l + 1 < NL:
                    sched.append((wA_d[(l + 1) * 12 + t], 4096))
                    if t == 10:
                        sched.append((wA_d[(l + 1) * 12 + 11], 4096))
            for t in range(8):
                sched.append((wFO_d[l * 8 + t], DFF))
    wstate = dict(issued=0, used=0)

    def wissue(upto):
        while wstate["issued"] < min(upto, len(sched)):
            i = wstate["issued"]
            src, ncols = sched[i]
            s = i % 3
            tr.dma("pool", out=ring[s][:, 0:ncols], in_=src, w=[ringres[s]])
            wstate["issued"] += 1

    def wnext(ahead=3):
        n = wstate["used"]
        wstate["used"] += 1
        wissue(n + ahead)
        s = n % 3
        return ring[s], ringres[s]

    for kc in range(8):
        tr.dma("sp", out=xT[:, kc, :], in_=xT_d[kc * 128:(kc + 1) * 128, :], w=xres[kc])
    tr.dma("sp", out=par[:], in_=par_d[:, :], w=[parres])
    tr.dma("sp", out=cTs[:], in_=cT_d[:, :], w=[miscres])
    tr.dma("pool", out=wf[:], in_=wf_d[:, :], w=[constres])
    op("pool", lambda g: g.memset(ones_b[:], 1.0), w=[constres])
    op("pool", lambda g: g.memset(ones_f[:], 1.0), w=[constres])
    op("pool", lambda g: g.memset(tmp[0][:], 0.0), w=[tmpres[0]])
    op("pool", lambda g: g.memset(tmp[1][:], 1.0), w=[tmpres[1]])
    op("pool", lambda g: g.affine_select(out=maskb[:], in_=tmp[0][:, 0:128], pattern=[[1, 128]], compare_op=ALU.is_ge,
                                         fill=-30000.0, base=0, channel_multiplier=-1), r=[tmpres[0]], w=[constres])
    op("pool", lambda g: g.affine_select(out=ident[:], in_=tmp[1][:, 0:128], pattern=[[1, 128]], compare_op=ALU.is_equal,
                                         fill=0.0, base=0, channel_multiplier=-1), r=[tmpres[1]], w=[constres])
    op("pool", lambda g: g.memset(va[:, :, :, 64:65], 1.0), w=vres)
    op("pool", lambda g: g.memset(uT[:, :, 0:30], 0.0), w=[ures[fc][0] for fc in range(4)])
    op("act", lambda a: a.activation(out=cact[:], in_=cTs[:], func=AF.Silu), r=[miscres], w=[constres])
    wissue(3)
    tr.barrier(("pe", "act", "dve", "pool"))

    evac_rr = [0]

    def evac(out_ap, in_ap, r, w, scale=None):
        evac_rr[0] ^= 1
        if evac_rr[0]:
            if scale is None:
                op("act", lambda a: a.activation(out=out_ap, in_=in_ap, func=AF.Copy), r=r, w=w)
            else:
                op("act", lambda a: a.activation(out=out_ap, in_=in_ap, func=AF.Identity, scale=scale), r=r, w=w)
        else:
            if scale is None:
                op("dve", lambda v: v.tensor_copy(out=out_ap, in_=in_ap), r=r, w=w)
            else:
                op("dve", lambda v: v.tensor_scalar(out=out_ap, in0=in_ap, scalar1=scale, scalar2=None, op0=ALU.mult),
                   r=r, w=w)

    def rstd_from(bk, scale):
        op("act", lambda a: a.activation(out=rstd_t[:], in_=ps[bk][:], func=AF.Ln, bias=EPS, scale=scale),
           r=[bank[bk]], w=[rstdres])
        op("act", lambda a: a.activation(out=rstd_t[:], in_=rstd_t[:], func=AF.Exp, scale=-0.5),
           r=[rstdres], w=[rstdres])

    trr = [0]

    def gettmp():
        trr[0] = (trr[0] + 1) % 3
        return tmp[trr[0]], tmpres[trr[0]]

    def ada_tile(l, t):
        W, Wr = wnext()
        for jj in range(4):
            j = t * 4 + jj
            for kc in range(8):
                op("pe", lambda p: p.matmul(ps[7][:, j:j + 1], lhsT=W[:, kc * 512 + jj * 128: kc * 512 + jj * 128 + 128],
                                            rhs=cact[:, kc:kc + 1], start=(kc == 0), stop=(kc == 7)),
                   r=[Wr, constres], w=[bank[7]], mark=(kc == 7 and jj == 3))

    def ada_fin(l):
        pb = l * NPAR
        mod, der, nbf, modres, derres = modA[l % 2], derA[l % 2], nbfA[l % 2], modresA[l % 2], derresA[l % 2]
        op("dve", lambda v: v.tensor_tensor(out=mod[:], in0=ps[7][:, 0:48], in1=par[:, pb + P_ADAB: pb + P_ADAB + 48], op=ALU.add),
           r=[bank[7], parres], w=[modres])
        op("dve", lambda v: v.scalar_tensor_tensor(out=der[:, 0:8], in0=mod[:, 8:16], scalar=1.0, in1=par[:, pb + P_PRE1: pb + P_PRE1 + 8],
                                                   op0=ALU.add, op1=ALU.mult), r=[modres, parres], w=[derres])
        op("dve", lambda v: v.tensor_tensor(out=der[:, 8:16], in0=mod[:, 16:24], in1=par[:, pb + P_POST1: pb + P_POST1 + 8], op=ALU.mult),
           r=[modres, parres], w=[derres])
        op("dve", lambda v: v.scalar_tensor_tensor(out=der[:, 16:24], in0=mod[:, 32:40], scalar=1.0, in1=par[:, pb + P_PRE2: pb + P_PRE2 + 8],
                                                   op0=ALU.add, op1=ALU.mult), r=[modres, parres], w=[derres])
        op("dve", lambda v: v.tensor_tensor(out=der[:, 24:32], in0=mod[:, 40:48], in1=par[:, pb + P_POST2: pb + P_POST2 + 8], op=ALU.mult),
           r=[modres, parres], w=[derres])
        op("dve", lambda v: v.tensor_scalar(out=nbf[:], in0=par[:, pb + P_BF: pb + P_BF + 1], scalar1=-1.0, scalar2=None, op0=ALU.mult),
           r=[parres], w=[derres])

    def prenorm_a(tb):
        ts = slice(tb * 512, (tb + 1) * 512)
        for kc in range(8):
            op("act", lambda a: a.activation(out=sq[:, kc, :], in_=xT[:, kc, ts], func=AF.Square),
               r=[xres[kc][tb]], w=[sqres[kc]])

    def prenorm_b(l, tb, a_off, sh_off):
        mod, der, modres, derres = modA[l % 2], derA[l % 2], modresA[l % 2], derresA[l % 2]
        ts = slice(tb * 512, (tb + 1) * 512)
        for kc in range(8):
            op("pe", lambda p: p.matmul(ps[6][:], lhsT=ones_b[:], rhs=sq[:, kc, :], start=(kc == 0), stop=(kc == 7)),
               r=[sqres[kc], constres], w=[bank[6]], mark=(kc == 7))
        rstd_from(6, 1.0 / D)
        for kc in range(8):
            tt, ttr = gettmp()
            op("dve", lambda v: v.tensor_tensor(out=tt[:], in0=xT[:, kc, ts], in1=rstd_t[:], op=ALU.mult),
               r=[xres[kc][tb], rstdres], w=[ttr])
            op("act", lambda a: a.activation(out=hT[:, kc, ts], in_=tt[:], func=AF.Identity,
                                             bias=mod[:, sh_off + kc: sh_off + kc + 1], scale=der[:, a_off + kc: a_off + kc + 1]),
               r=[ttr, modres, derres], w=[hres[kc][tb]])

    def postnorm_residual(l, ys, ysr, tb, g_off, extra_r=()):
        der, derres = derA[l % 2], derresA[l % 2]
        ts = slice(tb * 512, (tb + 1) * 512)
        for kc in range(8):
            op("pe", lambda p: p.matmul(ps[6][:], lhsT=ones_b[:], rhs=sq[:, kc, :], start=(kc == 0), stop=(kc == 7)),
               r=[sqres[kc], constres], w=[bank[6]], mark=(kc == 7))
        rstd_from(6, 1.0 / D)
        for kc in range(8):
            tt, ttr = gettmp()
            op("dve", lambda v: v.scalar_tensor_tensor(out=tt[:], in0=ys[:, kc, :], scalar=der[:, g_off + kc: g_off + kc + 1],
                                                       in1=rstd_t[:], op0=ALU.mult, op1=ALU.mult),
               r=(ysr[kc] if isinstance(ysr[kc], list) else [ysr[kc]]) + [derres, rstdres] + list(extra_r), w=[ttr])
            op("dve", lambda v: v.tensor_tensor(out=xT[:, kc, ts], in0=xT[:, kc, ts], in1=tt[:], op=ALU.add),
               r=[ttr, xres[kc][tb]], w=[xres[kc][tb]])

    brr = [0]

    def nextbank(n=6):
        brr[0] = (brr[0] + 1) % n
        return brr[0]

    def proj_fm(W, Wr, col0, tb, bk):
        ts = slice(tb * 512, (tb + 1) * 512)
        for kc in range(8):
            op("pe", lambda p: p.matmul(ps[bk][:], lhsT=W[:, kc * 512 + col0: kc * 512 + col0 + 128], rhs=hT[:, kc, ts],
                                        start=(kc == 0), stop=(kc == 7)),
               r=[Wr, hres[kc][tb]], w=[bank[bk]], mark=(kc == 7))

    def inproj(l):
        Wq, Wqr = wnext()
        Wk, Wkr = wnext(ahead=2)

        def qk(which, tbs):
            W, Wr = (Wq, Wqr) if which == 0 else (Wk, Wkr)
            dst, dres = (qT, qres) if which == 0 else (kT, kres)
            for tb in tbs:
                for fc in range(4):
                    bk = nextbank()
                    proj_fm(W, Wr, fc * 128, tb, bk)
                    evac(dst[:, fc, tb * 512:(tb + 1) * 512], ps[bk][:], [bank[bk]], [dres[fc][tb]],
                         scale=(0.125 if which == 0 else None))

        qk(0, (0, 1))
        qk(1, (0, 1))
        tr.barrier()
        op("dve", lambda v: v.memset(va[:, :, :, 64:65], 1.0), w=vres)
        for tb in range(4):
            ts = slice(tb * 512, (tb + 1) * 512)
            for kc in range(8):
                op("pe", lambda p: p.matmul(ps[6][0:8, :], lhsT=wf[:, l * 64 + kc * 8: l * 64 + kc * 8 + 8], rhs=hT[:, kc, ts],
                                            start=(kc == 0), stop=(kc == 7)),
                   r=[constres, hres[kc][tb]], w=[bank[6]], mark=(kc == 7))
            op("act", lambda a: a.activation(out=fT[0:8, ts], in_=ps[6][0:8, :], func=AF.Exp, bias=nbfA[l % 2][0:8, 0:1], scale=-1.0),
               r=[bank[6], derresA[l % 2]], w=[fres] + sqres)
        op("act", lambda a: a.activation(out=fT[:], in_=fT[:], func=AF.Ln, bias=1.0, scale=1.0), r=[fres], w=[fres])
        qk(0, (2, 3))
        fchain()
        op("dve", lambda v: v.memset(uT[:, :, 0:30], 0.0), w=[ures[fc][0] for fc in range(4)] + [splres])
        qk(1, (2, 3))
        W, Wr = wnext()
        for tk in range(16):
            bk = nextbank()
            for kc in range(8):
                op("pe", lambda p: p.matmul(ps[bk][:], lhsT=hT[:, kc, tk * 128:(tk + 1) * 128], rhs=W[:, kc * 512:(kc + 1) * 512],
                                            start=(kc == 0), stop=(kc == 7)),
                   r=[Wr, hres[kc][tk // 4]], w=[bank[bk]], mark=(kc == 7))
            evac(va[:, tk, :, 0:64], ps[bk][:].rearrange("p (h d) -> p h d", h=NH), [bank[bk]], [vres[tk]])
        for half in range(2):
            W, Wr = wnext()
            for c2 in range(2):
                fc = half * 2 + c2
                for tb in range(4):
                    b0 = nextbank()
                    proj_fm(W, Wr, c2 * 128, tb, b0)
                    b1 = nextbank()
                    proj_fm(W, Wr, 256 + c2 * 128, tb, b1)
                    tt, ttr = gettmp()
                    op("act", lambda a: a.activation(out=tt[:], in_=ps[b1][:], func=AF.Sigmoid), r=[bank[b1]], w=[ttr])
                    op("dve", lambda v: v.tensor_tensor(out=uT[:, fc, 30 + tb * 512: 30 + (tb + 1) * 512], in0=ps[b0][:], in1=tt[:],
                                                        op=ALU.mult), r=[bank[b0], ttr], w=[ures[fc][tb]])
    def fchain():
        op("dve", lambda v: v.tensor_tensor_scan(out=cumT[:], data0=fT[:], data1=fT[:], initial=0.0, op0=ALU.add, op1=ALU.max),
           r=[fres] + tmpres, w=[cumTres] + tmpres)
        op("dve", lambda v: v.tensor_copy(out=spl[:, 0:T], in_=cumT[:]), r=[cumTres], w=[splres])
        op("dve", lambda v: v.tensor_tensor(out=fT[:], in0=cumT[:], in1=spl[:, 0:T], op=ALU.subtract), r=[cumTres, splres], w=[fres])
        op("dve", lambda v: v.tensor_copy(out=spl[:, T:2 * T], in_=fT[:]), r=[fres], w=[splres])
        op("dve", lambda v: v.tensor_tensor(out=cumT[:], in0=fT[:], in1=spl[:, T:2 * T], op=ALU.subtract), r=[fres, splres], w=[cumTres])
        op("dve", lambda v: v.tensor_copy(out=spl[:, 2 * T:3 * T], in_=cumT[:]), r=[cumTres], w=[splres])
        tr.dma("sp", out=scr_d[:, :], in_=spl[:], r=[splres], w=[scrres])

    def attention():
        for b in range(2):
            extra = [] if b == 0 else ([fres] + sqres)
            op("dve", lambda v: v.memset(cumq[b][:], 1.0), w=[cumres[b]] + extra)
            op("dve", lambda v: v.memset(cumk[b][:], 0.0), w=[cumres[b]] + extra)
            op("dve", lambda v: v.memset(cumk[b][0:3, :], -1.0), w=[cumres[b]])
            op("dve", lambda v: v.memset(cumk[b][64:67, :], -1.0), w=[cumres[b]])
        def cum_load(c):
            b = c % 2
            for hx in range(2):
                h = 2 * c + hx
                src = scr_d[h:h + 1, :].rearrange("o (r t) -> (o r) t", r=3)
                tr.dma("sp", out=cumq[b][64 * hx:64 * hx + 3, :], in_=src, r=[scrres], w=[cumres[b]])
                tr.dma("sp", out=cumk[b][64 * hx + 3:64 * hx + 6, :], in_=src, r=[scrres], w=[cumres[b]])

        cnt = 0
        ocnt = 0
        pending = [None]
        for c in range(4):
            fc = c
            b = c % 2
            if c == 0:
                cum_load(0)
            for i in range(4):
                nj = 4 * i + 4
                if i == 3 and c < 3:
                    cum_load(c + 1)
                pbuf = ocnt % 2
                ob0 = 4 + 2 * pbuf
                ocnt += 1

                def s_tile(j):
                    nonlocal cnt
                    t = cnt % 2
                    cnt += 1
                    c0 = max(0, j - 4 * i) * 128
                    qs = slice(i * 512 + c0, (i + 1) * 512)
                    ks = slice(j * 128, (j + 1) * 128)
                    diagonal = j >= 4 * i
                    bks = [bank[2 * t], bank[2 * t + 1]]
                    for hx in range(2):
                        pr = 64 * hx
                        op("pe", lambda p: p.matmul(ps[2 * t + hx][:, c0:512], lhsT=kT[pr:pr + 64, fc, ks], rhs=qT[pr:pr + 64, fc, qs],
                                                    start=True, stop=False),
                           r=[kres[fc][j // 4], qres[fc][i]], w=[bks[hx]])
                    if diagonal:
                        for hx in range(2):
                            op("pe", lambda p: p.matmul(ps[2 * t + hx][:, c0:c0 + 128], lhsT=ident[:], rhs=maskb[:], start=False, stop=False),
                               r=[constres], w=[bks[hx]])
                    for hx in range(2):
                        pr = 64 * hx
                        op("pe", lambda p: p.matmul(ps[2 * t + hx][:, c0:512], lhsT=cumk[b][pr:pr + 64, ks], rhs=cumq[b][pr:pr + 64, qs],
                                                    start=False, stop=True),
                           r=[cumres[b]], w=[bks[hx]], mark=(hx == 1))
                    op("act", lambda a: a.activation(out=ptb2[t][:, :, c0:512], in_=pp[t][:, :, c0:512], func=AF.Exp),
                       r=bks, w=[ptres[t]])
                    return (j, c0, t)

                def pv_tile(tl):
                    j, c0, t = tl
                    for hx in range(2):
                        hh = 2 * c + hx
                        mw = 128 if hh < 7 else 65
                        op("pe", lambda p: p.matmul(ps[ob0 + hx][0:mw, c0:512], lhsT=va[:, j, :, :].rearrange("p h d -> p (h d)")[:, hh * 65:hh * 65 + mw],
                                                    rhs=ptb2[t][:, hx, c0:512],
                                                    start=(j == 0), stop=(j == nj - 1)),
                           r=[ptres[t], vres[j]], w=[bank[ob0 + hx]], mark=(j == nj - 1 and hx == 1))

                prev = s_tile(0)
                if pending[0] is not None:
                    pending[0]()
                    pending[0] = None
                for j in range(1, nj):
                    curt = s_tile(j)
                    pv_tile(prev)
                    prev = curt
                pv_tile(prev)
                for hx in range(2):
                    ob = ob0 + hx
                    k = (2 * ocnt + hx) % 8
                    rd, rdr = rden2[pbuf], rdres2[pbuf][hx]
                    op("dve", lambda v: v.reciprocal(out=rd[64:65, hx, :], in_=ps[ob][64:65, :]), r=[bank[ob]], w=[rdr])
                    tr.dma("sp", out=scrb_d[k:k + 1, :], in_=rd[64:65, hx, :], r=[rdr], w=[scrbres[k]])
                    tr.dma("sp", out=bcs[pbuf][0:64, hx, :], in_=scrb_d[k:k + 1, :].broadcast_to([64, 512]), r=[scrbres[k]], w=[bcres[pbuf][hx]])

                def fin(fc=fc, i=i, pbuf=pbuf, ob0=ob0):
                    for hx in range(2):
                        pr = 64 * hx
                        op("dve", lambda v: v.tensor_tensor(out=qT[pr:pr + 64, fc, i * 512:(i + 1) * 512], in0=ps[ob0 + hx][0:64, :],
                                                            in1=bcs[pbuf][0:64, hx, :], op=ALU.mult),
                           r=[bank[ob0 + hx], bcres[pbuf][hx]], w=[qres[fc][i]])

                pending[0] = fin
        pending[0]()

    def conv(l):
        pb = l * NPAR
        for fc in range(4):
            op("dve", lambda v: v.tensor_tensor(out=diag[:, fc, :, :],
                                                in0=ident[:].unsqueeze(1).broadcast_to([128, CK, 128]),
                                                in1=par[:, pb + P_CONVW + fc * CK: pb + P_CONVW + (fc + 1) * CK].unsqueeze(2).broadcast_to([128, CK, 128]),
                                                op=ALU.mult), r=[constres, parres], w=[diagres[fc]])
        for tb in (3, 2, 1, 0):
            for fc in range(4):
                rr = [ures[fc][tb], diagres[fc]] + ([ures[fc][tb - 1]] if tb > 0 else [])
                for k in range(CK):
                    op("pe", lambda p: p.matmul(ps[fc][:], lhsT=diag[:, fc, k, :], rhs=uT[:, fc, tb * 512 + k: tb * 512 + k + 512],
                                                start=(k == 0), stop=(k == CK - 1)), r=rr, w=[bank[fc]], mark=(k == CK - 1))
            for fc in range(4):
                op("act", lambda a: a.activation(out=ystage[:, fc, :], in_=ps[fc][:], func=AF.Identity,
                                                 bias=par[:, pb + P_CONVB + fc: pb + P_CONVB + fc + 1], scale=1.0),
                   r=[bank[fc], parres], w=[ysres[fc]])
                op("dve", lambda v: v.tensor_copy(out=sq[:, fc, :], in_=ystage[:, fc, :]), r=[ysres[fc]], w=[sqres[fc]])
                op("act", lambda a: a.activation(out=sq[:, 4 + fc, :], in_=ystage[:, fc, :], func=AF.Square),
                   r=[ysres[fc]], w=[sqres[4 + fc]])
            for fc in range(4):
                op("pe", lambda p: p.matmul(ps[4][:], lhsT=ones_b[:], rhs=sq[:, fc, :], start=(fc == 0), stop=(fc == 3)),
                   r=[sqres[fc], constres], w=[bank[4]], mark=(fc == 3))
            for fc in range(4):
                op("pe", lambda p: p.matmul(ps[5][:], lhsT=ones_b[:], rhs=sq[:, 4 + fc, :], start=(fc == 0), stop=(fc == 3)),
                   r=[sqres[4 + fc], constres], w=[bank[5]], mark=(fc == 3))
            mean, meanr = tmp[0], tmpres[0]
            var, varr = tmp[1], tmpres[1]
            op("dve", lambda v: v.tensor_scalar(out=mean[:], in0=ps[4][:], scalar1=1.0 / 512, scalar2=None, op0=ALU.mult),
               r=[bank[4]], w=[meanr])
            op("dve", lambda v: v.tensor_tensor(out=var[:], in0=mean[:], in1=mean[:], op=ALU.mult), r=[meanr], w=[varr])
            op("dve", lambda v: v.scalar_tensor_tensor(out=var[:], in0=ps[5][:], scalar=1.0 / 512, in1=var[:], op0=ALU.mult, op1=ALU.subtract),
               r=[bank[5], varr], w=[varr])
            op("act", lambda a: a.activation(out=rstd_t[:], in_=var[:], func=AF.Ln, bias=EPS, scale=1.0), r=[varr], w=[rstdres])
            op("act", lambda a: a.activation(out=rstd_t[:], in_=rstd_t[:], func=AF.Exp, scale=-0.5), r=[rstdres], w=[rstdres])
            for fc in range(4):
                op("dve", lambda v: v.tensor_tensor(out=ystage[:, fc, :], in0=ystage[:, fc, :], in1=mean[:], op=ALU.subtract),
                   r=[ysres[fc], meanr], w=[ysres[fc]])
                op("dve", lambda v: v.tensor_tensor(out=ystage[:, fc, :], in0=ystage[:, fc, :], in1=rstd_t[:], op=ALU.mult),
                   r=[ysres[fc], rstdres], w=[ysres[fc]])
                op("act", lambda a: a.activation(out=uT[:, fc, 30 + tb * 512: 30 + (tb + 1) * 512], in_=ystage[:, fc, :], func=AF.Silu,
                                                 bias=par[:, pb + P_LNB + fc: pb + P_LNB + fc + 1],
                                                 scale=par[:, pb + P_LNG + fc: pb + P_LNG + fc + 1]),
                   r=[ysres[fc], parres], w=[ures[fc][tb]])

    def wo(l):
        W0, W0r = wnext()
        W1, W1r = wnext(ahead=2)

        def mm(tb, oc):
            ts = slice(tb * 512, (tb + 1) * 512)
            us = slice(30 + tb * 512, 30 + (tb + 1) * 512)
            W, Wr = (W0, W0r) if oc < 4 else (W1, W1r)
            c0 = (oc % 4) * 128
            bk = nextbank(4)
            for kc in range(8):
                if kc < 4:
                    rhs, rr = qT[:, kc, ts], qres[kc][tb]
                else:
                    rhs, rr = uT[:, kc - 4, us], ures[kc - 4][tb]
                op("pe", lambda p: p.matmul(ps[bk][:], lhsT=W[:, kc * 512 + c0: kc * 512 + c0 + 128], rhs=rhs,
                                            start=(kc == 0), stop=(kc == 7)), r=[Wr, rr], w=[bank[bk]], mark=(kc == 7))
            return bk

        def ev(oc, bk):
            op("dve", lambda v: v.tensor_copy(out=ystage[:, oc, :], in_=ps[bk][:]), r=[bank[bk]], w=[ysres[oc]])
            op("act", lambda a: a.activation(out=sq[:, oc, :], in_=ystage[:, oc, :], func=AF.Square), r=[ysres[oc]], w=[sqres[oc]])

        for oc in range(8):
            ev(oc, mm(0, oc))
        for tb in range(4):
            ahead = [mm(tb + 1, oc) for oc in range(4)] if tb < 3 else []
            postnorm_residual(l, ystage, ysres, tb, 8)
            prenorm_a(tb)
            prenorm_b(l, tb, 16, 24)
            if tb < 3:
                for oc in range(4):
                    ev(oc, ahead[oc])
                for oc in range(4, 8):
                    ev(oc, mm(tb + 1, oc))

    def ffn(l):
        nxt = l + 1 < NL
        for half in range(2):
            fcnt = 0
            for s in range(11):
                W, Wr = wnext()
                for c2 in range(2):
                    hc = 2 * s + c2
                    for tbl in range(2):
                        tb = half * 2 + tbl
                        bg = fcnt % 2
                        bu = 2 + fcnt % 2
                        fcnt += 1
                        proj_fm(W, Wr, c2 * 128, tb, bg)
                        proj_fm(W, Wr, 256 + c2 * 128, tb, bu)
                        tt, ttr = gettmp()
                        op("act", lambda a: a.activation(out=tt[:], in_=ps[bg][:], func=AF.Silu), r=[bank[bg]], w=[ttr])
                        op("dve", lambda v: v.tensor_tensor(out=hid[:, hc, tbl * 512:(tbl + 1) * 512], in0=ps[bu][:], in1=tt[:], op=ALU.mult),
                           r=[bank[bu], ttr], w=[hidres[hc][tbl]])
                if half == 1 and s == 1:
                    deferred_post()
                if half == 0 and nxt:
                    ada_tile(l + 1, s)
                    if s == 10:
                        ada_tile(l + 1, 11)
                        ada_fin(l + 1)
            ys1 = hF32[:, :, half * 512:(half + 1) * 512]
            ys1res = [[hres[oc][half * 2], hres[oc][half * 2 + 1]] for oc in range(8)]
            for oc in range(8):
                W, Wr = wnext()
                pair = ((0, 1), (2, 3), (4, 5))[oc % 3]
                for tbl in range(2):
                    bk = pair[tbl]
                    for hc in range(NHC):
                        op("pe", lambda p: p.matmul(ps[bk][:], lhsT=W[:, hc * 128:(hc + 1) * 128], rhs=hid[:, hc, tbl * 512:(tbl + 1) * 512],
                                                    start=(hc == 0), stop=(hc == NHC - 1)),
                           r=[Wr, hidres[hc][tbl]], w=[bank[bk]], mark=(hc == NHC - 1))
                evac(ys0[:, oc, :], ps[pair[0]][:], [bank[pair[0]]], [ysres[oc]])
                evac(ys1[:, oc, :], ps[pair[1]][:], [bank[pair[1]]], ys1res[oc])
                if half == 1 and nxt:
                    if oc == 0:
                        prenorm_a(0)
                    elif oc == 2:
                        prenorm_b(l + 1, 0, 0, 0)
                    elif oc == 4:
                        prenorm_a(1)
                    elif oc == 6:
                        prenorm_b(l + 1, 1, 0, 0)
            def post(half=half, ys1=ys1, ys1res=ys1res):
                for kc in range(8):
                    op("act", lambda a: a.activation(out=sq[:, kc, :], in_=ys0[:, kc, :], func=AF.Square), r=[ysres[kc]], w=[sqres[kc]])
                postnorm_residual(l, ys0, ysres, half * 2, 24)
                for kc in range(8):
                    op("act", lambda a: a.activation(out=sq[:, kc, :], in_=ys1[:, kc, :], func=AF.Square), r=ys1res[kc], w=[sqres[kc]])
                postnorm_residual(l, ys1, ys1res, half * 2 + 1, 24)

            if half == 0:
                deferred_post = post
            else:
                post()
        if nxt:
            for tb in (2, 3):
                prenorm_a(tb)
                prenorm_b(l + 1, tb, 0, 0)

    for t in range(12):
        ada_tile(0, t)
    ada_fin(0)
    for tb in range(4):
        prenorm_a(tb)
        prenorm_b(0, tb, 0, 0)
    for l in range(NL):
        inproj(l)
        tr.barrier()
        attention()
        tr.barrier()
        conv(l)
        tr.barrier()
        wo(l)
        tr.barrier()
        ffn(l)
    tr.barrier()
    for kc in range(8):
        tr.dma("sp", out=out_d[kc * 128:(kc + 1) * 128, :], in_=xT[:, kc, :], r=xres[kc])
    tr.wait_all_dma("sp")
    return nc


def _kc_tile(w2d, cols):
    sub = w2d[:, cols]
    n = sub.shape[1]
    return np.ascontiguousarray(sub.reshape(8, 128, n).transpose(1, 0, 2).reshape(128, 8 * n))


def prep_weights(NL, w_in, b_f, conv_w, conv_b, conv_ln_g, conv_ln_b, w_o, w_ffn_in, w_ffn_out,
                 mix_pre_g, mix_post_g, ffn_pre_g, ffn_post_g, ada_w, ada_b):
    f32 = np.float32
    wA = np.empty((NL * 12, 128, 4096), f32)
    wIN = np.empty((NL * 5, 128, 4096), f32)
    wO = np.empty((NL * 2, 128, 4096), f32)
    wFI = np.empty((NL * 11, 128, 4096), f32)
    wFO = np.empty((NL * 8, 128, DFF), f32)
    par = np.zeros((128, NL * NPAR), f32)
    wf = np.empty((128, NL * 64), f32)
    ar = np.arange
    for l in range(NL):
        for t in range(12):
            wA[l * 12 + t] = _kc_tile(ada_w[l], ar(t * 512, (t + 1) * 512))
        wi = w_in[l]
        wIN[l * 5 + 0] = _kc_tile(wi, ar(0, 512))
        wIN[l * 5 + 1] = _kc_tile(wi, ar(512, 1024))
        wIN[l * 5 + 2] = _kc_tile(wi, ar(1024, 1536))
        for half in range(2):
            cols = np.concatenate([ar(1544 + half * 256, 1544 + half * 256 + 256), ar(2056 + half * 256, 2056 + half * 256 + 256)])
            wIN[l * 5 + 3 + half] = _kc_tile(wi, cols)
        wf[:, l * 64:(l + 1) * 64] = _kc_tile(wi, ar(1536, 1544))
        for t in range(2):
            wO[l * 2 + t] = _kc_tile(w_o[l], ar(t * 512, (t + 1) * 512))
        for s in range(11):
            cols = np.concatenate([ar(s * 256, s * 256 + 256), ar(DFF + s * 256, DFF + s * 256 + 256)])
            wFI[l * 11 + s] = _kc_tile(w_ffn_in[l], cols)
        wo3 = w_ffn_out[l].reshape(NHC, 128, 8, 128)
        wFO[l * 8:(l + 1) * 8] = wo3.transpose(2, 1, 0, 3).reshape(8, 128, DFF)
        pb = l * NPAR

        def fm(v):
            return v.reshape(8, 128).T

        par[:, pb + P_PRE1: pb + P_PRE1 + 8] = fm(mix_pre_g[l])
        par[:, pb + P_POST1: pb + P_POST1 + 8] = fm(mix_post_g[l])
        par[:, pb + P_PRE2: pb + P_PRE2 + 8] = fm(ffn_pre_g[l])
        par[:, pb + P_POST2: pb + P_POST2 + 8] = fm(ffn_post_g[l])
        par[:, pb + P_ADAB: pb + P_ADAB + 48] = ada_b[l].reshape(48, 128).T
        par[:, pb + P_CONVW: pb + P_CONVW + 4 * CK] = conv_w[l].reshape(CK, 4, 128).transpose(2, 1, 0).reshape(128, 4 * CK)
        par[:, pb + P_CONVB: pb + P_CONVB + 4] = conv_b[l].reshape(4, 128).T
        par[:, pb + P_LNG: pb + P_LNG + 4] = conv_ln_g[l].reshape(4, 128).T
        par[:, pb + P_LNB: pb + P_LNB + 4] = conv_ln_b[l].reshape(4, 128).T
        par[0:8, pb + P_BF] = b_f[l]
    return dict(wA=wA, wIN=wIN, wO=wO, wFI=wFI, wFO=wFO, par=par, wf=wf)


_NC_CACHE = {}


def run_layers(x, c, NL, weights, cores=None):
    B = x.shape[0]
    if cores is None:
        cores = list(range(B))
    if NL not in _NC_CACHE:
        _NC_CACHE[NL] = build(NL)
    nc = _NC_CACHE[NL]
    in_maps = []
    for b in cores:
        m = dict(weights)
        m["xT"] = np.ascontiguousarray(x[b].T)
        m["cT"] = np.ascontiguousarray(c[b].reshape(8, 128).T)
        in_maps.append(m)
    res = run_bass_kernel_spmd(nc, in_maps, core_ids=list(range(len(cores))))
    out = np.empty((len(cores), T, D), np.float32)
    for i in range(len(cores)):
        out[i] = res.results[i]["outT"].T
    return out


def kernel(x, c, w_in, b_f, conv_w, conv_b, conv_ln_g, conv_ln_b, w_o, w_ffn_in, w_ffn_out,
           mix_pre_g, mix_post_g, ffn_pre_g, ffn_post_g, ada_w, ada_b):
    a = [np.asarray(v, dtype=np.float32) for v in (w_in, b_f, conv_w, conv_b, conv_ln_g, conv_ln_b, w_o, w_ffn_in, w_ffn_out,
                                                   mix_pre_g, mix_post_g, ffn_pre_g, ffn_post_g, ada_w, ada_b)]
    weights = prep_weights(L_FULL, *a)
    x = np.asarray(x, dtype=np.float32)
    c = np.asarray(c, dtype=np.float32)
    return run_layers(x, c, L_FULL, weights)
```
